# Optimizing a Trainium2 kernel written in Bass

```python
import math
import jax, jax.numpy as jnp
from jax import lax
import numpy as np

D_MODEL = 2048
BATCH = 1
SEQ = 16384
DEPTH = 4

N_MIXERS = 3
N_FOX_LAYERS = (DEPTH + 2) // 3
N_RWKV_LAYERS = (DEPTH + 1) // 3
N_S5_LAYERS = DEPTH // 3

RMS_EPS = 1e-6
D_FF = ((int(2 * 4 * D_MODEL / 3) + 255) // 256) * 256

FOX_HEAD_DIM = 128
FOX_HEADS = D_MODEL // FOX_HEAD_DIM
FOX_BLOCK = 128
FOX_IN = 4 * D_MODEL + FOX_HEADS

RWKV_HEAD_DIM = 64
RWKV_HEADS = D_MODEL // RWKV_HEAD_DIM
RWKV_DECAY_LORA = max(32, int(round(1.8 * D_MODEL ** 0.5 / 32)) * 32)
RWKV_A_LORA = max(32, int(round(1.8 * D_MODEL ** 0.5 / 32)) * 32)
RWKV_GATE_LORA = max(32, int(round(0.6 * D_MODEL ** 0.8 / 32)) * 32)
RWKV_LN_EPS = 64e-5
RWKV_NORM_EPS = 1e-12

S5_GROUP = 16
S5_GROUPS = D_MODEL // S5_GROUP
S5_STATE = 64
S5_CHUNK = 128
S5_DT_MIN = 1e-3
S5_DT_MAX = 1e-1
S5_MAX_RE = -1e-4

kernel_name = "fox_rwkv7_s5_macaron_hybrid"


def rms_norm(x, g):
    xf = x.astype(jnp.float32)
    y = xf * lax.rsqrt(jnp.mean(xf * xf, axis=-1, keepdims=True) + RMS_EPS)
    return (y * g.astype(jnp.float32)).astype(x.dtype)


def swiglu(h, w_up, w_down):
    gate, up = jnp.split(h @ w_up, 2, axis=-1)
    return (jax.nn.silu(gate) * up) @ w_down


def fox_mixer(h, w_in, b_f, qk_gain, w_out):
    b, s, d = h.shape
    proj = h @ w_in
    q = proj[..., 0 * d:1 * d].reshape(b, s, FOX_HEADS, FOX_HEAD_DIM)
    k = proj[..., 1 * d:2 * d].reshape(b, s, FOX_HEADS, FOX_HEAD_DIM)
    v = proj[..., 2 * d:3 * d].reshape(b, s, FOX_HEADS, FOX_HEAD_DIM)
    g = proj[..., 3 * d:4 * d]
    f_logit = proj[..., 4 * d:]
    q = rms_norm(q, qk_gain[0]).transpose(0, 2, 1, 3)
    k = rms_norm(k, qk_gain[1]).transpose(0, 2, 1, 3)
    v = v.transpose(0, 2, 1, 3)
    log_f = jax.nn.log_sigmoid(f_logit.astype(jnp.float32) + b_f.astype(jnp.float32))
    c = jnp.cumsum(log_f, axis=1).transpose(0, 2, 1)
    nb = s // FOX_BLOCK
    q_blocks = q.reshape(b, FOX_HEADS, nb, FOX_BLOCK, FOX_HEAD_DIM).transpose(2, 0, 1, 3, 4)
    c_blocks = c.reshape(b, FOX_HEADS, nb, FOX_BLOCK).transpose(2, 0, 1, 3)
    starts = jnp.arange(nb, dtype=jnp.int32) * FOX_BLOCK
    kpos = jnp.arange(s, dtype=jnp.int32)
    scale = FOX_HEAD_DIM ** -0.5

    def one_block(args):
        qb, cb, st = args
        logits = jnp.einsum('bhqd,bhkd->bhqk', qb, k).astype(jnp.float32) * scale
        logits = logits + cb[..., :, None] - c[:, :, None, :]
        qpos = st + jnp.arange(FOX_BLOCK, dtype=jnp.int32)
        logits = jnp.where(kpos[None, :] <= qpos[:, None], logits, -jnp.inf)
        p = jax.nn.softmax(logits, axis=-1).astype(v.dtype)
        return jnp.einsum('bhqk,bhkd->bhqd', p, v)

    o = lax.map(one_block, (q_blocks, c_blocks, starts))
    o = o.transpose(1, 0, 3, 2, 4).reshape(b, s, d)
    return (o * jax.nn.sigmoid(g)) @ w_out


def rwkv7_mixer(h, mu, w_rkv, w0, w1, w2, a0, a1, a2, g1, g2, k_k, k_a, r_k, ln_w, ln_b, w_out):
    b, s, d = h.shape
    H, N = RWKV_HEADS, RWKV_HEAD_DIM
    f32 = jnp.float32
    xx = jnp.pad(h, ((0, 0), (1, 0), (0, 0)))[:, :-1] - h
    xr, xw, xk, xv, xa, xg = [h + xx * mu[i] for i in range(6)]
    rkv = jnp.einsum('nbsd,nde->nbse', jnp.stack([xr, xk, xv]), w_rkv)
    r, k, v = rkv[0].astype(f32), rkv[1].astype(f32), rkv[2].astype(f32)
    w_log = -jax.nn.softplus(-(w0 + jnp.tanh(xw @ w1) @ w2).astype(f32)) - 0.5
    decay = jnp.exp(-jnp.exp(w_log))
    a = jax.nn.sigmoid((a0 + (xa @ a1) @ a2).astype(f32))
    g = jax.nn.sigmoid(xg @ g1) @ g2
    kk = (k * k_k.astype(f32)).reshape(b, s, H, N)
    kk = kk / jnp.maximum(jnp.sqrt(jnp.sum(kk * kk, axis=-1, keepdims=True)), RWKV_NORM_EPS)
    k = k * (1.0 + (a - 1.0) * k_a.astype(f32))
    heads = lambda t: t.reshape(b, s, H, N)
    r4, k4, v4, w4, a4 = heads(r), heads(k), heads(v), heads(decay), heads(a)
    seq_first = lambda t: jnp.moveaxis(t, 1, 0)
    xs = (seq_first(r4), seq_first(w4), seq_first(k4), seq_first(v4), seq_first(-kk), seq_first(kk * a4))

    def step(state, inp):
        rt, wt, kt, vt, at, bt = inp
        sa = jnp.einsum('bhij,bhj->bhi', state, at)
        state = state * wt[:, :, None, :] + sa[..., None] * bt[:, :, None, :] + vt[..., None] * kt[:, :, None, :]
        return state, jnp.einsum('bhij,bhj->bhi', state, rt)

    state0 = jnp.zeros((b, H, N, N), f32)
    _, y = lax.scan(step, state0, xs)
    y = jnp.moveaxis(y, 0, 1)
    mean = jnp.mean(y, axis=-1, keepdims=True)
    var = jnp.mean((y - mean) ** 2, axis=-1, keepdims=True)
    y = ((y - mean) * lax.rsqrt(var + RWKV_LN_EPS)).reshape(b, s, d)
    y = y * ln_w.astype(f32) + ln_b.astype(f32)
    bonus = jnp.sum(r4 * k4 * r_k.astype(f32), axis=-1, keepdims=True) * v4
    y = (y + bonus.reshape(b, s, d)) * g.astype(f32)
    return (y.astype(h.dtype) @ w_out).astype(h.dtype)


def _complex_affine_combine(e1, e2):
    ar1, ai1, br1, bi1 = e1
    ar2, ai2, br2, bi2 = e2
    return (ar1 * ar2 - ai1 * ai2,
            ar1 * ai2 + ai1 * ar2,
            ar2 * br1 - ai2 * bi1 + br2,
            ar2 * bi1 + ai2 * br1 + bi2)


def s5_mixer(h, w_in, lam_re, lam_im, log_step, b_re, b_im, c_re, c_im, d_skip, w_out):
    bsz, s, d = h.shape
    f32 = jnp.float32
    G, P, Q, L = S5_GROUPS, S5_STATE, S5_GROUP, S5_CHUNK
    u = (h @ w_in).astype(f32)
    lr = jnp.minimum(lam_re.astype(f32), S5_MAX_RE)
    li = lam_im.astype(f32)
    dt = jnp.exp(log_step.astype(f32))[:, None]
    mag = jnp.exp(lr * dt)
    abar_re, abar_im = mag * jnp.cos(li * dt), mag * jnp.sin(li * dt)
    den = lr * lr + li * li
    nr, ni = abar_re - 1.0, abar_im
    q_re, q_im = (nr * lr + ni * li) / den, (ni * lr - nr * li) / den
    br, bi = b_re.astype(f32), b_im.astype(f32)
    bbar_re = q_re[..., None] * br - q_im[..., None] * bi
    bbar_im = q_re[..., None] * bi + q_im[..., None] * br
    cr, ci = c_re.astype(f32), c_im.astype(f32)
    nc = s // L
    u_chunks = jnp.moveaxis(u.reshape(bsz, nc, L, G, Q), 1, 0)

    def chunk_step(carry, u_c):
        hr0, hi0 = carry
        bu_re = jnp.einsum('blgq,gpq->blgp', u_c, bbar_re)
        bu_im = jnp.einsum('blgq,gpq->blgp', u_c, bbar_im)
        bu_re = bu_re.at[:, 0].add(abar_re * hr0 - abar_im * hi0)
        bu_im = bu_im.at[:, 0].add(abar_re * hi0 + abar_im * hr0)
        ar = jnp.broadcast_to(abar_re, bu_re.shape)
        ai = jnp.broadcast_to(abar_im, bu_im.shape)
        _, _, hr, hi = lax.associative_scan(_complex_affine_combine, (ar, ai, bu_re, bu_im), axis=1)
        y = jnp.einsum('blgp,gqp->blgq', hr, cr) - jnp.einsum('blgp,gqp->blgq', hi, ci)
        return (hr[:, -1], hi[:, -1]), y

    carry0 = (jnp.zeros((bsz, G, P), f32), jnp.zeros((bsz, G, P), f32))
    _, y = lax.scan(chunk_step, carry0, u_chunks)
    y = jnp.moveaxis(y, 0, 1).reshape(bsz, s, d) + d_skip.astype(f32) * u
    y = jax.nn.gelu(y).astype(h.dtype)
    val, gate = jnp.split(y @ w_out, 2, axis=-1)
    return (val * jax.nn.sigmoid(gate)).astype(h.dtype)


def setup_inputs(seed: int = 0) -> dict:
    key = jax.random.key(seed)
    it = iter(jax.random.split(key, 48))
    nrm = lambda shape, scale: jax.random.normal(next(it), shape, jnp.float32) * scale
    D, F = D_MODEL, D_FF
    NA, NB, NC = N_FOX_LAYERS, N_RWKV_LAYERS, N_S5_LAYERS
    inp = {}
    inp['x'] = nrm((BATCH, SEQ, D), 1.0)
    inp['norm_w'] = 1.0 + nrm((DEPTH, 3, D), 0.02)
    inp['ffn_w_up'] = nrm((DEPTH, 2, D, 2 * F), D ** -0.5)
    inp['ffn_w_down'] = nrm((DEPTH, 2, F, D), F ** -0.5)
    inp['fox_w_in'] = nrm((NA, D, FOX_IN), D ** -0.5)
    inp['fox_b_f'] = 2.0 + nrm((NA, FOX_HEADS), 0.5)
    inp['fox_qk_gain'] = 1.0 + nrm((NA, 2, FOX_HEAD_DIM), 0.02)
    inp['fox_w_out'] = nrm((NA, D, D), D ** -0.5)
    inp['rwkv_mu'] = jax.random.uniform(next(it), (NB, 6, D), jnp.float32)
    inp['rwkv_w_rkv'] = nrm((NB, 3, D, D), D ** -0.5)
    ramp = jnp.linspace(0.0, 1.0, D, dtype=jnp.float32) ** 0.9
    inp['rwkv_w0'] = (-6.0 + 5.0 * ramp + 0.5)[None, :] + nrm((NB, D), 0.1)
    inp['rwkv_w1'] = nrm((NB, D, RWKV_DECAY_LORA), D ** -0.5)
    inp['rwkv_w2'] = nrm((NB, RWKV_DECAY_LORA, D), 0.1 * RWKV_DECAY_LORA ** -0.5)
    inp['rwkv_a0'] = nrm((NB, D), 0.1)
    inp['rwkv_a1'] = nrm((NB, D, RWKV_A_LORA), D ** -0.5)
    inp['rwkv_a2'] = nrm((NB, RWKV_A_LORA, D), 0.1 * RWKV_A_LORA ** -0.5)
    inp['rwkv_g1'] = nrm((NB, D, RWKV_GATE_LORA), D ** -0.5)
    inp['rwkv_g2'] = nrm((NB, RWKV_GATE_LORA, D), RWKV_GATE_LORA ** -0.5)
    inp['rwkv_k_k'] = 0.85 + nrm((NB, D), 0.02)
    inp['rwkv_k_a'] = 1.0 + nrm((NB, D), 0.02)
    inp['rwkv_r_k'] = nrm((NB, RWKV_HEADS, RWKV_HEAD_DIM), 0.1)
    inp['rwkv_ln_w'] = 1.0 + nrm((NB, D), 0.02)
    inp['rwkv_ln_b'] = nrm((NB, D), 0.02)
    inp['rwkv_w_out'] = nrm((NB, D, D), D ** -0.5)
    inp['s5_w_in'] = nrm((NC, D, D), D ** -0.5)
    inp['s5_lam_re'] = -0.5 + nrm((NC, S5_GROUPS, S5_STATE), 0.01)
    inp['s5_lam_im'] = (math.pi * jnp.arange(S5_STATE, dtype=jnp.float32))[None, None, :] + nrm((NC, S5_GROUPS, S5_STATE), 0.01)
    inp['s5_log_step'] = jax.random.uniform(next(it), (NC, S5_GROUPS), jnp.float32, math.log(S5_DT_MIN), math.log(S5_DT_MAX))
    inp['s5_b_re'] = nrm((NC, S5_GROUPS, S5_STATE, S5_GROUP), (2 * S5_GROUP) ** -0.5)
    inp['s5_b_im'] = nrm((NC, S5_GROUPS, S5_STATE, S5_GROUP), (2 * S5_GROUP) ** -0.5)
    inp['s5_c_re'] = nrm((NC, S5_GROUPS, S5_GROUP, S5_STATE), S5_STATE ** -0.5)
    inp['s5_c_im'] = nrm((NC, S5_GROUPS, S5_GROUP, S5_STATE), S5_STATE ** -0.5)
    inp['s5_d'] = nrm((NC, D), 1.0)
    inp['s5_w_out'] = nrm((NC, D, 2 * D), D ** -0.5)
    inp['final_norm'] = 1.0 + nrm((D,), 0.02)
    return inp


def reference(x, norm_w, ffn_w_up, ffn_w_down, fox_w_in, fox_b_f, fox_qk_gain, fox_w_out,
              rwkv_mu, rwkv_w_rkv, rwkv_w0, rwkv_w1, rwkv_w2, rwkv_a0, rwkv_a1, rwkv_a2,
              rwkv_g1, rwkv_g2, rwkv_k_k, rwkv_k_a, rwkv_r_k, rwkv_ln_w, rwkv_ln_b, rwkv_w_out,
              s5_w_in, s5_lam_re, s5_lam_im, s5_log_step, s5_b_re, s5_b_im, s5_c_re, s5_c_im,
              s5_d, s5_w_out, final_norm):
    ia = ib = ic = 0
    for i in range(DEPTH):
        x = x + 0.5 * swiglu(rms_norm(x, norm_w[i, 0]), ffn_w_up[i, 0], ffn_w_down[i, 0])
        h = rms_norm(x, norm_w[i, 1])
        m = i % N_MIXERS
        if m == 0:
            x = x + fox_mixer(h, fox_w_in[ia], fox_b_f[ia], fox_qk_gain[ia], fox_w_out[ia])
            ia += 1
        elif m == 1:
            x = x + rwkv7_mixer(h, rwkv_mu[ib], rwkv_w_rkv[ib], rwkv_w0[ib], rwkv_w1[ib], rwkv_w2[ib],
                                rwkv_a0[ib], rwkv_a1[ib], rwkv_a2[ib], rwkv_g1[ib], rwkv_g2[ib],
                                rwkv_k_k[ib], rwkv_k_a[ib], rwkv_r_k[ib], rwkv_ln_w[ib], rwkv_ln_b[ib],
                                rwkv_w_out[ib])
            ib += 1
        else:
            x = x + s5_mixer(h, s5_w_in[ic], s5_lam_re[ic], s5_lam_im[ic], s5_log_step[ic],
                             s5_b_re[ic], s5_b_im[ic], s5_c_re[ic], s5_c_im[ic], s5_d[ic], s5_w_out[ic])
            ic += 1
        x = x + 0.5 * swiglu(rms_norm(x, norm_w[i, 2]), ffn_w_up[i, 1], ffn_w_down[i, 1])
    return rms_norm(x, final_norm)
```

```python
import numpy as np
import concourse.bass as bass
import concourse.mybir as mybir
from concourse.bass_utils import run_bass_kernel_spmd

F32 = mybir.dt.float32
BF16 = mybir.dt.bfloat16
AF = mybir.ActivationFunctionType
ALU = mybir.AluOpType
AX = mybir.AxisListType

ENGS = ("pe", "act", "dve", "pool", "sp")
EIDX = {e: i for i, e in enumerate(ENGS)}


class Op:
    __slots__ = ("eng", "fn", "idx", "dma", "waits", "sig", "sem", "target", "know", "dmaknow", "deps")

    def __init__(self, eng, fn, dma):
        self.eng = eng
        self.fn = fn
        self.dma = dma
        self.idx = -1
        self.waits = []
        self.sig = False
        self.sem = None
        self.target = 0
        self.know = None


class Sched:
    NDMA = 12

    def __init__(self, nc):
        self.nc = nc
        self.eng_ops = {e: [] for e in ENGS}
        self.last_w = {}
        self.readers = {}
        self.know = {e: [-1] * len(ENGS) for e in ENGS}
        self.dmaknow = {e: set() for e in ENGS}
        self.dma_slots = {e: [None] * self.NDMA for e in ENGS}
        self.dma_cnt = {e: 0 for e in ENGS}
        self.dma_slot_uses = {e: [0] * self.NDMA for e in ENGS}
        self.nops = 0

    def add(self, eng, fn, r=(), w=(), dma=False):
        op = Op(eng, fn, dma)
        ops = self.eng_ops[eng]
        op.idx = len(ops)
        deps = []
        for b in r:
            x = self.last_w.get(b)
            if x is not None:
                deps.append(x)
        for b in w:
            x = self.last_w.get(b)
            if x is not None:
                deps.append(x)
            deps.extend(self.readers.get(b, ()))
        if dma:
            k = self.dma_cnt[eng]
            slot = k % self.NDMA
            prev = self.dma_slots[eng][slot]
            if prev is not None:
                deps.append(prev)
            self.dma_slots[eng][slot] = op
            self.dma_cnt[eng] = k + 1
            self.dma_slot_uses[eng][slot] += 1
            op.sem = (eng, slot)
            op.target = 16 * self.dma_slot_uses[eng][slot]
            op.sig = True
        know = self.know[eng]
        dk = self.dmaknow[eng]
        best = {}
        for d in deps:
            if d is op:
                continue
            if d.dma:
                if id(d) in dk:
                    continue
                best[("dma", id(d))] = d
            else:
                if d.eng == "pe" and eng == "pe" and not dma:
                    continue
                if know[EIDX[d.eng]] >= d.idx:
                    continue
                cur = best.get(d.eng)
                if cur is None or cur.idx < d.idx:
                    best[d.eng] = d
        for key, d in best.items():
            if d.dma:
                op.waits.append(d)
                dk.add(id(d))
            else:
                if know[EIDX[d.eng]] >= d.idx:
                    continue
                op.waits.append(d)
                d.sig = True
                dkv = d.know
                for i in range(len(ENGS)):
                    if dkv[i] > know[i]:
                        know[i] = dkv[i]
                if know[EIDX[d.eng]] < d.idx:
                    know[EIDX[d.eng]] = d.idx
        op.know = list(know)
        for b in r:
            self.readers.setdefault(b, []).append(op)
        for b in w:
            self.last_w[b] = op
            self.readers[b] = []
        ops.append(op)
        self.nops += 1
        if len(dk) > 4096:
            dk.clear()
        return op

    def emit(self, final_wait_ops=()):
        nc = self.nc
        from contextlib import ExitStack
        with ExitStack() as es:
            esem = {e: es.enter_context(nc.semaphore("s_" + e)) for e in ENGS}
            dsem = {}
            for e in ENGS:
                if self.dma_cnt[e] > 0:
                    for s in range(min(self.NDMA, self.dma_cnt[e])):
                        dsem[(e, s)] = es.enter_context(nc.semaphore("d_%s_%d" % (e, s)))
            for e in ENGS:
                c = 0
                for op in self.eng_ops[e]:
                    if op.dma:
                        continue
                    if op.sig:
                        c += 1
                        op.target = c
                        op.sem = e
            block = es.enter_context(nc.Block())

            def run(e, eng):
                for op in self.eng_ops[e]:
                    for d in op.waits:
                        if d.dma:
                            eng.wait_ge(dsem[d.sem], d.target)
                        else:
                            eng.wait_ge(esem[d.sem], d.target)
                    ins = op.fn(eng)
                    if op.dma:
                        ins.then_inc(dsem[op.sem], 16)
                    elif op.sig:
                        ins.then_inc(esem[e], 1)
                if e == "sp":
                    fw = list(final_wait_ops)
                    for q in ENGS:
                        for d in self.dma_slots[q]:
                            if d is not None:
                                fw.append(d)
                    for d in fw:
                        if d.dma:
                            eng.wait_ge(dsem[d.sem], d.target)
                        else:
                            eng.wait_ge(esem[d.sem], d.target)

            @block.tensor
            def _(eng):
                run("pe", eng)

            @block.scalar
            def _(eng):
                run("act", eng)

            @block.vector
            def _(eng):
                run("dve", eng)

            @block.gpsimd
            def _(eng):
                run("pool", eng)

            @block.sync
            def _(eng):
                run("sp", eng)


from contextlib import ExitStack
import math


class CFG:
    def __init__(self, D=2048, S=16384, NCORE=8):
        self.D = D
        self.S = S
        self.NCORE = NCORE
        self.F = ((int(2 * 4 * D / 3) + 255) // 256) * 256
        self.TC = S // NCORE
        self.TT = min(512, self.TC)
        self.NTT = self.TC // self.TT
        self.KC = D // 128
        self.FH = D // 128
        self.FOXIN = 4 * D + self.FH


def _alloc(nc, es, name, shape, dt):
    return es.enter_context(nc.sbuf_tensor(name, shape, dt))


class P:
    def __init__(self):
        self.nc = bass.Bass("TRN2", target_bir_lowering=False)
        self.es = ExitStack()
        self.S = Sched(self.nc)
        self.banks = [self.es.enter_context(self.nc.psum_tensor("bank%d" % i, [128, 512], F32)) for i in range(8)]
        self.nbank = 0
        self.cnt = 0

    def inp(self, name, shape, dt=F32):
        return self.nc.dram_tensor(name, list(shape), dt, kind="ExternalInput").ap()

    def outp(self, name, shape, dt=F32):
        return self.nc.dram_tensor(name, list(shape), dt, kind="ExternalOutput").ap()

    def sb(self, name, shape, dt=F32):
        return _alloc(self.nc, self.es, name, list(shape), dt)

    def add(self, *a, **k):
        return self.S.add(*a, **k)

    def dma(self, q, out, in_, r=(), w=()):
        return self.S.add(q, lambda e: e.dma_start(out=out, in_=in_), r=r, w=w, dma=True)

    def finish(self):
        self.S.emit()
        self.es.close()
        return self.nc

    def const(self, name, val, shape=(128, 1)):
        t = self.sb(name, shape)
        self.add("pool", lambda e: e.memset(t[:], val), w=[name])
        return t


def emit_rstd(p, xs, xs_name, KC, width, ones, rstd, rstd_name, epst, eps_name, scr, D_total):
    ps = p.banks[0]
    for kc in range(KC):
        sqb = scr[kc % 2]
        sn = "scr%d" % (kc % 2)
        p.add("act", lambda e, kc=kc, sqb=sqb: e.activation(out=sqb[:, :width], in_=xs[:, kc, :width], func=AF.Square),
              r=[xs_name], w=[sn])
        p.add("pe", lambda e, kc=kc, sqb=sqb: e.matmul(ps[:, :width], lhsT=ones[:], rhs=sqb[:, :width], start=(kc == 0), stop=(kc == KC - 1)),
              r=["ones", sn], w=["bank0"])
    p.add("act", lambda e: e.activation(out=scr[0][:, :width], in_=ps[:, :width], func=AF.Sqrt, scale=1.0 / D_total, bias=epst[:]),
          r=["bank0", eps_name], w=["scr0"])
    p.add("dve", lambda e: e.reciprocal(out=rstd[:, :width], in_=scr[0][:, :width]), r=["scr0"], w=[rstd_name])


def emit_norm_bf(p, c, xT, g, abf, eps=1e-6):
    KC, TT = c.KC, c.TT
    ones = p.const("ones", 1.0, (128, 128))
    epst = p.const("epst", eps)
    gt = p.sb("gt", [128, KC])
    xs = p.sb("xs", [128, KC, TT])
    scr = [p.sb("scr%d" % i, [128, TT]) for i in range(2)]
    rstd = p.sb("rstd", [128, TT])
    p.dma("sp", gt[:], g, w=["gt"])
    xv = xT.rearrange("(c p) t -> p c t", p=128)
    for tt in range(c.NTT):
        tsl = slice(tt * TT, (tt + 1) * TT)
        p.dma("sp", xs[:], xv[:, :, tsl], w=["xs"])
        emit_rstd(p, xs, "xs", KC, TT, ones, rstd, "rstd", epst, "epst", scr, c.D)
        for kc in range(KC):
            p.add("dve", lambda e, kc=kc, tsl=tsl: e.scalar_tensor_tensor(
                out=abf[:, kc, tsl], in0=xs[:, kc, :], scalar=gt[:, kc:kc + 1], in1=rstd[:],
                op0=ALU.mult, op1=ALU.mult), r=["xs", "gt", "rstd"], w=["abf"])


def emit_gemm(p, c, w, wname, KC, abf, abf_name, MC, epi, t0=0, ntt=None, nbanks=8, krows=None, mrows=None, NW=3, after=None):
    TT = c.TT
    ntt = c.NTT if ntt is None else ntt
    wb = [p.sb("%s_b%d" % (wname, i), [128, KC * 128], BF16) for i in range(NW)]
    for j in range(MC):
        wbj = wb[j % NW]
        wn = "%s_b%d" % (wname, j % NW)
        p.dma("pool", wbj[:], w[j], w=[wn])
        ms = 128 if (mrows is None or j < MC - 1) else mrows
        for kc in range(KC):
            ks = 128 if (krows is None or kc < KC - 1) else krows
            for tt in range(ntt):
                bk = p.nbank_of(j, tt, ntt, nbanks)
                p.add("pe", lambda e, kc=kc, tt=tt, bk=bk, wbj=wbj, ks=ks, ms=ms: e.matmul(
                    p.banks[bk][:ms, :TT], lhsT=wbj[:ks, kc * 128:kc * 128 + ms],
                    rhs=abf[:ks, kc, t0 + tt * TT:t0 + (tt + 1) * TT], start=(kc == 0), stop=(kc == KC - 1)),
                    r=[wn, abf_name], w=["bank%d" % bk])
        for tt in range(ntt):
            bk = p.nbank_of(j, tt, ntt, nbanks)
            epi(j, tt, p.banks[bk], "bank%d" % bk, ms)


def _nbank_of(self, j, tt, ntt, nbanks):
    return (j * ntt + tt) % nbanks


P.nbank_of = _nbank_of


def build_ffn_up(c):
    p = P()
    D, F, TC, TT, KC = c.D, c.F, c.TC, c.TT, c.KC
    xT = p.inp("xT", [D, TC])
    g = p.inp("g", [128, KC])
    w = p.inp("w", [2 * F // 128, 128, KC * 128])
    out = p.outp("act", [F, TC], BF16)
    abf = p.sb("abf", [128, KC, TC], BF16)
    emit_norm_bf(p, c, xT, g, abf)
    NW = 3
    wb = [p.sb("wb%d" % i, [128, 2, KC * 128], BF16) for i in range(NW)]
    sg = [p.sb("sg%d" % i, [128, TT]) for i in range(2)]
    ob = [p.sb("ob%d" % i, [128, TC], BF16) for i in range(2)]
    FC = F // 128
    NTT = c.NTT
    for j in range(FC):
        wbj = wb[j % NW]
        wn = "wb%d" % (j % NW)
        p.dma("pool", wbj[:, 0, :], w[j], w=[wn + "g"])
        p.dma("pool", wbj[:, 1, :], w[FC + j], w=[wn + "u"])
        for half in range(2):
            for kc in range(KC):
                for tt in range(NTT):
                    bk = half * 4 + tt
                    p.add("pe", lambda e, half=half, kc=kc, tt=tt, bk=bk, wbj=wbj: e.matmul(
                        p.banks[bk][:, :TT], lhsT=wbj[:, half, kc * 128:(kc + 1) * 128],
                        rhs=abf[:, kc, tt * TT:(tt + 1) * TT], start=(kc == 0), stop=(kc == KC - 1)),
                        r=[wn + "gu"[half], "abf"], w=["bank%d" % bk])
        obj = ob[j % 2]
        on = "ob%d" % (j % 2)
        for tt in range(NTT):
            sgb = sg[tt % 2]
            sn = "sg%d" % (tt % 2)
            p.add("act", lambda e, tt=tt, sgb=sgb: e.activation(out=sgb[:], in_=p.banks[tt][:, :TT], func=AF.Silu),
                  r=["bank%d" % tt], w=[sn])
            p.add("dve", lambda e, tt=tt, sgb=sgb, obj=obj: e.tensor_tensor(
                out=obj[:, tt * TT:(tt + 1) * TT], in0=sgb[:], in1=p.banks[4 + tt][:, :TT], op=ALU.mult),
                r=[sn, "bank%d" % (4 + tt)], w=[on])
        p.dma("sp", out[j * 128:(j + 1) * 128, :], obj[:], r=[on])
    return p.finish()


def emit_resid_gemm(p, c, w, KCI, abf, xT, out, scale, t0, ntt):
    TT = c.TT
    xb = [p.sb("xb%d" % i, [128, ntt * TT]) for i in range(2)]
    ob = [p.sb("rob%d" % i, [128, ntt * TT]) for i in range(2)]

    def epi(j, tt, bank, bname, ms):
        if tt == 0:
            p.dma("sp", xb[j % 2][:], xT[j * 128:(j + 1) * 128, t0:t0 + ntt * TT], w=["xb%d" % (j % 2)])
        p.add("dve", lambda e: e.scalar_tensor_tensor(
            out=ob[j % 2][:, tt * TT:(tt + 1) * TT], in0=bank[:, :TT], scalar=scale, in1=xb[j % 2][:, tt * TT:(tt + 1) * TT],
            op0=ALU.mult, op1=ALU.add), r=[bname, "xb%d" % (j % 2)], w=["rob%d" % (j % 2)])
        if tt == ntt - 1:
            p.dma("sp", out[j * 128:(j + 1) * 128, t0:t0 + ntt * TT], ob[j % 2][:], r=["rob%d" % (j % 2)])

    emit_gemm(p, c, w, "wd", KCI, abf, "abf", c.KC, epi, t0=0, ntt=ntt, nbanks=8)


def build_ffn_down(c):
    p = P()
    D, F, TC, TT = c.D, c.F, c.TC, c.TT
    FC = F // 128
    act = p.inp("act", [F, TC], BF16)
    xT = p.inp("xT", [D, TC])
    w = p.inp("w", [c.KC, 128, FC * 128])
    out = p.outp("xo", [D, TC])
    ngrp = 2 if c.NTT >= 2 else 1
    ntt = c.NTT // ngrp
    abf = p.sb("abf", [128, FC, ntt * TT], BF16)
    av = act.rearrange("(c p) t -> p c t", p=128)
    TTg = ntt * TT
    xb = [p.sb("xb%d" % i, [128, TTg]) for i in range(2)]
    ob = [p.sb("rob%d" % i, [128, TTg]) for i in range(2)]
    NW = 3
    wb = [p.sb("wd_b%d" % i, [128, FC * 128], BF16) for i in range(NW)]
    it = 0
    for gi in range(ngrp):
        t0 = gi * TTg
        p.dma("sp", abf[:], av[:, :, t0:t0 + TTg], w=["abf"])
        for j in range(c.KC):
            wbj = wb[it % NW]
            wn = "wd_b%d" % (it % NW)
            it += 1
            p.dma("pool", wbj[:], w[j], w=[wn])
            for kc in range(FC):
                for tt in range(ntt):
                    bk = (j * ntt + tt) % 8
                    p.add("pe", lambda e, kc=kc, tt=tt, bk=bk, wbj=wbj: e.matmul(
                        p.banks[bk][:, :TT], lhsT=wbj[:, kc * 128:(kc + 1) * 128],
                        rhs=abf[:, kc, tt * TT:(tt + 1) * TT], start=(kc == 0), stop=(kc == FC - 1)),
                        r=[wn, "abf"], w=["bank%d" % bk])
            xbj, obj = xb[j % 2], ob[j % 2]
            p.dma("sp", xbj[:], xT[j * 128:(j + 1) * 128, t0:t0 + TTg], w=["xb%d" % (j % 2)])
            for tt in range(ntt):
                bk = (j * ntt + tt) % 8
                p.add("dve", lambda e, tt=tt, bk=bk, xbj=xbj, obj=obj: e.scalar_tensor_tensor(
                    out=obj[:, tt * TT:(tt + 1) * TT], in0=p.banks[bk][:, :TT], scalar=0.5, in1=xbj[:, tt * TT:(tt + 1) * TT],
                    op0=ALU.mult, op1=ALU.add), r=["bank%d" % bk, "xb%d" % (j % 2)], w=["rob%d" % (j % 2)])
            p.dma("sp", out[j * 128:(j + 1) * 128, t0:t0 + TTg], obj[:], r=["rob%d" % (j % 2)])
    return p.finish()


def build_norm_gemm(c, M):
    p = P()
    D, TC, TT, KC = c.D, c.TC, c.TT, c.KC
    MC = (M + 127) // 128
    xT = p.inp("xT", [D, TC])
    g = p.inp("g", [128, KC])
    w = p.inp("w", [MC, 128, KC * 128])
    out = p.outp("o", [MC * 128, TC])
    abf = p.sb("abf", [128, KC, TC], BF16)
    emit_norm_bf(p, c, xT, g, abf)
    ob = [p.sb("sob%d" % i, [128, TC]) for i in range(2)]

    def epi(j, tt, bank, bname, ms):
        p.add("act", lambda e: e.copy(out=ob[j % 2][:, tt * TT:(tt + 1) * TT], in_=bank[:, :TT]),
              r=[bname], w=["sob%d" % (j % 2)])
        if tt == c.NTT - 1:
            p.dma("sp", out[j * 128:(j + 1) * 128, :], ob[j % 2][:], r=["sob%d" % (j % 2)])

    emit_gemm(p, c, w, "wg", KC, abf, "abf", MC, epi)
    return p.finish()


def build_fox_out(c):
    p = P()
    D, TC, TT, KC = c.D, c.TC, c.TT, c.KC
    oT = p.inp("oT", [D, TC])
    gT = p.inp("gT", [D, TC])
    xT = p.inp("xT", [D, TC])
    w = p.inp("w", [KC, 128, KC * 128])
    out = p.outp("xo", [D, TC])
    abf = p.sb("abf", [128, KC, TC], BF16)
    ot = [p.sb("ot%d" % i, [128, TC]) for i in range(2)]
    gt = [p.sb("gtt%d" % i, [128, TC]) for i in range(2)]
    for kc in range(KC):
        o_, g_ = ot[kc % 2], gt[kc % 2]
        p.dma("sp", o_[:], oT[kc * 128:(kc + 1) * 128, :], w=["ot%d" % (kc % 2)])
        p.dma("sp", g_[:], gT[kc * 128:(kc + 1) * 128, :], w=["gtt%d" % (kc % 2)])
        p.add("act", lambda e, g_=g_: e.activation(out=g_[:], in_=g_[:], func=AF.Sigmoid),
              r=["gtt%d" % (kc % 2)], w=["gtt%d" % (kc % 2)])
        p.add("dve", lambda e, kc=kc, o_=o_, g_=g_: e.tensor_tensor(out=abf[:, kc, :], in0=o_[:], in1=g_[:], op=ALU.mult),
              r=["ot%d" % (kc % 2), "gtt%d" % (kc % 2)], w=["abf"])
    emit_resid_gemm(p, c, w, KC, abf, xT, out, 1.0, 0, c.NTT)
    return p.finish()


def build_fox_attn(c, HPC):
    p = P()
    S = c.S
    NB = S // 128
    QW = min(512, S)
    NQ = S // QW
    scale = 128 ** -0.5
    qT = p.inp("qT", [HPC, 128, S])
    kT = p.inp("kT", [HPC, 128, S])
    v = p.inp("v", [HPC, 128, (S // 128) * 128])
    fl = p.inp("fl", [HPC, S])
    negb = p.inp("negb", [HPC, 128, 1])
    qg = p.inp("qg", [128, 1])
    kg = p.inp("kg", [128, 1])
    masks = p.inp("masks", [4, 128, 512], BF16)
    ustr = p.inp("ustr", [128, 128])
    ident = p.inp("ident", [128, 128])
    oT = p.outp("oT", [HPC, 128, S])
    scr_d = p.nc.dram_tensor("cscr", [HPC, 3, S], BF16, kind="Internal").ap()

    ones = p.const("ones", 1.0, (128, 128))
    onesb = p.sb("onesb", [128, 128], BF16)
    p.add("dve", lambda e: e.tensor_copy(out=onesb[:], in_=ones[:]), r=["ones"], w=["onesb"])
    epst = p.const("epst", 1e-6)
    onec = p.const("onec", 1.0)
    qgs = p.sb("qgs", [128, 1])
    kgs = p.sb("kgs", [128, 1])
    p.dma("sp", qgs[:], qg, w=["qgs"])
    p.dma("sp", kgs[:], kg, w=["kgs"])
    p.add("dve", lambda e: e.tensor_scalar(out=qgs[:], in0=qgs[:], scalar1=scale, scalar2=None, op0=ALU.mult), r=["qgs"], w=["qgs"])
    us = p.sb("us", [128, 128])
    idt = p.sb("idt", [128, 128])
    idb = p.sb("idb", [128, 128], BF16)
    p.dma("sp", us[:], ustr, w=["us"])
    p.dma("sp", idt[:], ident, w=["idt"])
    p.add("dve", lambda e: e.tensor_copy(out=idb[:], in_=idt[:]), r=["idt"], w=["idb"])
    mk = p.sb("mk", [128, 4, 512], BF16)
    p.dma("sp", mk[:], masks.rearrange("d p q -> p d q"), w=["mk"])
    sel = p.sb("sel", [96, 128], BF16)
    p.add("pool", lambda e: e.memset(sel[:], 0.0), w=["sel"])
    for r_ in (0, 32, 64):
        p.add("pool", lambda e, r_=r_: e.memset(sel[r_:r_ + 1, :], -1.0), r=[], w=["sel"])

    qbf = p.sb("qbf", [128, S], BF16)
    kbf = p.sb("kbf", [128, S], BF16)
    vbf = p.sb("vbf", [128, NB, 128], BF16)
    xs = p.sb("xs", [128, 1, QW])
    scr = [p.sb("scr%d" % i, [128, QW]) for i in range(2)]
    rstd = p.sb("rstd", [128, QW])
    flt = p.sb("flt", [128, 128])
    l1 = p.sb("l1", [128, 128])
    cw = p.sb("cw", [128, 128])
    nbt = p.sb("nbt", [128, 1])
    off = p.sb("off", [128, 1])
    cposT = p.sb("cposT", [128, NB])
    crow = p.sb("crow", [96, S], BF16)
    chs = [p.sb("chs%d" % i, [128, 128], BF16) for i in range(3)]
    rr = p.sb("rr", [128, 128])
    pt = [p.sb("pt%d" % i, [128, QW], BF16) for i in range(3)]
    rl = p.sb("rl", [128, QW])
    osb = [p.sb("osb%d" % i, [128, QW]) for i in range(2)]
    p.add("pool", lambda e: e.memset(crow[:], 0.0), w=["crow"])
    LB = [2, 3, 4, 5]
    it = 0
    for h in range(HPC):
        p.dma("sp", flt[:NB, :], fl[h].rearrange("(b w) -> b w", w=128), w=["flt"])
        p.dma("sp", nbt[:], negb[h], w=["nbt"])
        p.add("act", lambda e: e.activation(out=l1[:NB, :], in_=flt[:NB, :], func=AF.Exp, scale=-1.0, bias=nbt[:NB, :]),
              r=["flt", "nbt"], w=["l1"])
        p.add("act", lambda e: e.activation(out=l1[:NB, :], in_=l1[:NB, :], func=AF.Ln, bias=onec[:NB, :]),
              r=["l1", "onec"], w=["l1"])
        p.add("dve", lambda e: e.tensor_tensor_scan(out=cw[:NB, :], data0=ones[:NB, :], data1=l1[:NB, :], initial=0.0,
                                                     op0=ALU.mult, op1=ALU.add), r=["ones", "l1"], w=["cw"])
        p.add("pe", lambda e: e.matmul(p.banks[1][:NB, 0:1], lhsT=us[:NB, :NB], rhs=cw[:NB, 127:128], start=True, stop=True),
              r=["us", "cw"], w=["bank1"])
        p.add("act", lambda e: e.copy(out=off[:NB, :], in_=p.banks[1][:NB, 0:1]), r=["bank1"], w=["off"])
        p.add("dve", lambda e: e.tensor_scalar(out=cw[:NB, :], in0=cw[:NB, :], scalar1=off[:NB, :], scalar2=None, op0=ALU.add),
              r=["cw", "off"], w=["cw"])
        p.add("pe", lambda e: e.transpose(p.banks[1][:, :NB], cw[:NB, :], idt[:NB, :NB]), r=["cw", "idt"], w=["bank1"])
        p.add("act", lambda e: e.copy(out=cposT[:], in_=p.banks[1][:, :NB]), r=["bank1"], w=["cposT"])
        p.add("dve", lambda e: e.tensor_copy(out=chs[0][:NB, :], in_=cw[:NB, :]), r=["cw"], w=["chs0"])
        p.add("dve", lambda e: e.tensor_tensor(out=rr[:NB, :], in0=cw[:NB, :], in1=chs[0][:NB, :], op=ALU.subtract), r=["cw", "chs0"], w=["rr"])
        p.add("dve", lambda e: e.tensor_copy(out=chs[1][:NB, :], in_=rr[:NB, :]), r=["rr"], w=["chs1"])
        p.add("dve", lambda e: e.tensor_tensor(out=rr[:NB, :], in0=rr[:NB, :], in1=chs[1][:NB, :], op=ALU.subtract), r=["rr", "chs1"], w=["rr"])
        p.add("dve", lambda e: e.tensor_copy(out=chs[2][:NB, :], in_=rr[:NB, :]), r=["rr"], w=["chs2"])
        for k3 in range(3):
            p.dma("sp", scr_d[h, k3].rearrange("(b w) -> b w", w=128), chs[k3][:NB, :], r=["chs%d" % k3], w=["cscr%d" % k3])
            p.dma("sp", crow[32 * k3:32 * k3 + 1, :], scr_d[h, k3:k3 + 1, :], r=["cscr%d" % k3], w=["crow"])
        for (src, dst, dn, gsb, gn) in ((qT, qbf, "qbf", qgs, "qgs"), (kT, kbf, "kbf", kgs, "kgs")):
            for qb in range(NQ):
                qs = slice(qb * QW, (qb + 1) * QW)
                p.dma("sp", xs[:, 0, :], src[h][:, qs], w=["xs"])
                emit_rstd(p, xs, "xs", 1, QW, ones, rstd, "rstd", epst, "epst", scr, 128)
                p.add("dve", lambda e, qs=qs, dst=dst, gsb=gsb: e.scalar_tensor_tensor(
                    out=dst[:, qs], in0=xs[:, 0, :], scalar=gsb[:, 0:1], in1=rstd[:], op0=ALU.mult, op1=ALU.mult),
                    r=["xs", gn, "rstd"], w=[dn])
        p.dma("pool", vbf[:], v[h].rearrange("w (b d) -> w b d", d=128), w=["vbf"])
        for qb in range(NQ):
            q0 = qb * QW
            qs = slice(q0, q0 + QW)
            nkb = (q0 + QW) // 128
            for kb in range(nkb):
                lb = LB[it % 4]
                ptb = pt[it % 3]
                pn = "pt%d" % (it % 3)
                it += 1
                d = kb * 128 - q0
                diag = d >= 0
                p.add("pe", lambda e, kb=kb, lb=lb, qs=qs: e.matmul(p.banks[lb][:, :QW], lhsT=kbf[:, kb * 128:(kb + 1) * 128],
                                                                    rhs=qbf[:, qs], start=True, stop=False),
                      r=["kbf", "qbf"], w=["bank%d" % lb])
                p.add("pe", lambda e, lb=lb, qs=qs, diag=diag: e.matmul(p.banks[lb][:, :QW], lhsT=sel[:, :], rhs=crow[:, qs],
                                                                          start=False, stop=(not diag)),
                      r=["sel", "crow"], w=["bank%d" % lb])
                if diag:
                    p.add("pe", lambda e, lb=lb, d=d: e.matmul(p.banks[lb][:, :QW], lhsT=idb[:], rhs=mk[:, d // 128, :QW],
                                                                 start=False, stop=True),
                          r=["idb", "mk"], w=["bank%d" % lb])
                p.add("act", lambda e, kb=kb, lb=lb, ptb=ptb: e.activation(out=ptb[:], in_=p.banks[lb][:, :QW], func=AF.Exp,
                                                                           bias=cposT[:, kb:kb + 1]),
                      r=["bank%d" % lb, "cposT"], w=[pn])
                p.add("pe", lambda e, kb=kb, ptb=ptb, nkb=nkb: e.matmul(p.banks[6][:, :QW], lhsT=vbf[:, kb, :], rhs=ptb[:],
                                                                         start=(kb == 0), stop=(kb == nkb - 1)),
                      r=["vbf", pn], w=["bank6"])
                p.add("pe", lambda e, kb=kb, ptb=ptb, nkb=nkb: e.matmul(p.banks[7][:, :QW], lhsT=onesb[:], rhs=ptb[:],
                                                                         start=(kb == 0), stop=(kb == nkb - 1)),
                      r=["onesb", pn], w=["bank7"])
            ob_ = osb[qb % 2]
            p.add("dve", lambda e: e.reciprocal(out=rl[:], in_=p.banks[7][:, :QW]), r=["bank7"], w=["rl"])
            p.add("dve", lambda e, ob_=ob_: e.tensor_tensor(out=ob_[:], in0=p.banks[6][:, :QW], in1=rl[:], op=ALU.mult),
                  r=["bank6", "rl"], w=["osb%d" % (qb % 2)])
            p.dma("sp", oT[h][:, qs], ob_[:], r=["osb%d" % (qb % 2)])
    return p.finish()


def build_s5_scan(c):
    p = P()
    S = c.S
    LS = min(512, S)
    NSEG = S // LS
    NT = 8
    LV = int(math.log2(LS))
    uT = p.inp("uT", [256, S])
    lre = p.inp("lre", [128, NT])
    lim = p.inp("lim", [128, NT])
    lstep = p.inp("lstep", [128, NT])
    bre = p.inp("bre", [NT, 128, 128])
    bim = p.inp("bim", [NT, 128, 128])
    cre = p.inp("cre", [NT, 128, 128])
    cim = p.inp("cim", [NT, 128, 128])
    dsk = p.inp("dsk", [128, 2])
    yT = p.outp("yT", [256, S])

    def small(name):
        return p.sb(name, [128, NT])

    def tt(out, a, b, op, eng="dve", r=(), w=()):
        p.add(eng, lambda e: e.tensor_tensor(out=out, in0=a, in1=b, op=op), r=r, w=w)

    lr, li, dt_, rho, th = small("lr"), small("li"), small("dt"), small("rho"), small("th")
    p.dma("sp", lr[:], lre, w=["lr"])
    p.dma("sp", li[:], lim, w=["li"])
    p.dma("sp", dt_[:], lstep, w=["dt"])
    dskt = p.sb("dskt", [128, 2])
    p.dma("sp", dskt[:], dsk, w=["dskt"])
    halfpi = p.const("halfpi", math.pi / 2)
    p.add("dve", lambda e: e.tensor_scalar(out=lr[:], in0=lr[:], scalar1=-1e-4, scalar2=None, op0=ALU.min), r=["lr"], w=["lr"])
    p.add("act", lambda e: e.activation(out=dt_[:], in_=dt_[:], func=AF.Exp), r=["dt"], w=["dt"])
    tt(rho[:], lr[:], dt_[:], ALU.mult, r=["lr", "dt"], w=["rho"])
    p.add("act", lambda e: e.activation(out=rho[:], in_=rho[:], func=AF.Exp), r=["rho"], w=["rho"])
    tt(th[:], li[:], dt_[:], ALU.mult, r=["li", "dt"], w=["th"])
    cr = [small("cr%d" % k) for k in range(LV + 1)]
    ci = [small("ci%d" % k) for k in range(LV + 1)]
    a_, b_ = small("tmpa"), small("tmpb")
    p.add("act", lambda e: e.activation(out=b_[:], in_=th[:], func=AF.Sin, scale=1.0 / 16), r=["th"], w=["tmpb"])
    p.add("act", lambda e: e.activation(out=a_[:], in_=th[:], func=AF.Sin, scale=-1.0 / 16, bias=halfpi[:]), r=["th", "halfpi"], w=["tmpa"])

    def csq(or_, oi_, ir_, ii_, names):
        t1, t2 = small("sq1_" + names), small("sq2_" + names)
        tt(t1[:], ir_[:], ir_[:], ALU.mult, r=[names + "ir"], w=[names + "t1"])
        tt(t2[:], ii_[:], ii_[:], ALU.mult, r=[names + "ii"], w=[names + "t2"])
        p.add("dve", lambda e: e.scalar_tensor_tensor(out=oi_[:], in0=ir_[:], scalar=2.0, in1=ii_[:], op0=ALU.mult, op1=ALU.mult),
              r=[names + "ir", names + "ii"], w=[names + "oi"])
        tt(or_[:], t1[:], t2[:], ALU.subtract, r=[names + "t1", names + "t2"], w=[names + "or"])

    chain = [(a_, b_)] + [(small("qa%d" % i), small("qb%d" % i)) for i in range(3)] + [(cr[0], ci[0])]
    for i in range(4):
        ir_, ii_ = chain[i]
        or_, oi_ = chain[i + 1]
        t1, t2 = small("s1_%d" % i), small("s2_%d" % i)
        tt(t1[:], ir_[:], ir_[:], ALU.mult, r=["tmpa", "tmpb", "prel"], w=["prel"])
        tt(t2[:], ii_[:], ii_[:], ALU.mult, r=["prel"], w=["prel"])
        p.add("dve", lambda e, oi_=oi_, ir_=ir_, ii_=ii_: e.scalar_tensor_tensor(out=oi_[:], in0=ir_[:], scalar=2.0, in1=ii_[:], op0=ALU.mult, op1=ALU.mult),
              r=["prel"], w=["prel"])
        tt(or_[:], t1[:], t2[:], ALU.subtract, r=["prel"], w=["prel"])
    for k in range(LV):
        t1, t2 = small("l1_%d" % k), small("l2_%d" % k)
        tt(t1[:], cr[k][:], cr[k][:], ALU.mult, r=["prel"], w=["prel"])
        tt(t2[:], ci[k][:], ci[k][:], ALU.mult, r=["prel"], w=["prel"])
        p.add("dve", lambda e, k=k: e.scalar_tensor_tensor(out=ci[k + 1][:], in0=cr[k][:], scalar=2.0, in1=ci[k][:], op0=ALU.mult, op1=ALU.mult),
              r=["prel"], w=["prel"])
        tt(cr[k + 1][:], t1[:], t2[:], ALU.subtract, r=["prel"], w=["prel"])
    den, nr, ni, qre, qim, t3 = small("den"), small("nr"), small("ni"), small("qre"), small("qim"), small("t3")
    tt(den[:], lr[:], lr[:], ALU.mult, r=["lr"], w=["prel"])
    tt(t3[:], li[:], li[:], ALU.mult, r=["li"], w=["prel"])
    tt(den[:], den[:], t3[:], ALU.add, r=["prel"], w=["prel"])
    p.add("dve", lambda e: e.reciprocal(out=den[:], in_=den[:]), r=["prel"], w=["prel"])
    tt(nr[:], rho[:], cr[0][:], ALU.mult, r=["rho", "prel"], w=["prel"])
    p.add("dve", lambda e: e.tensor_scalar(out=nr[:], in0=nr[:], scalar1=-1.0, scalar2=None, op0=ALU.add), r=["prel"], w=["prel"])
    tt(ni[:], rho[:], ci[0][:], ALU.mult, r=["rho", "prel"], w=["prel"])
    tt(qre[:], nr[:], lr[:], ALU.mult, r=["prel", "lr"], w=["prel"])
    tt(t3[:], ni[:], li[:], ALU.mult, r=["prel", "li"], w=["prel"])
    tt(qre[:], qre[:], t3[:], ALU.add, r=["prel"], w=["prel"])
    tt(qre[:], qre[:], den[:], ALU.mult, r=["prel"], w=["prel"])
    tt(qim[:], ni[:], lr[:], ALU.mult, r=["prel", "lr"], w=["prel"])
    tt(t3[:], nr[:], li[:], ALU.mult, r=["prel", "li"], w=["prel"])
    tt(qim[:], qim[:], t3[:], ALU.subtract, r=["prel"], w=["prel"])
    tt(qim[:], qim[:], den[:], ALU.mult, r=["prel"], w=["prel"])
    Er = p.sb("Er", [128, NT, LS])
    Ei = p.sb("Ei", [128, NT, LS])
    Fr = p.sb("Fr", [128, NT, LS])
    Fi = p.sb("Fi", [128, NT, LS])
    tmpL = p.sb("tmpL", [128, LS])
    p.add("pool", lambda e: e.memset(Er[:, :, 0:1], 1.0), w=["E"])
    p.add("pool", lambda e: e.memset(Ei[:, :, 0:1], 0.0), w=["E"])
    for j in range(NT):
        for k in range(LV):
            n = 1 << k
            crk, cik = cr[k][:, j:j + 1], ci[k][:, j:j + 1]
            p.add("dve", lambda e, j=j, n=n, cik=cik: e.tensor_scalar(out=tmpL[:, :n], in0=Ei[:, j, 0:n], scalar1=cik, scalar2=None, op0=ALU.mult),
                  r=["E", "prel"], w=["tmpL"])
            p.add("dve", lambda e, j=j, n=n, crk=crk: e.scalar_tensor_tensor(out=Er[:, j, n:2 * n], in0=Er[:, j, 0:n], scalar=crk, in1=tmpL[:, :n],
                                                                              op0=ALU.mult, op1=ALU.subtract), r=["E", "prel", "tmpL"], w=["Ern"])
            p.add("dve", lambda e, j=j, n=n, cik=cik: e.tensor_scalar(out=tmpL[:, :n], in0=Er[:, j, 0:n], scalar1=cik, scalar2=None, op0=ALU.mult),
                  r=["E", "prel", "Ern"], w=["tmpL"])
            p.add("dve", lambda e, j=j, n=n, crk=crk: e.scalar_tensor_tensor(out=Ei[:, j, n:2 * n], in0=Ei[:, j, 0:n], scalar=crk, in1=tmpL[:, :n],
                                                                              op0=ALU.mult, op1=ALU.add), r=["E", "prel", "tmpL"], w=["E"])
            p.add("dve", lambda e: e.engine_nop(), r=["Ern"], w=["E"]) if False else None
        qr_, qi_ = qre[:, j:j + 1], qim[:, j:j + 1]
        p.add("dve", lambda e, j=j, qi_=qi_: e.tensor_scalar(out=tmpL[:], in0=Ei[:, j, :], scalar1=qi_, scalar2=None, op0=ALU.mult),
              r=["E", "Ern", "prel"], w=["tmpL"])
        p.add("dve", lambda e, j=j, qr_=qr_: e.scalar_tensor_tensor(out=Fr[:, j, :], in0=Er[:, j, :], scalar=qr_, in1=tmpL[:], op0=ALU.mult, op1=ALU.add),
              r=["E", "Ern", "prel", "tmpL"], w=["F"])
        p.add("dve", lambda e, j=j, qr_=qr_: e.tensor_scalar(out=tmpL[:], in0=Ei[:, j, :], scalar1=qr_, scalar2=None, op0=ALU.mult),
              r=["E", "Ern", "prel", "F"], w=["tmpL"])
        p.add("dve", lambda e, j=j, qi_=qi_: e.scalar_tensor_tensor(out=Fi[:, j, :], in0=Er[:, j, :], scalar=qi_, in1=tmpL[:], op0=ALU.mult, op1=ALU.subtract),
              r=["E", "Ern", "prel", "tmpL"], w=["F"])
    bre_b = p.sb("bre_b", [128, NT, 128], BF16)
    bim_b = p.sb("bim_b", [128, NT, 128], BF16)
    cre_b = p.sb("cre_b", [128, NT, 128], BF16)
    cim_b = p.sb("cim_b", [128, NT, 128], BF16)
    for (dst, src, n_) in ((bre_b, bre, "bre_b"), (bim_b, bim, "bim_b"), (cre_b, cre, "cre_b"), (cim_b, cim, "cim_b")):
        p.dma("pool", dst[:], src.rearrange("j k m -> k j m"), w=[n_])
    ir_ = small("init_r")
    ii_ = small("init_i")
    p.add("pool", lambda e: e.memset(ir_[:], 0.0), w=["init"])
    p.add("pool", lambda e: e.memset(ii_[:], 0.0), w=["init"])
    ubf = [p.sb("ubf%d" % i, [128, 2, LS], BF16) for i in range(2)]
    uf = [p.sb("uf%d" % i, [128, 2, LS]) for i in range(2)]
    W = {}
    for nm in ("t1", "t2", "xr", "xi", "hr", "hi", "a", "b", "c", "d"):
        W[nm] = [p.sb("w_%s%d" % (nm, i), [128, LS]) for i in range(2)]
    hrb = [p.sb("hrb%d" % i, [128, LS], BF16) for i in range(2)]
    hib = [p.sb("hib%d" % i, [128, LS], BF16) for i in range(2)]
    yt = p.sb("yt", [128, LS])
    y2 = p.sb("y2", [128, LS])
    yo = [p.sb("yo%d" % i, [128, LS]) for i in range(2)]
    ELr, ELi = cr[LV], ci[LV]
    sc1s = [small("sc1_%d" % i) for i in range(2)]
    uv = uT.rearrange("(c p) t -> p c t", p=128)
    it = 0
    for sg in range(NSEG):
        ts = slice(sg * LS, (sg + 1) * LS)
        ub, uf_ = ubf[sg % 2], uf[sg % 2]
        ubn, ufn = "ubf%d" % (sg % 2), "uf%d" % (sg % 2)
        p.dma("pool", ub[:], uv[:, :, ts], w=[ubn])
        p.dma("sp", uf_[:], uv[:, :, ts], w=[ufn])
        for j in range(NT):
            cc = j // 4
            k_ = it % 2
            it += 1
            bA, bB = 2 + 2 * k_, 3 + 2 * k_
            nA, nB = "bank%d" % bA, "bank%d" % bB
            g = lambda nm: (W[nm][k_], "w_%s%d" % (nm, k_))
            p.add("pe", lambda e, j=j, cc=cc, bA=bA, ub=ub: e.matmul(p.banks[bA][:, :LS], lhsT=bre_b[:, j, :], rhs=ub[:, cc, :], start=True, stop=True),
                  r=["bre_b", ubn], w=[nA])
            p.add("pe", lambda e, j=j, cc=cc, bB=bB, ub=ub: e.matmul(p.banks[bB][:, :LS], lhsT=bim_b[:, j, :], rhs=ub[:, cc, :], start=True, stop=True),
                  r=["bim_b", ubn], w=[nB])
            (t1, t1n), (t2, t2n), (xr, xrn), (xi, xin) = g("t1"), g("t2"), g("xr"), g("xi")
            (hr, hrn), (hi, hin), (a, an), (b, bn), (c_, cn), (d_, dn) = g("hr"), g("hi"), g("a"), g("b"), g("c"), g("d")
            tt(t1[:], Fi[:, j, :], p.banks[bB][:, :LS], ALU.mult, r=["F", nB], w=[t1n])
            tt(xr[:], Fr[:, j, :], p.banks[bA][:, :LS], ALU.mult, r=["F", nA], w=[xrn])
            tt(t2[:], Fi[:, j, :], p.banks[bA][:, :LS], ALU.mult, r=["F", nA], w=[t2n])
            tt(xi[:], Fr[:, j, :], p.banks[bB][:, :LS], ALU.mult, r=["F", nB], w=[xin])
            tt(xr[:], xr[:], t1[:], ALU.subtract, eng="pool", r=[xrn, t1n], w=[xrn])
            tt(xi[:], xi[:], t2[:], ALU.add, eng="pool", r=[xin, t2n], w=[xin])
            rb = rho[:, j:j + 1].to_broadcast([128, LS])
            p.add("dve", lambda e, j=j, rb=rb, hr=hr, xr=xr: e.tensor_tensor_scan(out=hr[:], data0=rb, data1=xr[:], initial=ir_[:, j:j + 1],
                                                                                 op0=ALU.mult, op1=ALU.add), r=["rho", xrn, "init"], w=[hrn])
            p.add("dve", lambda e, j=j, rb=rb, hi=hi, xi=xi: e.tensor_tensor_scan(out=hi[:], data0=rb, data1=xi[:], initial=ii_[:, j:j + 1],
                                                                                 op0=ALU.mult, op1=ALU.add), r=["rho", xin, "init"], w=[hin])
            sc1 = sc1s[k_]
            p.add("dve", lambda e, j=j, hi=hi, sc1=sc1: e.tensor_scalar(out=sc1[:, 0:1], in0=hi[:, LS - 1:LS], scalar1=ELi[:, j:j + 1], scalar2=None, op0=ALU.mult),
                  r=[hin, "prel"], w=["sc1_%d" % k_])
            p.add("dve", lambda e, j=j, hr=hr, sc1=sc1: e.scalar_tensor_tensor(out=ir_[:, j:j + 1], in0=hr[:, LS - 1:LS], scalar=ELr[:, j:j + 1], in1=sc1[:, 0:1],
                                                                               op0=ALU.mult, op1=ALU.subtract), r=[hrn, "prel", "sc1_%d" % k_, "init"], w=["init"])
            p.add("dve", lambda e, j=j, hr=hr, sc1=sc1: e.tensor_scalar(out=sc1[:, 1:2], in0=hr[:, LS - 1:LS], scalar1=ELi[:, j:j + 1], scalar2=None, op0=ALU.mult),
                  r=[hrn, "prel"], w=["sc1_%d" % k_])
            p.add("dve", lambda e, j=j, hi=hi, sc1=sc1: e.scalar_tensor_tensor(out=ii_[:, j:j + 1], in0=hi[:, LS - 1:LS], scalar=ELr[:, j:j + 1], in1=sc1[:, 1:2],
                                                                               op0=ALU.mult, op1=ALU.add), r=[hin, "prel", "sc1_%d" % k_, "init"], w=["init"])
            tt(a[:], Er[:, j, :], hr[:], ALU.mult, eng="pool", r=["E", "Ern", hrn], w=[an])
            tt(b[:], Ei[:, j, :], hi[:], ALU.mult, eng="pool", r=["E", "Ern", hin], w=[bn])
            tt(c_[:], Er[:, j, :], hi[:], ALU.mult, eng="dve", r=["E", "Ern", hin], w=[cn])
            tt(d_[:], Ei[:, j, :], hr[:], ALU.mult, eng="dve", r=["E", "Ern", hrn], w=[dn])
            hb_, ib_ = hrb[k_], hib[k_]
            tt(hb_[:], a[:], b[:], ALU.subtract, eng="pool", r=[an, bn], w=["hrb%d" % k_])
            p.add("dve", lambda e, c_=c_, d_=d_, ib_=ib_: e.scalar_tensor_tensor(out=ib_[:], in0=c_[:], scalar=-1.0, in1=d_[:], op0=ALU.mult, op1=ALU.subtract),
                  r=[cn, dn], w=["hib%d" % k_])
            yb = 6 + (cc % 2)
            p.add("pe", lambda e, j=j, yb=yb, hb_=hb_: e.matmul(p.banks[yb][:, :LS], lhsT=cre_b[:, j, :], rhs=hb_[:], start=(j % 4 == 0), stop=False),
                  r=["cre_b", "hrb%d" % k_], w=["bank%d" % yb])
            p.add("pe", lambda e, j=j, yb=yb, ib_=ib_: e.matmul(p.banks[yb][:, :LS], lhsT=cim_b[:, j, :], rhs=ib_[:], start=False, stop=(j % 4 == 3)),
                  r=["cim_b", "hib%d" % k_], w=["bank%d" % yb])
            if j % 4 == 3:
                yo_ = yo[cc % 2]
                yon = "yo%d" % (cc % 2)
                p.add("dve", lambda e, cc=cc, yb=yb, uf_=uf_: e.scalar_tensor_tensor(out=yt[:], in0=uf_[:, cc, :], scalar=dskt[:, cc:cc + 1], in1=p.banks[yb][:, :LS],
                                                                                    op0=ALU.mult, op1=ALU.add), r=[ufn, "dskt", "bank%d" % yb], w=["yt"])
                p.add("act", lambda e: e.activation(out=y2[:], in_=yt[:], func=AF.Square), r=["yt"], w=["y2"])
                p.add("dve", lambda e: e.tensor_scalar(out=y2[:], in0=y2[:], scalar1=0.044715, scalar2=1.0, op0=ALU.mult, op1=ALU.add), r=["y2"], w=["y2"])
                tt(y2[:], y2[:], yt[:], ALU.mult, r=["y2", "yt"], w=["y2"])
                p.add("act", lambda e: e.activation(out=y2[:], in_=y2[:], func=AF.Tanh, scale=math.sqrt(2.0 / math.pi)), r=["y2"], w=["y2"])
                p.add("dve", lambda e: e.tensor_scalar(out=y2[:], in0=y2[:], scalar1=0.5, scalar2=0.5, op0=ALU.mult, op1=ALU.add), r=["y2"], w=["y2"])
                tt(yo_[:], y2[:], yt[:], ALU.mult, r=["y2", "yt"], w=[yon])
                p.dma("sp", yT[cc * 128:(cc + 1) * 128, ts], yo_[:], r=[yon])
    return p.finish()


def build_s5_out(c):
    p = P()
    D, TC, TT, KC, NTT = c.D, c.TC, c.TT, c.KC, c.NTT
    yT = p.inp("yT", [D, TC])
    xT = p.inp("xT", [D, TC])
    w = p.inp("w", [2 * KC, 128, KC * 128])
    out = p.outp("xo", [D, TC])
    abf = p.sb("abf", [128, KC, TC], BF16)
    p.dma("pool", abf[:], yT.rearrange("(c p) t -> p c t", p=128), w=["abf"])
    NW = 3
    wb = [p.sb("wb%d" % i, [128, 2, KC * 128], BF16) for i in range(NW)]
    sg = [p.sb("sg%d" % i, [128, TT]) for i in range(2)]
    xb = [p.sb("xb%d" % i, [128, TC]) for i in range(2)]
    ob = [p.sb("ob%d" % i, [128, TC]) for i in range(2)]
    for j in range(KC):
        wbj = wb[j % NW]
        wn = "wb%d" % (j % NW)
        p.dma("pool", wbj[:, 0, :], w[j], w=[wn + "g"])
        p.dma("pool", wbj[:, 1, :], w[KC + j], w=[wn + "u"])
        for half in range(2):
            for kc in range(KC):
                for tt in range(NTT):
                    bk = half * 4 + tt
                    p.add("pe", lambda e, half=half, kc=kc, tt=tt, bk=bk, wbj=wbj: e.matmul(
                        p.banks[bk][:, :TT], lhsT=wbj[:, half, kc * 128:(kc + 1) * 128],
                        rhs=abf[:, kc, tt * TT:(tt + 1) * TT], start=(kc == 0), stop=(kc == KC - 1)),
                        r=[wn + "gu"[half], "abf"], w=["bank%d" % bk])
        obj, xbj = ob[j % 2], xb[j % 2]
        on, xn = "ob%d" % (j % 2), "xb%d" % (j % 2)
        p.dma("sp", xbj[:], xT[j * 128:(j + 1) * 128, :], w=[xn])
        for tt in range(NTT):
            sgb = sg[tt % 2]
            sn = "sg%d" % (tt % 2)
            tsl = slice(tt * TT, (tt + 1) * TT)
            p.add("act", lambda e, tt=tt, sgb=sgb: e.activation(out=sgb[:], in_=p.banks[4 + tt][:, :TT], func=AF.Sigmoid),
                  r=["bank%d" % (4 + tt)], w=[sn])
            p.add("dve", lambda e, tt=tt, sgb=sgb: e.tensor_tensor(out=sgb[:], in0=sgb[:], in1=p.banks[tt][:, :TT], op=ALU.mult),
                  r=[sn, "bank%d" % tt], w=[sn])
            p.add("pool", lambda e, sgb=sgb, obj=obj, xbj=xbj, tsl=tsl: e.tensor_tensor(out=obj[:, tsl], in0=sgb[:], in1=xbj[:, tsl], op=ALU.add),
                  r=[sn, xn], w=[on])
        p.dma("sp", out[j * 128:(j + 1) * 128, :], obj[:], r=[on])
    return p.finish()


def build_rwkv_pre(c):
    p = P()
    D, TC, KC = c.D, c.TC, c.KC
    RT = min(256, TC)
    NRT = TC // RT
    LW = max(32, int(round(1.8 * D ** 0.5 / 32)) * 32)
    LG = max(32, int(round(0.6 * D ** 0.8 / 32)) * 32)
    LGC = (LG + 127) // 128
    xe = p.inp("xe", [D, TC + 1])
    g = p.inp("g", [128, KC])
    mu = p.inp("mu", [128, 6, KC])
    wr = p.inp("wr", [KC, 128, KC * 128])
    wk = p.inp("wk", [KC, 128, KC * 128])
    wv = p.inp("wv", [KC, 128, KC * 128])
    w1 = p.inp("w1", [1, 128, KC * 128])
    a1 = p.inp("a1", [1, 128, KC * 128])
    g1 = p.inp("g1", [LGC, 128, KC * 128])
    w2 = p.inp("w2", [KC, 128, 128])
    a2 = p.inp("a2", [KC, 128, 128])
    g2 = p.inp("g2", [KC, 128, LGC * 128])
    vecs = p.inp("vecs", [128, 4, KC])
    bd = p.inp("bd", [128, 128])
    outs = {n: p.outp(n, [D, TC]) for n in ("rT", "wT", "kT", "vT", "aT", "bT", "gT")}

    ones = p.const("ones", 1.0, (128, 128))
    epst = p.const("epst", 1e-6)
    gt = p.sb("gt", [128, KC])
    mut = p.sb("mut", [128, 6, KC])
    vt = p.sb("vt", [128, 4, KC])
    bdt = p.sb("bdt", [128, 128])
    p.dma("sp", gt[:], g, w=["gt"])
    p.dma("sp", mut[:], mu, w=["mut"])
    p.dma("sp", vt[:], vecs, w=["vt"])
    p.dma("sp", bdt[:], bd, w=["bdt"])
    xs = p.sb("xs", [128, KC, RT + 1])
    hh = p.sb("hh", [128, KC, RT + 1])
    xx = p.sb("xx", [128, KC, RT])
    scr = [p.sb("scr%d" % i, [128, RT + 1]) for i in range(2)]
    rstd = p.sb("rstd", [128, RT + 1])
    xi = [p.sb("xi%d" % i, [128, KC, RT], BF16) for i in range(2)]
    kf = p.sb("kf", [128, KC, RT])
    af = p.sb("af", [128, KC, RT])
    lo = p.sb("lo", [128, LGC, RT], BF16)
    wbig = [p.sb("wbig%d" % i, [128, KC * 128], BF16) for i in range(3)]
    wsm = [p.sb("wsm%d" % i, [128, LGC * 128], BF16) for i in range(2)]
    st = [p.sb("st%d" % i, [128, RT]) for i in range(4)]
    tmp = [p.sb("tmp%d" % i, [128, RT]) for i in range(4)]
    xv = xe.rearrange("(c p) t -> p c t", p=128)
    cnt = {"w": 0, "s": 0, "st": 0, "bank": 0}

    def lerp(i, buf):
        for kc in range(KC):
            p.add("dve", lambda e, kc=kc: e.scalar_tensor_tensor(out=xi[buf][:, kc, :], in0=xx[:, kc, :], scalar=mut[:, i, kc:kc + 1],
                                                                  in1=hh[:, kc, 1:RT + 1], op0=ALU.mult, op1=ALU.add),
                  r=["xx", "hh", "mut"], w=["xi%d" % buf])

    def gemm(wd, MC, KCI, rhs_of, rhs_name, epi, small=False, ks_last=128):
        for j in range(MC):
            if small:
                wb_, wn = wsm[cnt["s"] % 2], "wsm%d" % (cnt["s"] % 2)
                cnt["s"] += 1
            else:
                wb_, wn = wbig[cnt["w"] % 3], "wbig%d" % (cnt["w"] % 3)
                cnt["w"] += 1
            p.dma("pool", wb_[:, :KCI * 128], wd[j], w=[wn])
            bk = 1 + cnt["bank"] % 7
            cnt["bank"] += 1
            for kc in range(KCI):
                p.add("pe", lambda e, kc=kc, bk=bk, wb_=wb_: e.matmul(p.banks[bk][:, :RT], lhsT=wb_[:, kc * 128:(kc + 1) * 128], rhs=rhs_of(kc),
                                                                      start=(kc == 0), stop=(kc == KCI - 1)), r=[wn, rhs_name], w=["bank%d" % bk])
            epi(j, p.banks[bk], "bank%d" % bk)

    def store_epi(name, t0):
        def epi(j, bank, bn):
            s_ = st[cnt["st"] % 4]
            sn = "st%d" % (cnt["st"] % 4)
            cnt["st"] += 1
            p.add("act", lambda e: e.copy(out=s_[:], in_=bank[:, :RT]), r=[bn], w=[sn])
            p.dma("sp", outs[name][j * 128:(j + 1) * 128, t0:t0 + RT], s_[:], r=[sn])
        return epi

    def store(name, j, t0, src, srcname):
        p.dma("sp", outs[name][j * 128:(j + 1) * 128, t0:t0 + RT], src, r=[srcname])

    for tt in range(NRT):
        t0 = tt * RT
        p.dma("sp", xs[:], xv[:, :, t0:t0 + RT + 1], w=["xs"])
        emit_rstd(p, xs, "xs", KC, RT + 1, ones, rstd, "rstd", epst, "epst", scr, D)
        for kc in range(KC):
            p.add("dve", lambda e, kc=kc: e.scalar_tensor_tensor(out=hh[:, kc, :], in0=xs[:, kc, :], scalar=gt[:, kc:kc + 1], in1=rstd[:],
                                                                  op0=ALU.mult, op1=ALU.mult), r=["xs", "gt", "rstd"], w=["hh"])
        p.add("pool", lambda e: e.tensor_tensor(out=xx[:], in0=hh[:, :, 0:RT], in1=hh[:, :, 1:RT + 1], op=ALU.subtract), r=["hh"], w=["xx"])
        lerp(0, 0)
        gemm(wr, KC, KC, lambda kc: xi[0][:, kc, :], "xi0", store_epi("rT", t0))
        lerp(2, 1)

        def k_epi(j, bank, bn):
            p.add("act", lambda e: e.copy(out=kf[:, j, :], in_=bank[:, :RT]), r=[bn], w=["kf"])
        gemm(wk, KC, KC, lambda kc: xi[1][:, kc, :], "xi1", k_epi)
        lerp(3, 0)
        gemm(wv, KC, KC, lambda kc: xi[0][:, kc, :], "xi0", store_epi("vT", t0))
        lerp(1, 1)

        def w1_epi(j, bank, bn):
            p.add("act", lambda e: e.activation(out=lo[:, 0, :], in_=bank[:, :RT], func=AF.Tanh), r=[bn], w=["lo"])
        gemm(w1, 1, KC, lambda kc: xi[1][:, kc, :], "xi1", w1_epi)

        def w2_epi(j, bank, bn):
            s_ = st[cnt["st"] % 4]
            sn = "st%d" % (cnt["st"] % 4)
            cnt["st"] += 1
            p.add("act", lambda e: e.activation(out=s_[:], in_=bank[:, :RT], func=AF.Sigmoid, bias=vt[:, 0, j:j + 1]), r=[bn, "vt"], w=[sn])
            p.add("act", lambda e: e.activation(out=s_[:], in_=s_[:], func=AF.Exp, scale=-math.exp(-0.5)), r=[sn], w=[sn])
            store("wT", j, t0, s_[:], sn)
        gemm(w2, KC, 1, lambda kc: lo[:, 0, :], "lo", w2_epi, small=True)
        lerp(4, 0)

        def a1_epi(j, bank, bn):
            p.add("act", lambda e: e.copy(out=lo[:, 0, :], in_=bank[:, :RT]), r=[bn], w=["lo"])
        gemm(a1, 1, KC, lambda kc: xi[0][:, kc, :], "xi0", a1_epi)

        def a2_epi(j, bank, bn):
            p.add("act", lambda e: e.activation(out=af[:, j, :], in_=bank[:, :RT], func=AF.Sigmoid, bias=vt[:, 1, j:j + 1]), r=[bn, "vt"], w=["af"])
        gemm(a2, KC, 1, lambda kc: lo[:, 0, :], "lo", a2_epi, small=True)
        lerp(5, 1)

        def g1_epi(j, bank, bn):
            p.add("act", lambda e: e.activation(out=lo[:, j, :], in_=bank[:, :RT], func=AF.Sigmoid), r=[bn], w=["lo"])
        gemm(g1, LGC, KC, lambda kc: xi[1][:, kc, :], "xi1", g1_epi)
        gemm(g2, KC, LGC, lambda kc: lo[:, kc, :], "lo", store_epi("gT", t0), small=True)
        for kc in range(KC):
            kk, sq, rn, t4 = tmp
            p.add("dve", lambda e, kc=kc: e.tensor_scalar(out=kk[:], in0=kf[:, kc, :], scalar1=vt[:, 2, kc:kc + 1], scalar2=None, op0=ALU.mult),
                  r=["kf", "vt"], w=["tmp0"])
            p.add("act", lambda e: e.activation(out=sq[:], in_=kk[:], func=AF.Square), r=["tmp0"], w=["tmp1"])
            p.add("pe", lambda e: e.matmul(p.banks[0][:, :RT], lhsT=bdt[:], rhs=sq[:], start=True, stop=True), r=["bdt", "tmp1"], w=["bank0"])
            p.add("act", lambda e: e.activation(out=rn[:], in_=p.banks[0][:, :RT], func=AF.Sqrt), r=["bank0"], w=["tmp2"])
            p.add("dve", lambda e: e.tensor_scalar(out=rn[:], in0=rn[:], scalar1=1e-12, scalar2=None, op0=ALU.max), r=["tmp2"], w=["tmp2"])
            p.add("dve", lambda e: e.reciprocal(out=rn[:], in_=rn[:]), r=["tmp2"], w=["tmp2"])
            p.add("dve", lambda e: e.tensor_tensor(out=kk[:], in0=kk[:], in1=rn[:], op=ALU.mult), r=["tmp0", "tmp2"], w=["tmp0"])
            s_a = st[cnt["st"] % 4]; na = "st%d" % (cnt["st"] % 4); cnt["st"] += 1
            s_b = st[cnt["st"] % 4]; nb = "st%d" % (cnt["st"] % 4); cnt["st"] += 1
            s_k = st[cnt["st"] % 4]; nk = "st%d" % (cnt["st"] % 4); cnt["st"] += 1
            p.add("act", lambda e, s_a=s_a: e.mul(out=s_a[:], in_=kk[:], mul=-1.0), r=["tmp0"], w=[na])
            store("aT", kc, t0, s_a[:], na)
            p.add("pool", lambda e, kc=kc, s_b=s_b: e.tensor_tensor(out=s_b[:], in0=kk[:], in1=af[:, kc, :], op=ALU.mult), r=["tmp0", "af"], w=[nb])
            store("bT", kc, t0, s_b[:], nb)
            p.add("dve", lambda e, kc=kc: e.tensor_scalar(out=t4[:], in0=af[:, kc, :], scalar1=-1.0, scalar2=vt[:, 3, kc:kc + 1], op0=ALU.add, op1=ALU.mult),
                  r=["af", "vt"], w=["tmp3"])
            p.add("dve", lambda e, kc=kc, s_k=s_k: e.scalar_tensor_tensor(out=s_k[:], in0=t4[:], scalar=1.0, in1=kf[:, kc, :], op0=ALU.add, op1=ALU.mult),
                  r=["tmp3", "kf"], w=[nk])
            store("kT", kc, t0, s_k[:], nk)
    return p.finish()


def build_rwkv_scan(c):
    p = P()
    S = c.S
    CH = 8
    NCH = S // CH
    names = ("w", "k", "a", "b", "r")
    xin = {n: p.inp(n + "h", [4, S * 64]) for n in names}
    vin = p.inp("vv", [128, S, 2])
    sel = p.inp("sel", [2, 4, 128])
    yout = p.outp("yy", [128, S, 2])
    selt = p.sb("selt", [4, 2, 128])
    p.dma("sp", selt[:], sel.rearrange("g h m -> h g m"), w=["selt"])
    LD = 4
    xh = {n: [p.sb("xh_%s%d" % (n, i), [4, LD * CH * 64]) for i in range(2)] for n in names}
    NB = 3
    xb = {n: [p.sb("xb_%s%d" % (n, i), [128, CH, 2, 64]) for i in range(NB)] for n in names}
    kv = [p.sb("kv%d" % i, [128, CH, 2, 64]) for i in range(NB)]
    vb = [p.sb("vb%d" % i, [128, LD * CH, 2]) for i in range(2)]
    yb = [p.sb("yb%d" % i, [128, LD * CH, 2]) for i in range(2)]
    St = p.sb("St", [128, 2, 64])
    SW = p.sb("SW", [128, 2, 64])
    m = p.sb("m", [128, 2, 64])
    T = p.sb("T", [128, 2, 64])
    mrb = [p.sb("mrb%d" % i, [128, CH, 2, 64]) for i in range(NB)]
    sa = p.sb("sa", [128, 2])
    p.add("pool", lambda e: e.memset(St[:], 0.0), w=["St"])
    bankc = 0
    for ch in range(NCH):
        ld, li = divmod(ch, LD)
        if li == 0:
            for n in names:
                p.dma("sp", xh[n][ld % 2][:], xin[n][:, ld * LD * CH * 64:(ld + 1) * LD * CH * 64], w=["xh_%s%d" % (n, ld % 2)])
            p.dma("sp", vb[ld % 2][:], vin[:, ld * LD * CH:(ld + 1) * LD * CH, :], w=["vb%d" % (ld % 2)])
        b_ = ch % NB
        for n in names:
            for gi in range(2):
                bk = bankc % 8
                bankc += 1
                p.add("pe", lambda e, n=n, gi=gi, bk=bk, ld=ld, li=li: e.matmul(
                    p.banks[bk][:, :CH * 64], lhsT=selt[:, gi, :], rhs=xh[n][ld % 2][:, li * CH * 64:(li + 1) * CH * 64], start=True, stop=True),
                    r=["selt", "xh_%s%d" % (n, ld % 2)], w=["bank%d" % bk])
                p.add("act", lambda e, n=n, gi=gi, bk=bk, b_=b_: e.copy(out=xb[n][b_][:, :, gi, :],
                                                                       in_=p.banks[bk][:, :CH * 64].rearrange("p (t j) -> p t j", j=64)),
                      r=["bank%d" % bk], w=["xb_%s%d" % (n, b_)])
        vsl = vb[ld % 2][:, li * CH:(li + 1) * CH, :]
        p.add("pool", lambda e, b_=b_, vsl=vsl: e.tensor_tensor(out=kv[b_][:], in0=xb["k"][b_][:], in1=vsl.unsqueeze(3).to_broadcast([128, CH, 2, 64]), op=ALU.mult),
              r=["xb_k%d" % b_, "vb%d" % (ld % 2)], w=["kv%d" % b_])
        for s_ in range(CH):
            A, W_, B, R = (xb[n][b_][:, s_, :, :] for n in ("a", "w", "b", "r"))
            an, wn, bn, rn = ("xb_%s%d" % (n, b_) for n in ("a", "w", "b", "r"))
            p.add("dve", lambda e, A=A: e.tensor_tensor(out=m[:], in0=St[:], in1=A, op=ALU.mult), r=["St", an], w=["m"])
            p.add("pool", lambda e, W_=W_: e.tensor_tensor(out=SW[:], in0=St[:], in1=W_, op=ALU.mult), r=["St", wn], w=["SW"])
            p.add("dve", lambda e: e.tensor_reduce(out=sa[:], in_=m[:], axis=AX.X, op=ALU.add), r=["m"], w=["sa"])
            p.add("dve", lambda e, B=B: e.tensor_tensor(out=T[:], in0=B, in1=sa[:].unsqueeze(2).to_broadcast([128, 2, 64]), op=ALU.mult),
                  r=[bn, "sa"], w=["T"])
            p.add("dve", lambda e, s_=s_, b_=b_: e.tensor_tensor(out=T[:], in0=T[:], in1=kv[b_][:, s_, :, :], op=ALU.add), r=["T", "kv%d" % b_], w=["T"])
            p.add("dve", lambda e: e.tensor_tensor(out=St[:], in0=SW[:], in1=T[:], op=ALU.add), r=["SW", "T"], w=["St"])
            p.add("pool", lambda e, R=R, s_=s_, b_=b_: e.tensor_tensor(out=mrb[b_][:, s_, :, :], in0=St[:], in1=R, op=ALU.mult), r=["St", rn], w=["mrb%d" % b_])
        p.add("dve", lambda e, ld=ld, li=li, b_=b_: e.tensor_reduce(out=yb[ld % 2][:, li * CH:(li + 1) * CH, :], in_=mrb[b_][:], axis=AX.X, op=ALU.add),
              r=["mrb%d" % b_], w=["yb%d" % (ld % 2)])
        if li == LD - 1:
            p.dma("sp", yout[:, ld * LD * CH:(ld + 1) * LD * CH, :], yb[ld % 2][:], r=["yb%d" % (ld % 2)])
    return p.finish()


def build_rwkv_post(c):
    p = P()
    D, TC, TT, KC, NTT = c.D, c.TC, c.TT, c.KC, c.NTT
    ins = {n: p.inp(n, [D, TC]) for n in ("yT", "rT", "kT", "vT", "gT", "xT")}
    vecs = p.inp("vecs", [128, 3, KC])
    bd = p.inp("bd", [128, 128])
    w = p.inp("w", [KC, 128, KC * 128])
    out = p.outp("xo", [D, TC])
    vt = p.sb("vt", [128, 3, KC])
    bdt = p.sb("bdt", [128, 128])
    p.dma("sp", vt[:], vecs, w=["vt"])
    p.dma("sp", bdt[:], bd, w=["bdt"])
    lneps = p.const("lneps", 64e-5)
    abf = p.sb("abf", [128, KC, TC], BF16)
    L = {n: [p.sb("l_%s%d" % (n, i), [128, TT]) for i in range(2)] for n in ("yT", "rT", "kT", "vT", "gT")}
    t1, t2, t3 = (p.sb("pt%d" % i, [128, TT]) for i in range(3))
    it = 0
    for kc in range(KC):
        for tt in range(NTT):
            b_ = it % 2
            it += 1
            tsl = slice(tt * TT, (tt + 1) * TT)
            for n in L:
                p.dma("sp", L[n][b_][:], ins[n][kc * 128:(kc + 1) * 128, tsl], w=["l_%s%d" % (n, b_)])
            y, r_, k_, v_, g_ = (L[n][b_] for n in ("yT", "rT", "kT", "vT", "gT"))
            yn, rn, kn, vn, gn = ("l_%s%d" % (n, b_) for n in ("yT", "rT", "kT", "vT", "gT"))
            B0, B1, B2 = p.banks[0], p.banks[1], p.banks[2]
            p.add("pe", lambda e, y=y: e.matmul(B0[:, :TT], lhsT=bdt[:], rhs=y[:], start=True, stop=True), r=["bdt", yn], w=["bank0"])
            p.add("dve", lambda e, y=y: e.scalar_tensor_tensor(out=t1[:], in0=B0[:, :TT], scalar=-1.0 / 64, in1=y[:], op0=ALU.mult, op1=ALU.add),
                  r=["bank0", yn], w=["pt0"])
            p.add("act", lambda e: e.activation(out=t2[:], in_=t1[:], func=AF.Square), r=["pt0"], w=["pt1"])
            p.add("pe", lambda e: e.matmul(B1[:, :TT], lhsT=bdt[:], rhs=t2[:], start=True, stop=True), r=["bdt", "pt1"], w=["bank1"])
            p.add("act", lambda e: e.activation(out=t2[:], in_=B1[:, :TT], func=AF.Sqrt, scale=1.0 / 64, bias=lneps[:]), r=["bank1", "lneps"], w=["pt1"])
            p.add("dve", lambda e: e.reciprocal(out=t2[:], in_=t2[:]), r=["pt1"], w=["pt1"])
            p.add("dve", lambda e: e.tensor_tensor(out=t1[:], in0=t1[:], in1=t2[:], op=ALU.mult), r=["pt0", "pt1"], w=["pt0"])
            p.add("dve", lambda e, kc=kc: e.tensor_scalar(out=t1[:], in0=t1[:], scalar1=vt[:, 0, kc:kc + 1], scalar2=vt[:, 1, kc:kc + 1], op0=ALU.mult, op1=ALU.add),
                  r=["pt0", "vt"], w=["pt0"])
            p.add("dve", lambda e, kc=kc, r_=r_, k_=k_: e.scalar_tensor_tensor(out=t3[:], in0=r_[:], scalar=vt[:, 2, kc:kc + 1], in1=k_[:], op0=ALU.mult, op1=ALU.mult),
                  r=[rn, kn, "vt"], w=["pt2"])
            p.add("pe", lambda e: e.matmul(B2[:, :TT], lhsT=bdt[:], rhs=t3[:], start=True, stop=True), r=["bdt", "pt2"], w=["bank2"])
            p.add("dve", lambda e, v_=v_: e.tensor_tensor(out=t3[:], in0=B2[:, :TT], in1=v_[:], op=ALU.mult), r=["bank2", vn, "pt2"], w=["pt2"])
            p.add("pool", lambda e: e.tensor_tensor(out=t1[:], in0=t1[:], in1=t3[:], op=ALU.add), r=["pt0", "pt2"], w=["pt0"])
            p.add("dve", lambda e, kc=kc, tsl=tsl, g_=g_: e.tensor_tensor(out=abf[:, kc, tsl], in0=t1[:], in1=g_[:], op=ALU.mult), r=["pt0", gn], w=["abf"])
    emit_resid_gemm(p, c, w, KC, abf, ins["xT"], out, 1.0, 0, NTT)
    return p.finish()


def build_final_norm(c):
    p = P()
    D, TC, TT, KC = c.D, c.TC, c.TT, c.KC
    xT = p.inp("xT", [D, TC])
    g = p.inp("g", [128, KC])
    out = p.outp("o", [D, TC])
    ones = p.const("ones", 1.0, (128, 128))
    epst = p.const("epst", 1e-6)
    gt = p.sb("gt", [128, KC])
    xs = p.sb("xs", [128, KC, TT])
    ob = p.sb("ob", [128, KC, TT])
    scr = [p.sb("scr%d" % i, [128, TT]) for i in range(2)]
    rstd = p.sb("rstd", [128, TT])
    p.dma("sp", gt[:], g, w=["gt"])
    xv = xT.rearrange("(c p) t -> p c t", p=128)
    ov = out.rearrange("(c p) t -> p c t", p=128)
    for tt in range(c.NTT):
        tsl = slice(tt * TT, (tt + 1) * TT)
        p.dma("sp", xs[:], xv[:, :, tsl], w=["xs"])
        emit_rstd(p, xs, "xs", KC, TT, ones, rstd, "rstd", epst, "epst", scr, D)
        for kc in range(KC):
            p.add("dve", lambda e, kc=kc: e.scalar_tensor_tensor(out=ob[:, kc, :], in0=xs[:, kc, :], scalar=gt[:, kc:kc + 1], in1=rstd[:],
                                                                  op0=ALU.mult, op1=ALU.mult), r=["xs", "gt", "rstd"], w=["ob"])
        p.dma("sp", ov[:, :, tsl], ob[:], r=["ob"])
    return p.finish()


_PROGS = {}


def _prog(name, builder, *a):
    if name not in _PROGS:
        _PROGS[name] = builder(*a)
    return _PROGS[name]


def _run(nc, in_maps):
    n = len(in_maps)
    res = run_bass_kernel_spmd(nc, in_maps, core_ids=list(range(n)))
    return res.results


def _tile_w(w, K, M):
    Kp = (K + 127) // 128 * 128
    Mp = (M + 127) // 128 * 128
    if (Kp, Mp) != (K, M):
        wp = np.zeros((Kp, Mp), np.float32)
        wp[:K, :M] = w
    else:
        wp = np.asarray(w, np.float32)
    return np.ascontiguousarray(wp.reshape(Kp // 128, 128, Mp // 128, 128).transpose(2, 1, 0, 3)).reshape(Mp // 128, 128, (Kp // 128) * 128)


def _pc(v):
    return np.ascontiguousarray(np.asarray(v, np.float32).reshape(-1, 128).T)


def _cat(res, key):
    return np.concatenate([r[key] for r in res], 1)


def _sl(A, i, TC):
    return np.ascontiguousarray(A[:, i * TC:(i + 1) * TC])


def _ffn(c, xT, g, wup, wdn):
    NC_ = c.NCORE
    wt = _tile_w(wup, c.D, 2 * c.F)
    gl = _pc(g)
    r1 = _run(_prog("ffn_up", build_ffn_up, c), [{"xT": xT[i], "g": gl, "w": wt} for i in range(NC_)])
    del wt
    wd = _tile_w(wdn, c.F, c.D)
    r2 = _run(_prog("ffn_down", build_ffn_down, c), [{"act": r1[i]["act"], "xT": xT[i], "w": wd} for i in range(NC_)])
    return [r2[i]["xo"] for i in range(NC_)]


def _fox(c, xT, g, w_in, b_f, qk_gain, w_out):
    import ml_dtypes
    D, S, TC, NC_ = c.D, c.S, c.TC, c.NCORE
    wt = _tile_w(w_in, D, c.FOXIN)
    gl = _pc(g)
    r1 = _run(_prog("fox_in", build_norm_gemm, c, c.FOXIN), [{"xT": xT[i], "g": gl, "w": wt} for i in range(NC_)])
    projT = _cat(r1, "o")
    del r1, wt
    FH = c.FH
    HPC = max(1, FH // NC_)
    ncore = FH // HPC
    kk = np.arange(128)[:, None]
    qq = np.arange(512)[None, :]
    masks = np.stack([np.where(kk + d * 128 <= qq, 0.0, -30000.0) for d in range(4)]).astype(ml_dtypes.bfloat16)
    ustr = (np.arange(128)[:, None] < np.arange(128)[None, :]).astype(np.float32)
    ident = np.eye(128, dtype=np.float32)
    ins = []
    for ci in range(ncore):
        hs = list(range(ci * HPC, (ci + 1) * HPC))
        ins.append({"qT": np.stack([projT[hh * 128:(hh + 1) * 128] for hh in hs]),
                    "kT": np.stack([projT[D + hh * 128: D + (hh + 1) * 128] for hh in hs]),
                    "v": np.stack([np.ascontiguousarray(projT[2 * D + hh * 128: 2 * D + (hh + 1) * 128].reshape(128, S // 128, 128).transpose(2, 1, 0)).reshape(128, S) for hh in hs]),
                    "fl": np.stack([projT[4 * D + hh] for hh in hs]),
                    "negb": np.stack([np.full((128, 1), -np.float32(b_f[hh]), np.float32) for hh in hs]),
                    "qg": np.asarray(qk_gain[0], np.float32).reshape(128, 1).copy(),
                    "kg": np.asarray(qk_gain[1], np.float32).reshape(128, 1).copy(),
                    "masks": masks, "ustr": ustr, "ident": ident})
    r2 = _run(_prog("fox_attn", build_fox_attn, c, HPC), ins)
    oT = np.concatenate([r2[ci]["oT"][j] for ci in range(ncore) for j in range(HPC)], 0)
    gT = projT[3 * D:4 * D]
    wo = _tile_w(w_out, D, D)
    r3 = _run(_prog("fox_out", build_fox_out, c), [{"oT": _sl(oT, i, TC), "gT": _sl(gT, i, TC), "xT": xT[i], "w": wo} for i in range(NC_)])
    return [r3[i]["xo"] for i in range(NC_)]


def _s5(c, xT, g, w_in, lam_re, lam_im, log_step, b_re, b_im, c_re, c_im, d_skip, w_out):
    D, S, TC, NC_ = c.D, c.S, c.TC, c.NCORE
    wt = _tile_w(w_in, D, D)
    gl = _pc(g)
    r1 = _run(_prog("s5_in", build_norm_gemm, c, D), [{"xT": xT[i], "g": gl, "w": wt} for i in range(NC_)])
    uT = _cat(r1, "o")
    del r1
    ncs = D // 256
    lstep = np.repeat(np.asarray(log_step, np.float32)[:, None], 64, 1)
    ins = []
    for ci in range(ncs):
        g0 = ci * 16
        st = lambda a: np.ascontiguousarray(np.asarray(a, np.float32)[g0:g0 + 16].reshape(8, 128).T)
        bre = np.zeros((8, 128, 128), np.float32)
        bim = np.zeros_like(bre)
        cre = np.zeros_like(bre)
        cim = np.zeros_like(bre)
        for j in range(8):
            cc = j // 4
            for gs in range(2):
                gl_ = 2 * j + gs
                gc = gl_ - cc * 8
                gg = g0 + gl_
                bre[j, gc * 16:(gc + 1) * 16, gs * 64:(gs + 1) * 64] = b_re[gg].T
                bim[j, gc * 16:(gc + 1) * 16, gs * 64:(gs + 1) * 64] = b_im[gg].T
                cre[j, gs * 64:(gs + 1) * 64, gc * 16:(gc + 1) * 16] = c_re[gg].T
                cim[j, gs * 64:(gs + 1) * 64, gc * 16:(gc + 1) * 16] = c_im[gg].T
        ins.append({"uT": np.ascontiguousarray(uT[ci * 256:(ci + 1) * 256]), "lre": st(lam_re), "lim": st(lam_im), "lstep": st(lstep),
                    "bre": bre, "bim": bim, "cre": cre, "cim": cim,
                    "dsk": np.ascontiguousarray(np.asarray(d_skip, np.float32)[ci * 256:(ci + 1) * 256].reshape(2, 128).T)})
    r2 = _run(_prog("s5_scan", build_s5_scan, c), ins)
    yT = np.concatenate([r2[ci]["yT"] for ci in range(ncs)], 0)
    wo = _tile_w(w_out, D, 2 * D)
    r3 = _run(_prog("s5_out", build_s5_out, c), [{"yT": _sl(yT, i, TC), "xT": xT[i], "w": wo} for i in range(NC_)])
    return [r3[i]["xo"] for i in range(NC_)]


def _rwkv(c, xT, g, Q):
    D, S, TC, KC, NC_ = c.D, c.S, c.TC, c.KC, c.NCORE
    bd = (np.arange(128)[:, None] // 64 == np.arange(128)[None, :] // 64).astype(np.float32)
    common = {"g": _pc(g), "mu": np.ascontiguousarray(np.asarray(Q['mu'], np.float32).reshape(6, KC, 128).transpose(2, 0, 1)),
              "wr": _tile_w(Q['w_rkv'][0], D, D), "wk": _tile_w(Q['w_rkv'][1], D, D), "wv": _tile_w(Q['w_rkv'][2], D, D),
              "w1": _tile_w(Q['w1'], D, Q['w1'].shape[1]), "a1": _tile_w(Q['a1'], D, Q['a1'].shape[1]), "g1": _tile_w(Q['g1'], D, Q['g1'].shape[1]),
              "w2": _tile_w(Q['w2'], Q['w2'].shape[0], D), "a2": _tile_w(Q['a2'], Q['a2'].shape[0], D), "g2": _tile_w(Q['g2'], Q['g2'].shape[0], D),
              "vecs": np.ascontiguousarray(np.stack([Q['w0'], Q['a0'], Q['k_k'], Q['k_a']]).astype(np.float32).reshape(4, KC, 128).transpose(2, 0, 1)),
              "bd": bd}
    ins = []
    for i in range(NC_):
        xe = np.zeros((D, TC + 1), np.float32)
        xe[:, 1:] = xT[i]
        if i > 0:
            xe[:, 0] = xT[i - 1][:, -1]
        d = dict(common)
        d["xe"] = xe
        ins.append(d)
    r1 = _run(_prog("rwkv_pre", build_rwkv_pre, c), ins)
    X = {n: _cat(r1, n) for n in ("rT", "wT", "kT", "vT", "aT", "bT", "gT")}
    del r1
    H = D // 64
    ncs = H // 4
    sel = np.zeros((2, 4, 128), np.float32)
    for gi in range(2):
        for m in range(128):
            sel[gi, gi * 2 + m // 64, m] = 1
    ins2 = []
    for ci in range(ncs):
        d = {"sel": sel}
        for n, key in (("w", "wT"), ("k", "kT"), ("a", "aT"), ("b", "bT"), ("r", "rT")):
            blk = X[key][ci * 256:(ci + 1) * 256].reshape(4, 64, S)
            d[n + "h"] = np.ascontiguousarray(blk.transpose(0, 2, 1)).reshape(4, S * 64)
        vb = X["vT"][ci * 256:(ci + 1) * 256].reshape(2, 2, 64, S)
        d["vv"] = np.ascontiguousarray(vb.transpose(1, 2, 3, 0)).reshape(128, S, 2)
        ins2.append(d)
    r2 = _run(_prog("rwkv_scan", build_rwkv_scan, c), ins2)
    yT = np.concatenate([np.ascontiguousarray(r2[ci]["yy"].reshape(2, 64, S, 2).transpose(3, 0, 1, 2)).reshape(256, S) for ci in range(ncs)], 0)
    del r2, ins2
    vecs = np.ascontiguousarray(np.stack([Q['ln_w'], Q['ln_b'], np.asarray(Q['r_k']).reshape(-1)]).astype(np.float32).reshape(3, KC, 128).transpose(2, 0, 1))
    wo = _tile_w(Q['w_out'], D, D)
    ins3 = [{"yT": _sl(yT, i, TC), "rT": _sl(X["rT"], i, TC), "kT": _sl(X["kT"], i, TC), "vT": _sl(X["vT"], i, TC), "gT": _sl(X["gT"], i, TC),
             "xT": xT[i], "vecs": vecs, "bd": bd, "w": wo} for i in range(NC_)]
    r3 = _run(_prog("rwkv_post", build_rwkv_post, c), ins3)
    return [r3[i]["xo"] for i in range(NC_)]


def _forward(c, inp, depth):
    x = np.asarray(inp['x'], np.float32)[0]
    TC, NC_ = c.TC, c.NCORE
    xT = [np.ascontiguousarray(x[i * TC:(i + 1) * TC].T) for i in range(NC_)]
    ia = ib = ic = 0
    for l in range(depth):
        xT = _ffn(c, xT, inp['norm_w'][l, 0], inp['ffn_w_up'][l, 0], inp['ffn_w_down'][l, 0])
        g = inp['norm_w'][l, 1]
        m = l % 3
        if m == 0:
            xT = _fox(c, xT, g, inp['fox_w_in'][ia], inp['fox_b_f'][ia], inp['fox_qk_gain'][ia], inp['fox_w_out'][ia])
            ia += 1
        elif m == 1:
            Q = {k[5:]: np.asarray(inp[k][ib]) for k in inp if k.startswith('rwkv_')}
            xT = _rwkv(c, xT, g, Q)
            ib += 1
        else:
            xT = _s5(c, xT, g, inp['s5_w_in'][ic], inp['s5_lam_re'][ic], inp['s5_lam_im'][ic], inp['s5_log_step'][ic],
                     inp['s5_b_re'][ic], inp['s5_b_im'][ic], inp['s5_c_re'][ic], inp['s5_c_im'][ic], inp['s5_d'][ic], inp['s5_w_out'][ic])
            ic += 1
        xT = _ffn(c, xT, inp['norm_w'][l, 2], inp['ffn_w_up'][l, 1], inp['ffn_w_down'][l, 1])
    rf = _run(_prog("final_norm", build_final_norm, c), [{"xT": xT[i], "g": _pc(inp['final_norm'])} for i in range(NC_)])
    out = np.concatenate([rf[i]["o"].T for i in range(NC_)], 0)
    return np.ascontiguousarray(out[None]).astype(np.float32)


def kernel(**inputs):
    inp = {k: np.asarray(v) for k, v in inputs.items()}
    D = inp['x'].shape[-1]
    S = inp['x'].shape[1]
    c = CFG(D=D, S=S)
    return _forward(c, inp, inp['norm_w'].shape[0])
```

```python
import numpy as np
import concourse.bass as bass
import concourse.mybir as mybir
from concourse.bass_utils import run_bass_kernel_spmd

F32 = mybir.dt.float32
BF16 = mybir.dt.bfloat16
AF = mybir.ActivationFunctionType
ALU = mybir.AluOpType
AX = mybir.AxisListType

ENGS = ("pe", "act", "dve", "pool", "sp")
EIDX = {e: i for i, e in enumerate(ENGS)}


class Op:
    __slots__ = ("eng", "fn", "idx", "dma", "waits", "sig", "sem", "target", "know", "dmaknow", "deps")

    def __init__(self, eng, fn, dma):
        self.eng = eng
        self.fn = fn
        self.dma = dma
        self.idx = -1
        self.waits = []
        self.sig = False
        self.sem = None
        self.target = 0
        self.know = None


class Sched:
    NDMA = 12

    def __init__(self, nc):
        self.nc = nc
        self.eng_ops = {e: [] for e in ENGS}
        self.last_w = {}
        self.readers = {}
        self.know = {e: [-1] * len(ENGS) for e in ENGS}
        self.dmaknow = {e: set() for e in ENGS}
        self.dma_slots = {e: [None] * self.NDMA for e in ENGS}
        self.dma_cnt = {e: 0 for e in ENGS}
        self.dma_slot_uses = {e: [0] * self.NDMA for e in ENGS}
        self.nops = 0

    def add(self, eng, fn, r=(), w=(), dma=False):
        op = Op(eng, fn, dma)
        ops = self.eng_ops[eng]
        op.idx = len(ops)
        deps = []
        for b in r:
            x = self.last_w.get(b)
            if x is not None:
                deps.append(x)
        for b in w:
            x = self.last_w.get(b)
            if x is not None:
                deps.append(x)
            deps.extend(self.readers.get(b, ()))
        if dma:
            k = self.dma_cnt[eng]
            slot = k % self.NDMA
            prev = self.dma_slots[eng][slot]
            if prev is not None:
                deps.append(prev)
            self.dma_slots[eng][slot] = op
            self.dma_cnt[eng] = k + 1
            self.dma_slot_uses[eng][slot] += 1
            op.sem = (eng, slot)
            op.target = 16 * self.dma_slot_uses[eng][slot]
            op.sig = True
        know = self.know[eng]
        dk = self.dmaknow[eng]
        best = {}
        for d in deps:
            if d is op:
                continue
            if d.dma:
                if id(d) in dk:
                    continue
                best[("dma", id(d))] = d
            else:
                if d.eng == "pe" and eng == "pe" and not dma:
                    continue
                if know[EIDX[d.eng]] >= d.idx:
                    continue
                cur = best.get(d.eng)
                if cur is None or cur.idx < d.idx:
                    best[d.eng] = d
        for key, d in best.items():
            if d.dma:
                op.waits.append(d)
                dk.add(id(d))
            else:
                if know[EIDX[d.eng]] >= d.idx:
                    continue
                op.waits.append(d)
                d.sig = True
                dkv = d.know
                for i in range(len(ENGS)):
                    if dkv[i] > know[i]:
                        know[i] = dkv[i]
                if know[EIDX[d.eng]] < d.idx:
                    know[EIDX[d.eng]] = d.idx
        op.know = list(know)
        for b in r:
            self.readers.setdefault(b, []).append(op)
        for b in w:
            self.last_w[b] = op
            self.readers[b] = []
        ops.append(op)
        self.nops += 1
        if len(dk) > 4096:
            dk.clear()
        return op

    def emit(self, final_wait_ops=()):
        nc = self.nc
        from contextlib import ExitStack
        with ExitStack() as es:
            esem = {e: es.enter_context(nc.semaphore("s_" + e)) for e in ENGS}
            dsem = {}
            for e in ENGS:
                if self.dma_cnt[e] > 0:
                    for s in range(min(self.NDMA, self.dma_cnt[e])):
                        dsem[(e, s)] = es.enter_context(nc.semaphore("d_%s_%d" % (e, s)))
            for e in ENGS:
                c = 0
                for op in self.eng_ops[e]:
                    if op.dma:
                        continue
                    if op.sig:
                        c += 1
                        op.target = c
                        op.sem = e
            block = es.enter_context(nc.Block())

            def run(e, eng):
                for op in self.eng_ops[e]:
                    for d in op.waits:
                        if d.dma:
                            eng.wait_ge(dsem[d.sem], d.target)
                        else:
                            eng.wait_ge(esem[d.sem], d.target)
                    ins = op.fn(eng)
                    if op.dma:
                        ins.then_inc(dsem[op.sem], 16)
                    elif op.sig:
                        ins.then_inc(esem[e], 1)
                if e == "sp":
                    fw = list(final_wait_ops)
                    for q in ENGS:
                        for d in self.dma_slots[q]:
                            if d is not None:
                                fw.append(d)
                    for d in fw:
                        if d.dma:
                            eng.wait_ge(dsem[d.sem], d.target)
                        else:
                            eng.wait_ge(esem[d.sem], d.target)

            @block.tensor
            def _(eng):
                run("pe", eng)

            @block.scalar
            def _(eng):
                run("act", eng)

            @block.vector
            def _(eng):
                run("dve", eng)

            @block.gpsimd
            def _(eng):
                run("pool", eng)

            @block.sync
            def _(eng):
                run("sp", eng)


from contextlib import ExitStack
import math


class CFG:
    def __init__(self, D=2048, S=16384, NCORE=8):
        self.D = D
        self.S = S
        self.NCORE = NCORE
        self.F = ((int(2 * 4 * D / 3) + 255) // 256) * 256
        self.TC = S // NCORE
        self.TT = min(512, self.TC)
        self.NTT = self.TC // self.TT
        self.KC = D // 128
        self.FH = D // 128
        self.FOXIN = 4 * D + self.FH


def _alloc(nc, es, name, shape, dt):
    return es.enter_context(nc.sbuf_tensor(name, shape, dt))


class P:
    def __init__(self):
        self.nc = bass.Bass("TRN2", target_bir_lowering=False)
        self.es = ExitStack()
        self.S = Sched(self.nc)
        self.banks = [self.es.enter_context(self.nc.psum_tensor("bank%d" % i, [128, 512], F32)) for i in range(8)]
        self.nbank = 0
        self.cnt = 0

    def inp(self, name, shape, dt=F32):
        return self.nc.dram_tensor(name, list(shape), dt, kind="ExternalInput").ap()

    def outp(self, name, shape, dt=F32):
        return self.nc.dram_tensor(name, list(shape), dt, kind="ExternalOutput").ap()

    def sb(self, name, shape, dt=F32):
        return _alloc(self.nc, self.es, name, list(shape), dt)

    def add(self, *a, **k):
        return self.S.add(*a, **k)

    def dma(self, q, out, in_, r=(), w=()):
        return self.S.add(q, lambda e: e.dma_start(out=out, in_=in_), r=r, w=w, dma=True)

    def finish(self):
        self.S.emit()
        self.es.close()
        return self.nc

    def const(self, name, val, shape=(128, 1)):
        t = self.sb(name, shape)
        self.add("pool", lambda e: e.memset(t[:], val), w=[name])
        return t


def emit_rstd(p, xs, xs_name, KC, width, ones, rstd, rstd_name, epst, eps_name, scr, D_total):
    ps = p.banks[0]
    for kc in range(KC):
        sqb = scr[kc % 2]
        sn = "scr%d" % (kc % 2)
        p.add("act", lambda e, kc=kc, sqb=sqb: e.activation(out=sqb[:, :width], in_=xs[:, kc, :width], func=AF.Square),
              r=[xs_name], w=[sn])
        p.add("pe", lambda e, kc=kc, sqb=sqb: e.matmul(ps[:, :width], lhsT=ones[:], rhs=sqb[:, :width], start=(kc == 0), stop=(kc == KC - 1)),
              r=["ones", sn], w=["bank0"])
    p.add("act", lambda e: e.activation(out=scr[0][:, :width], in_=ps[:, :width], func=AF.Sqrt, scale=1.0 / D_total, bias=epst[:]),
          r=["bank0", eps_name], w=["scr0"])
    p.add("dve", lambda e: e.reciprocal(out=rstd[:, :width], in_=scr[0][:, :width]), r=["scr0"], w=[rstd_name])


def emit_norm_bf(p, c, xT, g, abf, eps=1e-6):
    KC, TT = c.KC, c.TT
    ones = p.const("ones", 1.0, (128, 128))
    epst = p.const("epst", eps)
    gt = p.sb("gt", [128, KC])
    xs = p.sb("xs", [128, KC, TT])
    scr = [p.sb("scr%d" % i, [128, TT]) for i in range(2)]
    rstd = p.sb("rstd", [128, TT])
    p.dma("sp", gt[:], g, w=["gt"])
    xv = xT.rearrange("(c p) t -> p c t", p=128)
    for tt in range(c.NTT):
        tsl = slice(tt * TT, (tt + 1) * TT)
        p.dma("sp", xs[:], xv[:, :, tsl], w=["xs"])
        emit_rstd(p, xs, "xs", KC, TT, ones, rstd, "rstd", epst, "epst", scr, c.D)
        for kc in range(KC):
            p.add("dve", lambda e, kc=kc, tsl=tsl: e.scalar_tensor_tensor(
                out=abf[:, kc, tsl], in0=xs[:, kc, :], scalar=gt[:, kc:kc + 1], in1=rstd[:],
                op0=ALU.mult, op1=ALU.mult), r=["xs", "gt", "rstd"], w=["abf"])


def emit_gemm(p, c, w, wname, KC, abf, abf_name, MC, epi, t0=0, ntt=None, nbanks=8, krows=None, mrows=None, NW=3, after=None):
    TT = c.TT
    ntt = c.NTT if ntt is None else ntt
    wb = [p.sb("%s_b%d" % (wname, i), [128, KC * 128], BF16) for i in range(NW)]
    for j in range(MC):
        wbj = wb[j % NW]
        wn = "%s_b%d" % (wname, j % NW)
        p.dma("pool", wbj[:], w[j], w=[wn])
        ms = 128 if (mrows is None or j < MC - 1) else mrows
        for kc in range(KC):
            ks = 128 if (krows is None or kc < KC - 1) else krows
            for tt in range(ntt):
                bk = p.nbank_of(j, tt, ntt, nbanks)
                p.add("pe", lambda e, kc=kc, tt=tt, bk=bk, wbj=wbj, ks=ks, ms=ms: e.matmul(
                    p.banks[bk][:ms, :TT], lhsT=wbj[:ks, kc * 128:kc * 128 + ms],
                    rhs=abf[:ks, kc, t0 + tt * TT:t0 + (tt + 1) * TT], start=(kc == 0), stop=(kc == KC - 1)),
                    r=[wn, abf_name], w=["bank%d" % bk])
        for tt in range(ntt):
            bk = p.nbank_of(j, tt, ntt, nbanks)
            epi(j, tt, p.banks[bk], "bank%d" % bk, ms)


def _nbank_of(self, j, tt, ntt, nbanks):
    return (j * ntt + tt) % nbanks


P.nbank_of = _nbank_of


def build_ffn_up(c):
    p = P()
    D, F, TC, TT, KC = c.D, c.F, c.TC, c.TT, c.KC
    xT = p.inp("xT", [D, TC])
    g = p.inp("g", [128, KC])
    w = p.inp("w", [2 * F // 128, 128, KC * 128])
    out = p.outp("act", [F, TC], BF16)
    abf = p.sb("abf", [128, KC, TC], BF16)
    emit_norm_bf(p, c, xT, g, abf)
    NW = 3
    wb = [p.sb("wb%d" % i, [128, 2, KC * 128], BF16) for i in range(NW)]
    sg = [p.sb("sg%d" % i, [128, TT]) for i in range(2)]
    ob = [p.sb("ob%d" % i, [128, TC], BF16) for i in range(2)]
    FC = F // 128
    NTT = c.NTT
    for j in range(FC):
        wbj = wb[j % NW]
        wn = "wb%d" % (j % NW)
        p.dma("pool", wbj[:, 0, :], w[j], w=[wn + "g"])
        p.dma("pool", wbj[:, 1, :], w[FC + j], w=[wn + "u"])
        for half in range(2):
            for kc in range(KC):
                for tt in range(NTT):
                    bk = half * 4 + tt
                    p.add("pe", lambda e, half=half, kc=kc, tt=tt, bk=bk, wbj=wbj: e.matmul(
                        p.banks[bk][:, :TT], lhsT=wbj[:, half, kc * 128:(kc + 1) * 128],
                        rhs=abf[:, kc, tt * TT:(tt + 1) * TT], start=(kc == 0), stop=(kc == KC - 1)),
                        r=[wn + "gu"[half], "abf"], w=["bank%d" % bk])
        obj = ob[j % 2]
        on = "ob%d" % (j % 2)
        for tt in range(NTT):
            sgb = sg[tt % 2]
            sn = "sg%d" % (tt % 2)
            p.add("act", lambda e, tt=tt, sgb=sgb: e.activation(out=sgb[:], in_=p.banks[tt][:, :TT], func=AF.Silu),
                  r=["bank%d" % tt], w=[sn])
            p.add("dve", lambda e, tt=tt, sgb=sgb, obj=obj: e.tensor_tensor(
                out=obj[:, tt * TT:(tt + 1) * TT], in0=sgb[:], in1=p.banks[4 + tt][:, :TT], op=ALU.mult),
                r=[sn, "bank%d" % (4 + tt)], w=[on])
        p.dma("sp", out[j * 128:(j + 1) * 128, :], obj[:], r=[on])
    return p.finish()


def emit_resid_gemm(p, c, w, KCI, abf, xT, out, scale, t0, ntt):
    TT = c.TT
    xb = [p.sb("xb%d" % i, [128, ntt * TT]) for i in range(2)]
    ob = [p.sb("rob%d" % i, [128, ntt * TT]) for i in range(2)]

    def epi(j, tt, bank, bname, ms):
        if tt == 0:
            p.dma("sp", xb[j % 2][:], xT[j * 128:(j + 1) * 128, t0:t0 + ntt * TT], w=["xb%d" % (j % 2)])
        p.add("dve", lambda e: e.scalar_tensor_tensor(
            out=ob[j % 2][:, tt * TT:(tt + 1) * TT], in0=bank[:, :TT], scalar=scale, in1=xb[j % 2][:, tt * TT:(tt + 1) * TT],
            op0=ALU.mult, op1=ALU.add), r=[bname, "xb%d" % (j % 2)], w=["rob%d" % (j % 2)])
        if tt == ntt - 1:
            p.dma("sp", out[j * 128:(j + 1) * 128, t0:t0 + ntt * TT], ob[j % 2][:], r=["rob%d" % (j % 2)])

    emit_gemm(p, c, w, "wd", KCI, abf, "abf", c.KC, epi, t0=0, ntt=ntt, nbanks=8)


def build_ffn_down(c):
    p = P()
    D, F, TC, TT = c.D, c.F, c.TC, c.TT
    FC = F // 128
    act = p.inp("act", [F, TC], BF16)
    xT = p.inp("xT", [D, TC])
    w = p.inp("w", [c.KC, 128, FC * 128])
    out = p.outp("xo", [D, TC])
    ngrp = 2 if c.NTT >= 2 else 1
    ntt = c.NTT // ngrp
    abf = p.sb("abf", [128, FC, ntt * TT], BF16)
    av = act.rearrange("(c p) t -> p c t", p=128)
    TTg = ntt * TT
    xb = [p.sb("xb%d" % i, [128, TTg]) for i in range(2)]
    ob = [p.sb("rob%d" % i, [128, TTg]) for i in range(2)]
    NW = 3
    wb = [p.sb("wd_b%d" % i, [128, FC * 128], BF16) for i in range(NW)]
    it = 0
    for gi in range(ngrp):
        t0 = gi * TTg
        p.dma("sp", abf[:], av[:, :, t0:t0 + TTg], w=["abf"])
        for j in range(c.KC):
            wbj = wb[it % NW]
            wn = "wd_b%d" % (it % NW)
            it += 1
            p.dma("pool", wbj[:], w[j], w=[wn])
            for kc in range(FC):
                for tt in range(ntt):
                    bk = (j * ntt + tt) % 8
                    p.add("pe", lambda e, kc=kc, tt=tt, bk=bk, wbj=wbj: e.matmul(
                        p.banks[bk][:, :TT], lhsT=wbj[:, kc * 128:(kc + 1) * 128],
                        rhs=abf[:, kc, tt * TT:(tt + 1) * TT], start=(kc == 0), stop=(kc == FC - 1)),
                        r=[wn, "abf"], w=["bank%d" % bk])
            xbj, obj = xb[j % 2], ob[j % 2]
            p.dma("sp", xbj[:], xT[j * 128:(j + 1) * 128, t0:t0 + TTg], w=["xb%d" % (j % 2)])
            for tt in range(ntt):
                bk = (j * ntt + tt) % 8
                p.add("dve", lambda e, tt=tt, bk=bk, xbj=xbj, obj=obj: e.scalar_tensor_tensor(
                    out=obj[:, tt * TT:(tt + 1) * TT], in0=p.banks[bk][:, :TT], scalar=0.5, in1=xbj[:, tt * TT:(tt + 1) * TT],
                    op0=ALU.mult, op1=ALU.add), r=["bank%d" % bk, "xb%d" % (j % 2)], w=["rob%d" % (j % 2)])
            p.dma("sp", out[j * 128:(j + 1) * 128, t0:t0 + TTg], obj[:], r=["rob%d" % (j % 2)])
    return p.finish()


def build_norm_gemm(c, M):
    p = P()
    D, TC, TT, KC = c.D, c.TC, c.TT, c.KC
    MC = (M + 127) // 128
    xT = p.inp("xT", [D, TC])
    g = p.inp("g", [128, KC])
    w = p.inp("w", [MC, 128, KC * 128])
    out = p.outp("o", [MC * 128, TC])
    abf = p.sb("abf", [128, KC, TC], BF16)
    emit_norm_bf(p, c, xT, g, abf)
    ob = [p.sb("sob%d" % i, [128, TC]) for i in range(2)]

    def epi(j, tt, bank, bname, ms):
        p.add("act", lambda e: e.copy(out=ob[j % 2][:, tt * TT:(tt + 1) * TT], in_=bank[:, :TT]),
              r=[bname], w=["sob%d" % (j % 2)])
        if tt == c.NTT - 1:
            p.dma("sp", out[j * 128:(j + 1) * 128, :], ob[j % 2][:], r=["sob%d" % (j % 2)])

    emit_gemm(p, c, w, "wg", KC, abf, "abf", MC, epi)
    return p.finish()


def build_fox_out(c):
    p = P()
    D, TC, TT, KC = c.D, c.TC, c.TT, c.KC
    oT = p.inp("oT", [D, TC])
    gT = p.inp("gT", [D, TC])
    xT = p.inp("xT", [D, TC])
    w = p.inp("w", [KC, 128, KC * 128])
    out = p.outp("xo", [D, TC])
    abf = p.sb("abf", [128, KC, TC], BF16)
    ot = [p.sb("ot%d" % i, [128, TC]) for i in range(2)]
    gt = [p.sb("gtt%d" % i, [128, TC]) for i in range(2)]
    for kc in range(KC):
        o_, g_ = ot[kc % 2], gt[kc % 2]
        p.dma("sp", o_[:], oT[kc * 128:(kc + 1) * 128, :], w=["ot%d" % (kc % 2)])
        p.dma("sp", g_[:], gT[kc * 128:(kc + 1) * 128, :], w=["gtt%d" % (kc % 2)])
        p.add("act", lambda e, g_=g_: e.activation(out=g_[:], in_=g_[:], func=AF.Sigmoid),
              r=["gtt%d" % (kc % 2)], w=["gtt%d" % (kc % 2)])
        p.add("dve", lambda e, kc=kc, o_=o_, g_=g_: e.tensor_tensor(out=abf[:, kc, :], in0=o_[:], in1=g_[:], op=ALU.mult),
              r=["ot%d" % (kc % 2), "gtt%d" % (kc % 2)], w=["abf"])
    emit_resid_gemm(p, c, w, KC, abf, xT, out, 1.0, 0, c.NTT)
    return p.finish()


def build_fox_attn(c, HPC):
    p = P()
    S = c.S
    NB = S // 128
    QW = min(512, S)
    NQ = S // QW
    scale = 128 ** -0.5
    qT = p.inp("qT", [HPC, 128, S])
    kT = p.inp("kT", [HPC, 128, S])
    v = p.inp("v", [HPC, 128, (S // 128) * 128])
    fl = p.inp("fl", [HPC, S])
    negb = p.inp("negb", [HPC, 128, 1])
    qg = p.inp("qg", [128, 1])
    kg = p.inp("kg", [128, 1])
    masks = p.inp("masks", [4, 128, 512], BF16)
    ustr = p.inp("ustr", [128, 128])
    ident = p.inp("ident", [128, 128])
    oT = p.outp("oT", [HPC, 128, S])
    scr_d = p.nc.dram_tensor("cscr", [HPC, 3, S], BF16, kind="Internal").ap()

    ones = p.const("ones", 1.0, (128, 128))
    onesb = p.sb("onesb", [128, 128], BF16)
    p.add("dve", lambda e: e.tensor_copy(out=onesb[:], in_=ones[:]), r=["ones"], w=["onesb"])
    epst = p.const("epst", 1e-6)
    onec = p.const("onec", 1.0)
    qgs = p.sb("qgs", [128, 1])
    kgs = p.sb("kgs", [128, 1])
    p.dma("sp", qgs[:], qg, w=["qgs"])
    p.dma("sp", kgs[:], kg, w=["kgs"])
    p.add("dve", lambda e: e.tensor_scalar(out=qgs[:], in0=qgs[:], scalar1=scale, scalar2=None, op0=ALU.mult), r=["qgs"], w=["qgs"])
    us = p.sb("us", [128, 128])
    idt = p.sb("idt", [128, 128])
    idb = p.sb("idb", [128, 128], BF16)
    p.dma("sp", us[:], ustr, w=["us"])
    p.dma("sp", idt[:], ident, w=["idt"])
    p.add("dve", lambda e: e.tensor_copy(out=idb[:], in_=idt[:]), r=["idt"], w=["idb"])
    mk = p.sb("mk", [128, 4, 512], BF16)
    p.dma("sp", mk[:], masks.rearrange("d p q -> p d q"), w=["mk"])
    sel = p.sb("sel", [96, 128], BF16)
    p.add("pool", lambda e: e.memset(sel[:], 0.0), w=["sel"])
    for r_ in (0, 32, 64):
        p.add("pool", lambda e, r_=r_: e.memset(sel[r_:r_ + 1, :], -1.0), r=[], w=["sel"])

    qbf = p.sb("qbf", [128, S], BF16)
    kbf = p.sb("kbf", [128, S], BF16)
    vbf = p.sb("vbf", [128, NB, 128], BF16)
    xs = p.sb("xs", [128, 1, QW])
    scr = [p.sb("scr%d" % i, [128, QW]) for i in range(2)]
    rstd = p.sb("rstd", [128, QW])
    flt = p.sb("flt", [128, 128])
    l1 = p.sb("l1", [128, 128])
    cw = p.sb("cw", [128, 128])
    nbt = p.sb("nbt", [128, 1])
    off = p.sb("off", [128, 1])
    cposT = p.sb("cposT", [128, NB])
    crow = p.sb("crow", [96, S], BF16)
    chs = [p.sb("chs%d" % i, [128, 128], BF16) for i in range(3)]
    rr = p.sb("rr", [128, 128])
    pt = [p.sb("pt%d" % i, [128, QW], BF16) for i in range(4)]
    rl = p.sb("rl", [128, QW])
    osb = [p.sb("osb%d" % i, [128, QW]) for i in range(2)]
    p.add("pool", lambda e: e.memset(crow[:], 0.0), w=["crow"])
    LB = [2, 3, 4, 5]
    it = 0
    for h in range(HPC):
        p.dma("sp", flt[:NB, :], fl[h].rearrange("(b w) -> b w", w=128), w=["flt"])
        p.dma("sp", nbt[:], negb[h], w=["nbt"])
        p.add("act", lambda e: e.activation(out=l1[:NB, :], in_=flt[:NB, :], func=AF.Exp, scale=-1.0, bias=nbt[:NB, :]),
              r=["flt", "nbt"], w=["l1"])
        p.add("act", lambda e: e.activation(out=l1[:NB, :], in_=l1[:NB, :], func=AF.Ln, bias=onec[:NB, :]),
              r=["l1", "onec"], w=["l1"])
        p.add("dve", lambda e: e.tensor_tensor_scan(out=cw[:NB, :], data0=ones[:NB, :], data1=l1[:NB, :], initial=0.0,
                                                     op0=ALU.mult, op1=ALU.add), r=["ones", "l1"], w=["cw"])
        p.add("pe", lambda e: e.matmul(p.banks[1][:NB, 0:1], lhsT=us[:NB, :NB], rhs=cw[:NB, 127:128], start=True, stop=True),
              r=["us", "cw"], w=["bank1"])
        p.add("act", lambda e: e.copy(out=off[:NB, :], in_=p.banks[1][:NB, 0:1]), r=["bank1"], w=["off"])
        p.add("dve", lambda e: e.tensor_scalar(out=cw[:NB, :], in0=cw[:NB, :], scalar1=off[:NB, :], scalar2=None, op0=ALU.add),
              r=["cw", "off"], w=["cw"])
        p.add("pe", lambda e: e.transpose(p.banks[1][:, :NB], cw[:NB, :], idt[:NB, :NB]), r=["cw", "idt"], w=["bank1"])
        p.add("act", lambda e: e.copy(out=cposT[:], in_=p.banks[1][:, :NB]), r=["bank1"], w=["cposT"])
        p.add("dve", lambda e: e.tensor_copy(out=chs[0][:NB, :], in_=cw[:NB, :]), r=["cw"], w=["chs0"])
        p.add("dve", lambda e: e.tensor_tensor(out=rr[:NB, :], in0=cw[:NB, :], in1=chs[0][:NB, :], op=ALU.subtract), r=["cw", "chs0"], w=["rr"])
        p.add("dve", lambda e: e.tensor_copy(out=chs[1][:NB, :], in_=rr[:NB, :]), r=["rr"], w=["chs1"])
        p.add("dve", lambda e: e.tensor_tensor(out=rr[:NB, :], in0=rr[:NB, :], in1=chs[1][:NB, :], op=ALU.subtract), r=["rr", "chs1"], w=["rr"])
        p.add("dve", lambda e: e.tensor_copy(out=chs[2][:NB, :], in_=rr[:NB, :]), r=["rr"], w=["chs2"])
        for k3 in range(3):
            p.dma("sp", scr_d[h, k3].rearrange("(b w) -> b w", w=128), chs[k3][:NB, :], r=["chs%d" % k3], w=["cscr%d" % k3])
            p.dma("sp", crow[32 * k3:32 * k3 + 1, :], scr_d[h, k3:k3 + 1, :], r=["cscr%d" % k3], w=["crow"])
        for (src, dst, dn, gsb, gn) in ((qT, qbf, "qbf", qgs, "qgs"), (kT, kbf, "kbf", kgs, "kgs")):
            for qb in range(NQ):
                qs = slice(qb * QW, (qb + 1) * QW)
                p.dma("sp", xs[:, 0, :], src[h][:, qs], w=["xs"])
                emit_rstd(p, xs, "xs", 1, QW, ones, rstd, "rstd", epst, "epst", scr, 128)
                p.add("dve", lambda e, qs=qs, dst=dst, gsb=gsb: e.scalar_tensor_tensor(
                    out=dst[:, qs], in0=xs[:, 0, :], scalar=gsb[:, 0:1], in1=rstd[:], op0=ALU.mult, op1=ALU.mult),
                    r=["xs", gn, "rstd"], w=[dn])
        p.dma("pool", vbf[:], v[h].rearrange("w (b d) -> w b d", d=128), w=["vbf"])
        tiles = []
        for qb in range(NQ):
            q0 = qb * QW
            nkb = (q0 + QW) // 128
            for kb in range(nkb):
                tiles.append((qb, kb, nkb))
        LOOK = 2
        NPT = len(pt)

        def emit_logits(ti):
            qb, kb, nkb = tiles[ti]
            q0 = qb * QW
            qs = slice(q0, q0 + QW)
            lb = LB[ti % 4]
            ptb = pt[ti % NPT]
            pn = "pt%d" % (ti % NPT)
            d = kb * 128 - q0
            diag = d >= 0
            p.add("pe", lambda e: e.matmul(p.banks[lb][:, :QW], lhsT=kbf[:, kb * 128:(kb + 1) * 128], rhs=qbf[:, qs], start=True, stop=False),
                  r=["kbf", "qbf"], w=["bank%d" % lb])
            p.add("pe", lambda e: e.matmul(p.banks[lb][:, :QW], lhsT=sel[:, :], rhs=crow[:, qs], start=False, stop=(not diag)),
                  r=["sel", "crow"], w=["bank%d" % lb])
            if diag:
                p.add("pe", lambda e: e.matmul(p.banks[lb][:, :QW], lhsT=idb[:], rhs=mk[:, d // 128, :QW], start=False, stop=True),
                      r=["idb", "mk"], w=["bank%d" % lb])
            p.add("act", lambda e: e.activation(out=ptb[:], in_=p.banks[lb][:, :QW], func=AF.Exp, bias=cposT[:, kb:kb + 1]),
                  r=["bank%d" % lb, "cposT"], w=[pn])

        def emit_pv(ti):
            qb, kb, nkb = tiles[ti]
            q0 = qb * QW
            qs = slice(q0, q0 + QW)
            ptb = pt[ti % NPT]
            pn = "pt%d" % (ti % NPT)
            ob_, lb_ = (6, 7) if (qb % 2 == 0) else (0, 1)
            p.add("pe", lambda e: e.matmul(p.banks[ob_][:, :QW], lhsT=vbf[:, kb, :], rhs=ptb[:], start=(kb == 0), stop=(kb == nkb - 1)),
                  r=["vbf", pn], w=["bank%d" % ob_])
            p.add("pe", lambda e: e.matmul(p.banks[lb_][:, :QW], lhsT=onesb[:], rhs=ptb[:], start=(kb == 0), stop=(kb == nkb - 1)),
                  r=["onesb", pn], w=["bank%d" % lb_])
            if kb == nkb - 1:
                osb_ = osb[qb % 2]
                p.add("dve", lambda e: e.reciprocal(out=rl[:], in_=p.banks[lb_][:, :QW]), r=["bank%d" % lb_], w=["rl"])
                p.add("dve", lambda e: e.tensor_tensor(out=osb_[:], in0=p.banks[ob_][:, :QW], in1=rl[:], op=ALU.mult),
                      r=["bank%d" % ob_, "rl"], w=["osb%d" % (qb % 2)])
                p.dma("sp", oT[h][:, qs], osb_[:], r=["osb%d" % (qb % 2)])

        for ti in range(len(tiles) + LOOK):
            if ti < len(tiles):
                emit_logits(ti)
            if ti - LOOK >= 0:
                emit_pv(ti - LOOK)
    return p.finish()


def build_s5_scan(c):
    p = P()
    S = c.S
    LS = min(512, S)
    NSEG = S // LS
    NT = 8
    LV = int(math.log2(LS))
    uT = p.inp("uT", [256, S])
    lre = p.inp("lre", [128, NT])
    lim = p.inp("lim", [128, NT])
    lstep = p.inp("lstep", [128, NT])
    bre = p.inp("bre", [NT, 128, 128])
    bim = p.inp("bim", [NT, 128, 128])
    cre = p.inp("cre", [NT, 128, 128])
    cim = p.inp("cim", [NT, 128, 128])
    dsk = p.inp("dsk", [128, 2])
    yT = p.outp("yT", [256, S])

    def small(name):
        return p.sb(name, [128, NT])

    def tt(out, a, b, op, eng="dve", r=(), w=()):
        p.add(eng, lambda e: e.tensor_tensor(out=out, in0=a, in1=b, op=op), r=r, w=w)

    lr, li, dt_, rho, th = small("lr"), small("li"), small("dt"), small("rho"), small("th")
    p.dma("sp", lr[:], lre, w=["lr"])
    p.dma("sp", li[:], lim, w=["li"])
    p.dma("sp", dt_[:], lstep, w=["dt"])
    dskt = p.sb("dskt", [128, 2])
    p.dma("sp", dskt[:], dsk, w=["dskt"])
    halfpi = p.const("halfpi", math.pi / 2)
    p.add("dve", lambda e: e.tensor_scalar(out=lr[:], in0=lr[:], scalar1=-1e-4, scalar2=None, op0=ALU.min), r=["lr"], w=["lr"])
    p.add("act", lambda e: e.activation(out=dt_[:], in_=dt_[:], func=AF.Exp), r=["dt"], w=["dt"])
    tt(rho[:], lr[:], dt_[:], ALU.mult, r=["lr", "dt"], w=["rho"])
    p.add("act", lambda e: e.activation(out=rho[:], in_=rho[:], func=AF.Exp), r=["rho"], w=["rho"])
    tt(th[:], li[:], dt_[:], ALU.mult, r=["li", "dt"], w=["th"])
    cr = [small("cr%d" % k) for k in range(LV + 1)]
    ci = [small("ci%d" % k) for k in range(LV + 1)]
    a_, b_ = small("tmpa"), small("tmpb")
    p.add("act", lambda e: e.activation(out=b_[:], in_=th[:], func=AF.Sin, scale=1.0 / 16), r=["th"], w=["tmpb"])
    p.add("act", lambda e: e.activation(out=a_[:], in_=th[:], func=AF.Sin, scale=-1.0 / 16, bias=halfpi[:]), r=["th", "halfpi"], w=["tmpa"])

    def csq(or_, oi_, ir_, ii_, names):
        t1, t2 = small("sq1_" + names), small("sq2_" + names)
        tt(t1[:], ir_[:], ir_[:], ALU.mult, r=[names + "ir"], w=[names + "t1"])
        tt(t2[:], ii_[:], ii_[:], ALU.mult, r=[names + "ii"], w=[names + "t2"])
        p.add("dve", lambda e: e.scalar_tensor_tensor(out=oi_[:], in0=ir_[:], scalar=2.0, in1=ii_[:], op0=ALU.mult, op1=ALU.mult),
              r=[names + "ir", names + "ii"], w=[names + "oi"])
        tt(or_[:], t1[:], t2[:], ALU.subtract, r=[names + "t1", names + "t2"], w=[names + "or"])

    chain = [(a_, b_)] + [(small("qa%d" % i), small("qb%d" % i)) for i in range(3)] + [(cr[0], ci[0])]
    for i in range(4):
        ir_, ii_ = chain[i]
        or_, oi_ = chain[i + 1]
        t1, t2 = small("s1_%d" % i), small("s2_%d" % i)
        tt(t1[:], ir_[:], ir_[:], ALU.mult, r=["tmpa", "tmpb", "prel"], w=["prel"])
        tt(t2[:], ii_[:], ii_[:], ALU.mult, r=["prel"], w=["prel"])
        p.add("dve", lambda e, oi_=oi_, ir_=ir_, ii_=ii_: e.scalar_tensor_tensor(out=oi_[:], in0=ir_[:], scalar=2.0, in1=ii_[:], op0=ALU.mult, op1=ALU.mult),
              r=["prel"], w=["prel"])
        tt(or_[:], t1[:], t2[:], ALU.subtract, r=["prel"], w=["prel"])
    for k in range(LV):
        t1, t2 = small("l1_%d" % k), small("l2_%d" % k)
        tt(t1[:], cr[k][:], cr[k][:], ALU.mult, r=["prel"], w=["prel"])
        tt(t2[:], ci[k][:], ci[k][:], ALU.mult, r=["prel"], w=["prel"])
        p.add("dve", lambda e, k=k: e.scalar_tensor_tensor(out=ci[k + 1][:], in0=cr[k][:], scalar=2.0, in1=ci[k][:], op0=ALU.mult, op1=ALU.mult),
              r=["prel"], w=["prel"])
        tt(cr[k + 1][:], t1[:], t2[:], ALU.subtract, r=["prel"], w=["prel"])
    den, nr, ni, qre, qim, t3 = small("den"), small("nr"), small("ni"), small("qre"), small("qim"), small("t3")
    tt(den[:], lr[:], lr[:], ALU.mult, r=["lr"], w=["prel"])
    tt(t3[:], li[:], li[:], ALU.mult, r=["li"], w=["prel"])
    tt(den[:], den[:], t3[:], ALU.add, r=["prel"], w=["prel"])
    p.add("dve", lambda e: e.reciprocal(out=den[:], in_=den[:]), r=["prel"], w=["prel"])
    tt(nr[:], rho[:], cr[0][:], ALU.mult, r=["rho", "prel"], w=["prel"])
    p.add("dve", lambda e: e.tensor_scalar(out=nr[:], in0=nr[:], scalar1=-1.0, scalar2=None, op0=ALU.add), r=["prel"], w=["prel"])
    tt(ni[:], rho[:], ci[0][:], ALU.mult, r=["rho", "prel"], w=["prel"])
    tt(qre[:], nr[:], lr[:], ALU.mult, r=["prel", "lr"], w=["prel"])
    tt(t3[:], ni[:], li[:], ALU.mult, r=["prel", "li"], w=["prel"])
    tt(qre[:], qre[:], t3[:], ALU.add, r=["prel"], w=["prel"])
    tt(qre[:], qre[:], den[:], ALU.mult, r=["prel"], w=["prel"])
    tt(qim[:], ni[:], lr[:], ALU.mult, r=["prel", "lr"], w=["prel"])
    tt(t3[:], nr[:], li[:], ALU.mult, r=["prel", "li"], w=["prel"])
    tt(qim[:], qim[:], t3[:], ALU.subtract, r=["prel"], w=["prel"])
    tt(qim[:], qim[:], den[:], ALU.mult, r=["prel"], w=["prel"])
    Er = p.sb("Er", [128, NT, LS])
    Ei = p.sb("Ei", [128, NT, LS])
    Fr = p.sb("Fr", [128, NT, LS])
    Fi = p.sb("Fi", [128, NT, LS])
    tmpL = p.sb("tmpL", [128, LS])
    p.add("pool", lambda e: e.memset(Er[:, :, 0:1], 1.0), w=["E"])
    p.add("pool", lambda e: e.memset(Ei[:, :, 0:1], 0.0), w=["E"])
    for j in range(NT):
        for k in range(LV):
            n = 1 << k
            crk, cik = cr[k][:, j:j + 1], ci[k][:, j:j + 1]
            p.add("dve", lambda e, j=j, n=n, cik=cik: e.tensor_scalar(out=tmpL[:, :n], in0=Ei[:, j, 0:n], scalar1=cik, scalar2=None, op0=ALU.mult),
                  r=["E", "prel"], w=["tmpL"])
            p.add("dve", lambda e, j=j, n=n, crk=crk: e.scalar_tensor_tensor(out=Er[:, j, n:2 * n], in0=Er[:, j, 0:n], scalar=crk, in1=tmpL[:, :n],
                                                                              op0=ALU.mult, op1=ALU.subtract), r=["E", "prel", "tmpL"], w=["Ern"])
            p.add("dve", lambda e, j=j, n=n, cik=cik: e.tensor_scalar(out=tmpL[:, :n], in0=Er[:, j, 0:n], scalar1=cik, scalar2=None, op0=ALU.mult),
                  r=["E", "prel", "Ern"], w=["tmpL"])
            p.add("dve", lambda e, j=j, n=n, crk=crk: e.scalar_tensor_tensor(out=Ei[:, j, n:2 * n], in0=Ei[:, j, 0:n], scalar=crk, in1=tmpL[:, :n],
                                                                              op0=ALU.mult, op1=ALU.add), r=["E", "prel", "tmpL"], w=["E"])
            p.add("dve", lambda e: e.engine_nop(), r=["Ern"], w=["E"]) if False else None
        qr_, qi_ = qre[:, j:j + 1], qim[:, j:j + 1]
        p.add("dve", lambda e, j=j, qi_=qi_: e.tensor_scalar(out=tmpL[:], in0=Ei[:, j, :], scalar1=qi_, scalar2=None, op0=ALU.mult),
              r=["E", "Ern", "prel"], w=["tmpL"])
        p.add("dve", lambda e, j=j, qr_=qr_: e.scalar_tensor_tensor(out=Fr[:, j, :], in0=Er[:, j, :], scalar=qr_, in1=tmpL[:], op0=ALU.mult, op1=ALU.add),
              r=["E", "Ern", "prel", "tmpL"], w=["F"])
        p.add("dve", lambda e, j=j, qr_=qr_: e.tensor_scalar(out=tmpL[:], in0=Ei[:, j, :], scalar1=qr_, scalar2=None, op0=ALU.mult),
              r=["E", "Ern", "prel", "F"], w=["tmpL"])
        p.add("dve", lambda e, j=j, qi_=qi_: e.scalar_tensor_tensor(out=Fi[:, j, :], in0=Er[:, j, :], scalar=qi_, in1=tmpL[:], op0=ALU.mult, op1=ALU.subtract),
              r=["E", "Ern", "prel", "tmpL"], w=["F"])
    bre_b = p.sb("bre_b", [128, NT, 128], BF16)
    bim_b = p.sb("bim_b", [128, NT, 128], BF16)
    cre_b = p.sb("cre_b", [128, NT, 128], BF16)
    cim_b = p.sb("cim_b", [128, NT, 128], BF16)
    for (dst, src, n_) in ((bre_b, bre, "bre_b"), (bim_b, bim, "bim_b"), (cre_b, cre, "cre_b"), (cim_b, cim, "cim_b")):
        p.dma("pool", dst[:], src.rearrange("j k m -> k j m"), w=[n_])
    ir_ = small("init_r")
    ii_ = small("init_i")
    p.add("pool", lambda e: e.memset(ir_[:], 0.0), w=["init"])
    p.add("pool", lambda e: e.memset(ii_[:], 0.0), w=["init"])
    ubf = [p.sb("ubf%d" % i, [128, 2, LS], BF16) for i in range(2)]
    uf = [p.sb("uf%d" % i, [128, 2, LS]) for i in range(2)]
    W = {}
    for nm in ("t1", "t2", "xr", "xi", "hr", "hi", "a", "b", "c", "d"):
        W[nm] = [p.sb("w_%s%d" % (nm, i), [128, LS]) for i in range(2)]
    hrb = [p.sb("hrb%d" % i, [128, LS], BF16) for i in range(2)]
    hib = [p.sb("hib%d" % i, [128, LS], BF16) for i in range(2)]
    yt = p.sb("yt", [128, LS])
    y2 = p.sb("y2", [128, LS])
    yo = [p.sb("yo%d" % i, [128, LS]) for i in range(2)]
    ELr, ELi = cr[LV], ci[LV]
    sc1s = [small("sc1_%d" % i) for i in range(2)]
    uv = uT.rearrange("(c p) t -> p c t", p=128)
    it = 0
    for sg in range(NSEG):
        ts = slice(sg * LS, (sg + 1) * LS)
        ub, uf_ = ubf[sg % 2], uf[sg % 2]
        ubn, ufn = "ubf%d" % (sg % 2), "uf%d" % (sg % 2)
        p.dma("pool", ub[:], uv[:, :, ts], w=[ubn])
        p.dma("sp", uf_[:], uv[:, :, ts], w=[ufn])
        for j in range(NT):
            cc = j // 4
            k_ = it % 2
            it += 1
            bA, bB = 2 + 2 * k_, 3 + 2 * k_
            nA, nB = "bank%d" % bA, "bank%d" % bB
            g = lambda nm: (W[nm][k_], "w_%s%d" % (nm, k_))
            p.add("pe", lambda e, j=j, cc=cc, bA=bA, ub=ub: e.matmul(p.banks[bA][:, :LS], lhsT=bre_b[:, j, :], rhs=ub[:, cc, :], start=True, stop=True),
                  r=["bre_b", ubn], w=[nA])
            p.add("pe", lambda e, j=j, cc=cc, bB=bB, ub=ub: e.matmul(p.banks[bB][:, :LS], lhsT=bim_b[:, j, :], rhs=ub[:, cc, :], start=True, stop=True),
                  r=["bim_b", ubn], w=[nB])
            (t1, t1n), (t2, t2n), (xr, xrn), (xi, xin) = g("t1"), g("t2"), g("xr"), g("xi")
            (hr, hrn), (hi, hin), (a, an), (b, bn), (c_, cn), (d_, dn) = g("hr"), g("hi"), g("a"), g("b"), g("c"), g("d")
            tt(t1[:], Fi[:, j, :], p.banks[bB][:, :LS], ALU.mult, r=["F", nB], w=[t1n])
            tt(xr[:], Fr[:, j, :], p.banks[bA][:, :LS], ALU.mult, r=["F", nA], w=[xrn])
            tt(t2[:], Fi[:, j, :], p.banks[bA][:, :LS], ALU.mult, r=["F", nA], w=[t2n])
            tt(xi[:], Fr[:, j, :], p.banks[bB][:, :LS], ALU.mult, r=["F", nB], w=[xin])
            tt(xr[:], xr[:], t1[:], ALU.subtract, eng="pool", r=[xrn, t1n], w=[xrn])
            tt(xi[:], xi[:], t2[:], ALU.add, eng="pool", r=[xin, t2n], w=[xin])
            rb = rho[:, j:j + 1].to_broadcast([128, LS])
            p.add("dve", lambda e, j=j, rb=rb, hr=hr, xr=xr: e.tensor_tensor_scan(out=hr[:], data0=rb, data1=xr[:], initial=ir_[:, j:j + 1],
                                                                                 op0=ALU.mult, op1=ALU.add), r=["rho", xrn, "init"], w=[hrn])
            p.add("dve", lambda e, j=j, rb=rb, hi=hi, xi=xi: e.tensor_tensor_scan(out=hi[:], data0=rb, data1=xi[:], initial=ii_[:, j:j + 1],
                                                                                 op0=ALU.mult, op1=ALU.add), r=["rho", xin, "init"], w=[hin])
            sc1 = sc1s[k_]
            p.add("dve", lambda e, j=j, hi=hi, sc1=sc1: e.tensor_scalar(out=sc1[:, 0:1], in0=hi[:, LS - 1:LS], scalar1=ELi[:, j:j + 1], scalar2=None, op0=ALU.mult),
                  r=[hin, "prel"], w=["sc1_%d" % k_])
            p.add("dve", lambda e, j=j, hr=hr, sc1=sc1: e.scalar_tensor_tensor(out=ir_[:, j:j + 1], in0=hr[:, LS - 1:LS], scalar=ELr[:, j:j + 1], in1=sc1[:, 0:1],
                                                                               op0=ALU.mult, op1=ALU.subtract), r=[hrn, "prel", "sc1_%d" % k_, "init"], w=["init"])
            p.add("dve", lambda e, j=j, hr=hr, sc1=sc1: e.tensor_scalar(out=sc1[:, 1:2], in0=hr[:, LS - 1:LS], scalar1=ELi[:, j:j + 1], scalar2=None, op0=ALU.mult),
                  r=[hrn, "prel"], w=["sc1_%d" % k_])
            p.add("dve", lambda e, j=j, hi=hi, sc1=sc1: e.scalar_tensor_tensor(out=ii_[:, j:j + 1], in0=hi[:, LS - 1:LS], scalar=ELr[:, j:j + 1], in1=sc1[:, 1:2],
                                                                               op0=ALU.mult, op1=ALU.add), r=[hin, "prel", "sc1_%d" % k_, "init"], w=["init"])
            tt(a[:], Er[:, j, :], hr[:], ALU.mult, eng="pool", r=["E", "Ern", hrn], w=[an])
            tt(b[:], Ei[:, j, :], hi[:], ALU.mult, eng="pool", r=["E", "Ern", hin], w=[bn])
            tt(c_[:], Er[:, j, :], hi[:], ALU.mult, eng="dve", r=["E", "Ern", hin], w=[cn])
            tt(d_[:], Ei[:, j, :], hr[:], ALU.mult, eng="dve", r=["E", "Ern", hrn], w=[dn])
            hb_, ib_ = hrb[k_], hib[k_]
            tt(hb_[:], a[:], b[:], ALU.subtract, eng="pool", r=[an, bn], w=["hrb%d" % k_])
            p.add("dve", lambda e, c_=c_, d_=d_, ib_=ib_: e.scalar_tensor_tensor(out=ib_[:], in0=c_[:], scalar=-1.0, in1=d_[:], op0=ALU.mult, op1=ALU.subtract),
                  r=[cn, dn], w=["hib%d" % k_])
            yb = 6 + (cc % 2)
            p.add("pe", lambda e, j=j, yb=yb, hb_=hb_: e.matmul(p.banks[yb][:, :LS], lhsT=cre_b[:, j, :], rhs=hb_[:], start=(j % 4 == 0), stop=False),
                  r=["cre_b", "hrb%d" % k_], w=["bank%d" % yb])
            p.add("pe", lambda e, j=j, yb=yb, ib_=ib_: e.matmul(p.banks[yb][:, :LS], lhsT=cim_b[:, j, :], rhs=ib_[:], start=False, stop=(j % 4 == 3)),
                  r=["cim_b", "hib%d" % k_], w=["bank%d" % yb])
            if j % 4 == 3:
                yo_ = yo[cc % 2]
                yon = "yo%d" % (cc % 2)
                p.add("dve", lambda e, cc=cc, yb=yb, uf_=uf_: e.scalar_tensor_tensor(out=yt[:], in0=uf_[:, cc, :], scalar=dskt[:, cc:cc + 1], in1=p.banks[yb][:, :LS],
                                                                                    op0=ALU.mult, op1=ALU.add), r=[ufn, "dskt", "bank%d" % yb], w=["yt"])
                p.add("act", lambda e: e.activation(out=y2[:], in_=yt[:], func=AF.Square), r=["yt"], w=["y2"])
                p.add("dve", lambda e: e.tensor_scalar(out=y2[:], in0=y2[:], scalar1=0.044715, scalar2=1.0, op0=ALU.mult, op1=ALU.add), r=["y2"], w=["y2"])
                tt(y2[:], y2[:], yt[:], ALU.mult, r=["y2", "yt"], w=["y2"])
                p.add("act", lambda e: e.activation(out=y2[:], in_=y2[:], func=AF.Tanh, scale=math.sqrt(2.0 / math.pi)), r=["y2"], w=["y2"])
                p.add("dve", lambda e: e.tensor_scalar(out=y2[:], in0=y2[:], scalar1=0.5, scalar2=0.5, op0=ALU.mult, op1=ALU.add), r=["y2"], w=["y2"])
                tt(yo_[:], y2[:], yt[:], ALU.mult, r=["y2", "yt"], w=[yon])
                p.dma("sp", yT[cc * 128:(cc + 1) * 128, ts], yo_[:], r=[yon])
    return p.finish()


def build_s5_out(c):
    p = P()
    D, TC, TT, KC, NTT = c.D, c.TC, c.TT, c.KC, c.NTT
    yT = p.inp("yT", [D, TC])
    xT = p.inp("xT", [D, TC])
    w = p.inp("w", [2 * KC, 128, KC * 128])
    out = p.outp("xo", [D, TC])
    abf = p.sb("abf", [128, KC, TC], BF16)
    p.dma("pool", abf[:], yT.rearrange("(c p) t -> p c t", p=128), w=["abf"])
    NW = 3
    wb = [p.sb("wb%d" % i, [128, 2, KC * 128], BF16) for i in range(NW)]
    sg = [p.sb("sg%d" % i, [128, TT]) for i in range(2)]
    xb = [p.sb("xb%d" % i, [128, TC]) for i in range(2)]
    ob = [p.sb("ob%d" % i, [128, TC]) for i in range(2)]
    for j in range(KC):
        wbj = wb[j % NW]
        wn = "wb%d" % (j % NW)
        p.dma("pool", wbj[:, 0, :], w[j], w=[wn + "g"])
        p.dma("pool", wbj[:, 1, :], w[KC + j], w=[wn + "u"])
        for half in range(2):
            for kc in range(KC):
                for tt in range(NTT):
                    bk = half * 4 + tt
                    p.add("pe", lambda e, half=half, kc=kc, tt=tt, bk=bk, wbj=wbj: e.matmul(
                        p.banks[bk][:, :TT], lhsT=wbj[:, half, kc * 128:(kc + 1) * 128],
                        rhs=abf[:, kc, tt * TT:(tt + 1) * TT], start=(kc == 0), stop=(kc == KC - 1)),
                        r=[wn + "gu"[half], "abf"], w=["bank%d" % bk])
        obj, xbj = ob[j % 2], xb[j % 2]
        on, xn = "ob%d" % (j % 2), "xb%d" % (j % 2)
        p.dma("sp", xbj[:], xT[j * 128:(j + 1) * 128, :], w=[xn])
        for tt in range(NTT):
            sgb = sg[tt % 2]
            sn = "sg%d" % (tt % 2)
            tsl = slice(tt * TT, (tt + 1) * TT)
            p.add("act", lambda e, tt=tt, sgb=sgb: e.activation(out=sgb[:], in_=p.banks[4 + tt][:, :TT], func=AF.Sigmoid),
                  r=["bank%d" % (4 + tt)], w=[sn])
            p.add("dve", lambda e, tt=tt, sgb=sgb: e.tensor_tensor(out=sgb[:], in0=sgb[:], in1=p.banks[tt][:, :TT], op=ALU.mult),
                  r=[sn, "bank%d" % tt], w=[sn])
            p.add("pool", lambda e, sgb=sgb, obj=obj, xbj=xbj, tsl=tsl: e.tensor_tensor(out=obj[:, tsl], in0=sgb[:], in1=xbj[:, tsl], op=ALU.add),
                  r=[sn, xn], w=[on])
        p.dma("sp", out[j * 128:(j + 1) * 128, :], obj[:], r=[on])
    return p.finish()


def build_rwkv_pre(c):
    p = P()
    D, TC, KC = c.D, c.TC, c.KC
    RT = min(256, TC)
    NRT = TC // RT
    LW = max(32, int(round(1.8 * D ** 0.5 / 32)) * 32)
    LG = max(32, int(round(0.6 * D ** 0.8 / 32)) * 32)
    LGC = (LG + 127) // 128
    xe = p.inp("xe", [D, TC + 1])
    g = p.inp("g", [128, KC])
    mu = p.inp("mu", [128, 6, KC])
    wr = p.inp("wr", [KC, 128, KC * 128])
    wk = p.inp("wk", [KC, 128, KC * 128])
    wv = p.inp("wv", [KC, 128, KC * 128])
    w1 = p.inp("w1", [1, 128, KC * 128])
    a1 = p.inp("a1", [1, 128, KC * 128])
    g1 = p.inp("g1", [LGC, 128, KC * 128])
    w2 = p.inp("w2", [KC, 128, 128])
    a2 = p.inp("a2", [KC, 128, 128])
    g2 = p.inp("g2", [KC, 128, LGC * 128])
    vecs = p.inp("vecs", [128, 4, KC])
    bd = p.inp("bd", [128, 128])
    outs = {n: p.outp(n, [D, TC]) for n in ("rT", "kT", "vT", "gT")}
    souts = {n: p.outp(n, [3, D, TC], BF16) for n in ("rS", "wS", "kS", "aS", "bS")}

    ones = p.const("ones", 1.0, (128, 128))
    epst = p.const("epst", 1e-6)
    gt = p.sb("gt", [128, KC])
    mut = p.sb("mut", [128, 6, KC])
    vt = p.sb("vt", [128, 4, KC])
    bdt = p.sb("bdt", [128, 128])
    p.dma("sp", gt[:], g, w=["gt"])
    p.dma("sp", mut[:], mu, w=["mut"])
    p.dma("sp", vt[:], vecs, w=["vt"])
    p.dma("sp", bdt[:], bd, w=["bdt"])
    xs = p.sb("xs", [128, KC, RT + 1])
    hh = p.sb("hh", [128, KC, RT + 1])
    xx = p.sb("xx", [128, KC, RT])
    scr = [p.sb("scr%d" % i, [128, RT + 1]) for i in range(2)]
    rstd = p.sb("rstd", [128, RT + 1])
    xi = [p.sb("xi%d" % i, [128, KC, RT], BF16) for i in range(2)]
    kf = p.sb("kf", [128, KC, RT])
    af = p.sb("af", [128, KC, RT])
    lo = p.sb("lo", [128, LGC, RT], BF16)
    wbig = [p.sb("wbig%d" % i, [128, KC * 128], BF16) for i in range(3)]
    wsm = [p.sb("wsm%d" % i, [128, LGC * 128], BF16) for i in range(2)]
    st = [p.sb("st%d" % i, [128, RT]) for i in range(4)]
    tmp = [p.sb("tmp%d" % i, [128, RT]) for i in range(4)]
    xv = xe.rearrange("(c p) t -> p c t", p=128)
    cnt = {"w": 0, "s": 0, "st": 0, "bank": 0, "sp": 0}

    def lerp(i, buf):
        for kc in range(KC):
            p.add("dve", lambda e, kc=kc: e.scalar_tensor_tensor(out=xi[buf][:, kc, :], in0=xx[:, kc, :], scalar=mut[:, i, kc:kc + 1],
                                                                  in1=hh[:, kc, 1:RT + 1], op0=ALU.mult, op1=ALU.add),
                  r=["xx", "hh", "mut"], w=["xi%d" % buf])

    def gemm(wd, MC, KCI, rhs_of, rhs_name, epi, small=False, ks_last=128):
        for j in range(MC):
            if small:
                wb_, wn = wsm[cnt["s"] % 2], "wsm%d" % (cnt["s"] % 2)
                cnt["s"] += 1
            else:
                wb_, wn = wbig[cnt["w"] % 3], "wbig%d" % (cnt["w"] % 3)
                cnt["w"] += 1
            p.dma("pool", wb_[:, :KCI * 128], wd[j], w=[wn])
            bk = 1 + cnt["bank"] % 7
            cnt["bank"] += 1
            for kc in range(KCI):
                p.add("pe", lambda e, kc=kc, bk=bk, wb_=wb_: e.matmul(p.banks[bk][:, :RT], lhsT=wb_[:, kc * 128:(kc + 1) * 128], rhs=rhs_of(kc),
                                                                      start=(kc == 0), stop=(kc == KCI - 1)), r=[wn, rhs_name], w=["bank%d" % bk])
            epi(j, p.banks[bk], "bank%d" % bk)

    def store_epi(name, t0, split=None):
        def epi(j, bank, bn):
            s_ = st[cnt["st"] % 4]
            sn = "st%d" % (cnt["st"] % 4)
            cnt["st"] += 1
            p.add("act", lambda e: e.copy(out=s_[:], in_=bank[:, :RT]), r=[bn], w=[sn])
            p.dma("sp", outs[name][j * 128:(j + 1) * 128, t0:t0 + RT], s_[:], r=[sn])
            if split:
                split_store(split, j, t0, s_[:], sn)
        return epi

    def store(name, j, t0, src, srcname):
        p.dma("sp", outs[name][j * 128:(j + 1) * 128, t0:t0 + RT], src, r=[srcname])

    spb = [[p.sb("spb%d_%d" % (i, k), [128, RT], BF16) for k in range(3)] for i in range(2)]
    spr = [p.sb("spr%d" % i, [128, RT]) for i in range(2)]

    def split_store(name, j, t0, src, srcname):
        i = cnt["sp"] % 2
        cnt["sp"] += 1
        hi, mid, lo = spb[i]
        rr_ = spr[i]
        nh, nm, nl, nr_ = ("spb%d_%d" % (i, 0), "spb%d_%d" % (i, 1), "spb%d_%d" % (i, 2), "spr%d" % i)
        p.add("act", lambda e: e.copy(out=hi[:], in_=src), r=[srcname], w=[nh])
        p.add("pool", lambda e: e.tensor_tensor(out=rr_[:], in0=src, in1=hi[:], op=ALU.subtract), r=[srcname, nh], w=[nr_])
        p.add("act", lambda e: e.copy(out=mid[:], in_=rr_[:]), r=[nr_], w=[nm])
        p.add("pool", lambda e: e.tensor_tensor(out=rr_[:], in0=rr_[:], in1=mid[:], op=ALU.subtract), r=[nr_, nm], w=[nr_])
        p.add("act", lambda e: e.copy(out=lo[:], in_=rr_[:]), r=[nr_], w=[nl])
        for k, (t_, n_) in enumerate(((hi, nh), (mid, nm), (lo, nl))):
            p.dma("sp", souts[name][k, j * 128:(j + 1) * 128, t0:t0 + RT], t_[:], r=[n_])

    for tt in range(NRT):
        t0 = tt * RT
        p.dma("sp", xs[:], xv[:, :, t0:t0 + RT + 1], w=["xs"])
        emit_rstd(p, xs, "xs", KC, RT + 1, ones, rstd, "rstd", epst, "epst", scr, D)
        for kc in range(KC):
            p.add("dve", lambda e, kc=kc: e.scalar_tensor_tensor(out=hh[:, kc, :], in0=xs[:, kc, :], scalar=gt[:, kc:kc + 1], in1=rstd[:],
                                                                  op0=ALU.mult, op1=ALU.mult), r=["xs", "gt", "rstd"], w=["hh"])
        p.add("pool", lambda e: e.tensor_tensor(out=xx[:], in0=hh[:, :, 0:RT], in1=hh[:, :, 1:RT + 1], op=ALU.subtract), r=["hh"], w=["xx"])
        lerp(0, 0)
        gemm(wr, KC, KC, lambda kc: xi[0][:, kc, :], "xi0", store_epi("rT", t0, split="rS"))
        lerp(2, 1)

        def k_epi(j, bank, bn):
            p.add("act", lambda e: e.copy(out=kf[:, j, :], in_=bank[:, :RT]), r=[bn], w=["kf"])
        gemm(wk, KC, KC, lambda kc: xi[1][:, kc, :], "xi1", k_epi)
        lerp(3, 0)
        gemm(wv, KC, KC, lambda kc: xi[0][:, kc, :], "xi0", store_epi("vT", t0))
        lerp(1, 1)

        def w1_epi(j, bank, bn):
            p.add("act", lambda e: e.activation(out=lo[:, 0, :], in_=bank[:, :RT], func=AF.Tanh), r=[bn], w=["lo"])
        gemm(w1, 1, KC, lambda kc: xi[1][:, kc, :], "xi1", w1_epi)

        def w2_epi(j, bank, bn):
            s_ = st[cnt["st"] % 4]
            sn = "st%d" % (cnt["st"] % 4)
            cnt["st"] += 1
            p.add("act", lambda e: e.activation(out=s_[:], in_=bank[:, :RT], func=AF.Sigmoid, bias=vt[:, 0, j:j + 1]), r=[bn, "vt"], w=[sn])
            p.add("act", lambda e: e.activation(out=s_[:], in_=s_[:], func=AF.Exp, scale=-math.exp(-0.5)), r=[sn], w=[sn])
            split_store("wS", j, t0, s_[:], sn)
        gemm(w2, KC, 1, lambda kc: lo[:, 0, :], "lo", w2_epi, small=True)
        lerp(4, 0)

        def a1_epi(j, bank, bn):
            p.add("act", lambda e: e.copy(out=lo[:, 0, :], in_=bank[:, :RT]), r=[bn], w=["lo"])
        gemm(a1, 1, KC, lambda kc: xi[0][:, kc, :], "xi0", a1_epi)

        def a2_epi(j, bank, bn):
            p.add("act", lambda e: e.activation(out=af[:, j, :], in_=bank[:, :RT], func=AF.Sigmoid, bias=vt[:, 1, j:j + 1]), r=[bn, "vt"], w=["af"])
        gemm(a2, KC, 1, lambda kc: lo[:, 0, :], "lo", a2_epi, small=True)
        lerp(5, 1)

        def g1_epi(j, bank, bn):
            p.add("act", lambda e: e.activation(out=lo[:, j, :], in_=bank[:, :RT], func=AF.Sigmoid), r=[bn], w=["lo"])
        gemm(g1, LGC, KC, lambda kc: xi[1][:, kc, :], "xi1", g1_epi)
        gemm(g2, KC, LGC, lambda kc: lo[:, kc, :], "lo", store_epi("gT", t0), small=True)
        for kc in range(KC):
            kk, sq, rn, t4 = tmp
            p.add("dve", lambda e, kc=kc: e.tensor_scalar(out=kk[:], in0=kf[:, kc, :], scalar1=vt[:, 2, kc:kc + 1], scalar2=None, op0=ALU.mult),
                  r=["kf", "vt"], w=["tmp0"])
            p.add("act", lambda e: e.activation(out=sq[:], in_=kk[:], func=AF.Square), r=["tmp0"], w=["tmp1"])
            p.add("pe", lambda e: e.matmul(p.banks[0][:, :RT], lhsT=bdt[:], rhs=sq[:], start=True, stop=True), r=["bdt", "tmp1"], w=["bank0"])
            p.add("act", lambda e: e.activation(out=rn[:], in_=p.banks[0][:, :RT], func=AF.Sqrt), r=["bank0"], w=["tmp2"])
            p.add("dve", lambda e: e.tensor_scalar(out=rn[:], in0=rn[:], scalar1=1e-12, scalar2=None, op0=ALU.max), r=["tmp2"], w=["tmp2"])
            p.add("dve", lambda e: e.reciprocal(out=rn[:], in_=rn[:]), r=["tmp2"], w=["tmp2"])
            p.add("dve", lambda e: e.tensor_tensor(out=kk[:], in0=kk[:], in1=rn[:], op=ALU.mult), r=["tmp0", "tmp2"], w=["tmp0"])
            s_a = st[cnt["st"] % 4]; na = "st%d" % (cnt["st"] % 4); cnt["st"] += 1
            s_b = st[cnt["st"] % 4]; nb = "st%d" % (cnt["st"] % 4); cnt["st"] += 1
            s_k = st[cnt["st"] % 4]; nk = "st%d" % (cnt["st"] % 4); cnt["st"] += 1
            p.add("act", lambda e, s_a=s_a: e.mul(out=s_a[:], in_=kk[:], mul=-1.0), r=["tmp0"], w=[na])
            split_store("aS", kc, t0, s_a[:], na)
            p.add("pool", lambda e, kc=kc, s_b=s_b: e.tensor_tensor(out=s_b[:], in0=kk[:], in1=af[:, kc, :], op=ALU.mult), r=["tmp0", "af"], w=[nb])
            split_store("bS", kc, t0, s_b[:], nb)
            p.add("dve", lambda e, kc=kc: e.tensor_scalar(out=t4[:], in0=af[:, kc, :], scalar1=-1.0, scalar2=vt[:, 3, kc:kc + 1], op0=ALU.add, op1=ALU.mult),
                  r=["af", "vt"], w=["tmp3"])
            p.add("dve", lambda e, kc=kc, s_k=s_k: e.scalar_tensor_tensor(out=s_k[:], in0=t4[:], scalar=1.0, in1=kf[:, kc, :], op0=ALU.add, op1=ALU.mult),
                  r=["tmp3", "kf"], w=[nk])
            store("kT", kc, t0, s_k[:], nk)
            split_store("kS", kc, t0, s_k[:], nk)
    return p.finish()


def build_rwkv_scan(c):
    p = P()
    S = c.S
    CH = 8
    NCH = S // CH
    names = ("w", "k", "a", "b", "r")
    xin = {n: p.inp(n + "h", [12, S * 64], BF16) for n in names}
    vin = p.inp("vv", [128, S, 2])
    sel = p.inp("sel", [2, 12, 128], BF16)
    yout = p.outp("yy", [128, S, 2])
    selt = p.sb("selt", [12, 2, 128], BF16)
    p.dma("sp", selt[:], sel.rearrange("g h m -> h g m"), w=["selt"])
    LD = 8
    xh = {n: [p.sb("xh_%s%d" % (n, i), [12, LD * CH * 64], BF16) for i in range(2)] for n in names}
    NB = 3
    xb = {n: [p.sb("xb_%s%d" % (n, i), [128, CH, 2, 64]) for i in range(NB)] for n in names}
    kv = [p.sb("kv%d" % i, [128, CH, 2, 64]) for i in range(NB)]
    vb = [p.sb("vb%d" % i, [128, LD * CH, 2]) for i in range(2)]
    yb = [p.sb("yb%d" % i, [128, LD * CH, 2]) for i in range(2)]
    St = p.sb("St", [128, 2, 64])
    SW = p.sb("SW", [128, 2, 64])
    SWK = p.sb("SWK", [128, 2, 64])
    m = p.sb("m", [128, 2, 64])
    T = p.sb("T", [128, 2, 64])
    mrb = [p.sb("mrb%d" % i, [128, CH, 2, 64]) for i in range(NB)]
    sa = p.sb("sa", [128, 2])
    p.add("pool", lambda e: e.memset(St[:], 0.0), w=["St"])
    bankc = 0
    pending = [None]
    for ch in range(NCH):
        ld, li = divmod(ch, LD)
        if li == 0:
            for n in names:
                p.dma("sp", xh[n][ld % 2][:], xin[n][:, ld * LD * CH * 64:(ld + 1) * LD * CH * 64], w=["xh_%s%d" % (n, ld % 2)])
            p.dma("sp", vb[ld % 2][:], vin[:, ld * LD * CH:(ld + 1) * LD * CH, :], w=["vb%d" % (ld % 2)])
        b_ = ch % NB
        for n in names:
            for gi in range(2):
                bk = bankc % 8
                bankc += 1
                p.add("pe", lambda e, n=n, gi=gi, bk=bk, ld=ld, li=li: e.matmul(
                    p.banks[bk][:, :CH * 64], lhsT=selt[:, gi, :], rhs=xh[n][ld % 2][:, li * CH * 64:(li + 1) * CH * 64], start=True, stop=True),
                    r=["selt", "xh_%s%d" % (n, ld % 2)], w=["bank%d" % bk])
                p.add("act", lambda e, n=n, gi=gi, bk=bk, b_=b_: e.copy(out=xb[n][b_][:, :, gi, :],
                                                                       in_=p.banks[bk][:, :CH * 64].rearrange("p (t j) -> p t j", j=64)),
                      r=["bank%d" % bk], w=["xb_%s%d" % (n, b_)])
        vsl = vb[ld % 2][:, li * CH:(li + 1) * CH, :]
        p.add("pool", lambda e, b_=b_, vsl=vsl: e.tensor_tensor(out=kv[b_][:], in0=xb["k"][b_][:], in1=vsl.unsqueeze(3).to_broadcast([128, CH, 2, 64]), op=ALU.mult),
              r=["xb_k%d" % b_, "vb%d" % (ld % 2)], w=["kv%d" % b_])
        for s_ in range(CH):
            A, W_, B, R = (xb[n][b_][:, s_, :, :] for n in ("a", "w", "b", "r"))
            an, wn, bn, rn = ("xb_%s%d" % (n, b_) for n in ("a", "w", "b", "r"))
            p.add("dve", lambda e, A=A: e.tensor_tensor(out=m[:], in0=St[:], in1=A, op=ALU.mult), r=["St", an], w=["m"])
            p.add("pool", lambda e, W_=W_: e.tensor_tensor(out=SW[:], in0=St[:], in1=W_, op=ALU.mult), r=["St", wn], w=["SW"])
            if pending[0] is not None:
                pending[0]()
                pending[0] = None
            p.add("dve", lambda e: e.tensor_reduce(out=sa[:], in_=m[:], axis=AX.X, op=ALU.add), r=["m"], w=["sa"])
            p.add("dve", lambda e, s_=s_, b_=b_: e.tensor_tensor(out=SWK[:], in0=SW[:], in1=kv[b_][:, s_, :, :], op=ALU.add), r=["SW", "kv%d" % b_], w=["SWK"])
            p.add("dve", lambda e, B=B: e.tensor_tensor(out=T[:], in0=B, in1=sa[:].unsqueeze(2).to_broadcast([128, 2, 64]), op=ALU.mult),
                  r=[bn, "sa"], w=["T"])
            p.add("dve", lambda e: e.tensor_tensor(out=St[:], in0=SWK[:], in1=T[:], op=ALU.add), r=["SWK", "T"], w=["St"])
            def _mk(R=R, s_=s_, b_=b_, rn=rn, ld=ld, li=li):
                def f():
                    p.add("pool", lambda e: e.tensor_tensor(out=mrb[b_][:, s_, :, :], in0=St[:], in1=R, op=ALU.mult), r=["St", rn], w=["mrb%d" % b_])
                    if s_ == CH - 1:
                        p.add("dve", lambda e: e.tensor_reduce(out=yb[ld % 2][:, li * CH:(li + 1) * CH, :], in_=mrb[b_][:], axis=AX.X, op=ALU.add),
                              r=["mrb%d" % b_], w=["yb%d" % (ld % 2)])
                        if li == LD - 1:
                            p.dma("sp", yout[:, ld * LD * CH:(ld + 1) * LD * CH, :], yb[ld % 2][:], r=["yb%d" % (ld % 2)])
                return f
            pending[0] = _mk()
    if pending[0] is not None:
        pending[0]()
    return p.finish()


def build_rwkv_post(c):
    p = P()
    D, TC, TT, KC, NTT = c.D, c.TC, c.TT, c.KC, c.NTT
    ins = {n: p.inp(n, [D, TC]) for n in ("yT", "rT", "kT", "vT", "gT", "xT")}
    vecs = p.inp("vecs", [128, 3, KC])
    bd = p.inp("bd", [128, 128])
    w = p.inp("w", [KC, 128, KC * 128])
    out = p.outp("xo", [D, TC])
    vt = p.sb("vt", [128, 3, KC])
    bdt = p.sb("bdt", [128, 128])
    p.dma("sp", vt[:], vecs, w=["vt"])
    p.dma("sp", bdt[:], bd, w=["bdt"])
    lneps = p.const("lneps", 64e-5)
    abf = p.sb("abf", [128, KC, TC], BF16)
    L = {n: [p.sb("l_%s%d" % (n, i), [128, TT]) for i in range(2)] for n in ("yT", "rT", "kT", "vT", "gT")}
    t1, t2, t3 = (p.sb("pt%d" % i, [128, TT]) for i in range(3))
    it = 0
    for kc in range(KC):
        for tt in range(NTT):
            b_ = it % 2
            it += 1
            tsl = slice(tt * TT, (tt + 1) * TT)
            for n in L:
                p.dma("sp", L[n][b_][:], ins[n][kc * 128:(kc + 1) * 128, tsl], w=["l_%s%d" % (n, b_)])
            y, r_, k_, v_, g_ = (L[n][b_] for n in ("yT", "rT", "kT", "vT", "gT"))
            yn, rn, kn, vn, gn = ("l_%s%d" % (n, b_) for n in ("yT", "rT", "kT", "vT", "gT"))
            B0, B1, B2 = p.banks[0], p.banks[1], p.banks[2]
            p.add("pe", lambda e, y=y: e.matmul(B0[:, :TT], lhsT=bdt[:], rhs=y[:], start=True, stop=True), r=["bdt", yn], w=["bank0"])
            p.add("dve", lambda e, y=y: e.scalar_tensor_tensor(out=t1[:], in0=B0[:, :TT], scalar=-1.0 / 64, in1=y[:], op0=ALU.mult, op1=ALU.add),
                  r=["bank0", yn], w=["pt0"])
            p.add("act", lambda e: e.activation(out=t2[:], in_=t1[:], func=AF.Square), r=["pt0"], w=["pt1"])
            p.add("pe", lambda e: e.matmul(B1[:, :TT], lhsT=bdt[:], rhs=t2[:], start=True, stop=True), r=["bdt", "pt1"], w=["bank1"])
            p.add("act", lambda e: e.activation(out=t2[:], in_=B1[:, :TT], func=AF.Sqrt, scale=1.0 / 64, bias=lneps[:]), r=["bank1", "lneps"], w=["pt1"])
            p.add("dve", lambda e: e.reciprocal(out=t2[:], in_=t2[:]), r=["pt1"], w=["pt1"])
            p.add("dve", lambda e: e.tensor_tensor(out=t1[:], in0=t1[:], in1=t2[:], op=ALU.mult), r=["pt0", "pt1"], w=["pt0"])
            p.add("dve", lambda e, kc=kc: e.tensor_scalar(out=t1[:], in0=t1[:], scalar1=vt[:, 0, kc:kc + 1], scalar2=vt[:, 1, kc:kc + 1], op0=ALU.mult, op1=ALU.add),
                  r=["pt0", "vt"], w=["pt0"])
            p.add("dve", lambda e, kc=kc, r_=r_, k_=k_: e.scalar_tensor_tensor(out=t3[:], in0=r_[:], scalar=vt[:, 2, kc:kc + 1], in1=k_[:], op0=ALU.mult, op1=ALU.mult),
                  r=[rn, kn, "vt"], w=["pt2"])
            p.add("pe", lambda e: e.matmul(B2[:, :TT], lhsT=bdt[:], rhs=t3[:], start=True, stop=True), r=["bdt", "pt2"], w=["bank2"])
            p.add("dve", lambda e, v_=v_: e.tensor_tensor(out=t3[:], in0=B2[:, :TT], in1=v_[:], op=ALU.mult), r=["bank2", vn, "pt2"], w=["pt2"])
            p.add("pool", lambda e: e.tensor_tensor(out=t1[:], in0=t1[:], in1=t3[:], op=ALU.add), r=["pt0", "pt2"], w=["pt0"])
            p.add("dve", lambda e, kc=kc, tsl=tsl, g_=g_: e.tensor_tensor(out=abf[:, kc, tsl], in0=t1[:], in1=g_[:], op=ALU.mult), r=["pt0", gn], w=["abf"])
    emit_resid_gemm(p, c, w, KC, abf, ins["xT"], out, 1.0, 0, NTT)
    return p.finish()


def build_final_norm(c):
    p = P()
    D, TC, TT, KC = c.D, c.TC, c.TT, c.KC
    xT = p.inp("xT", [D, TC])
    g = p.inp("g", [128, KC])
    out = p.outp("o", [D, TC])
    ones = p.const("ones", 1.0, (128, 128))
    epst = p.const("epst", 1e-6)
    gt = p.sb("gt", [128, KC])
    xs = p.sb("xs", [128, KC, TT])
    ob = p.sb("ob", [128, KC, TT])
    scr = [p.sb("scr%d" % i, [128, TT]) for i in range(2)]
    rstd = p.sb("rstd", [128, TT])
    p.dma("sp", gt[:], g, w=["gt"])
    xv = xT.rearrange("(c p) t -> p c t", p=128)
    ov = out.rearrange("(c p) t -> p c t", p=128)
    for tt in range(c.NTT):
        tsl = slice(tt * TT, (tt + 1) * TT)
        p.dma("sp", xs[:], xv[:, :, tsl], w=["xs"])
        emit_rstd(p, xs, "xs", KC, TT, ones, rstd, "rstd", epst, "epst", scr, D)
        for kc in range(KC):
            p.add("dve", lambda e, kc=kc: e.scalar_tensor_tensor(out=ob[:, kc, :], in0=xs[:, kc, :], scalar=gt[:, kc:kc + 1], in1=rstd[:],
                                                                  op0=ALU.mult, op1=ALU.mult), r=["xs", "gt", "rstd"], w=["ob"])
        p.dma("sp", ov[:, :, tsl], ob[:], r=["ob"])
    return p.finish()


_PROGS = {}


def _prog(name, builder, *a):
    if name not in _PROGS:
        _PROGS[name] = builder(*a)
    return _PROGS[name]


def _run(nc, in_maps):
    n = len(in_maps)
    res = run_bass_kernel_spmd(nc, in_maps, core_ids=list(range(n)))
    return res.results


def _tile_w(w, K, M):
    Kp = (K + 127) // 128 * 128
    Mp = (M + 127) // 128 * 128
    if (Kp, Mp) != (K, M):
        wp = np.zeros((Kp, Mp), np.float32)
        wp[:K, :M] = w
    else:
        wp = np.asarray(w, np.float32)
    return np.ascontiguousarray(wp.reshape(Kp // 128, 128, Mp // 128, 128).transpose(2, 1, 0, 3)).reshape(Mp // 128, 128, (Kp // 128) * 128)


def _pc(v):
    return np.ascontiguousarray(np.asarray(v, np.float32).reshape(-1, 128).T)


def _cat(res, key):
    return np.concatenate([r[key] for r in res], 1)


def _sl(A, i, TC):
    return np.ascontiguousarray(A[:, i * TC:(i + 1) * TC])


def _ffn(c, xT, g, wup, wdn):
    NC_ = c.NCORE
    wt = _tile_w(wup, c.D, 2 * c.F)
    gl = _pc(g)
    r1 = _run(_prog("ffn_up", build_ffn_up, c), [{"xT": xT[i], "g": gl, "w": wt} for i in range(NC_)])
    del wt
    wd = _tile_w(wdn, c.F, c.D)
    r2 = _run(_prog("ffn_down", build_ffn_down, c), [{"act": r1[i]["act"], "xT": xT[i], "w": wd} for i in range(NC_)])
    return [r2[i]["xo"] for i in range(NC_)]


def _fox(c, xT, g, w_in, b_f, qk_gain, w_out):
    import ml_dtypes
    D, S, TC, NC_ = c.D, c.S, c.TC, c.NCORE
    wt = _tile_w(w_in, D, c.FOXIN)
    gl = _pc(g)
    r1 = _run(_prog("fox_in", build_norm_gemm, c, c.FOXIN), [{"xT": xT[i], "g": gl, "w": wt} for i in range(NC_)])
    projT = _cat(r1, "o")
    del r1, wt
    FH = c.FH
    HPC = max(1, FH // NC_)
    ncore = FH // HPC
    kk = np.arange(128)[:, None]
    qq = np.arange(512)[None, :]
    masks = np.stack([np.where(kk + d * 128 <= qq, 0.0, -30000.0) for d in range(4)]).astype(ml_dtypes.bfloat16)
    ustr = (np.arange(128)[:, None] < np.arange(128)[None, :]).astype(np.float32)
    ident = np.eye(128, dtype=np.float32)
    ins = []
    for ci in range(ncore):
        hs = list(range(ci * HPC, (ci + 1) * HPC))
        ins.append({"qT": np.stack([projT[hh * 128:(hh + 1) * 128] for hh in hs]),
                    "kT": np.stack([projT[D + hh * 128: D + (hh + 1) * 128] for hh in hs]),
                    "v": np.stack([np.ascontiguousarray(projT[2 * D + hh * 128: 2 * D + (hh + 1) * 128].reshape(128, S // 128, 128).transpose(2, 1, 0)).reshape(128, S) for hh in hs]),
                    "fl": np.stack([projT[4 * D + hh] for hh in hs]),
                    "negb": np.stack([np.full((128, 1), -np.float32(b_f[hh]), np.float32) for hh in hs]),
                    "qg": np.asarray(qk_gain[0], np.float32).reshape(128, 1).copy(),
                    "kg": np.asarray(qk_gain[1], np.float32).reshape(128, 1).copy(),
                    "masks": masks, "ustr": ustr, "ident": ident})
    r2 = _run(_prog("fox_attn", build_fox_attn, c, HPC), ins)
    oT = np.concatenate([r2[ci]["oT"][j] for ci in range(ncore) for j in range(HPC)], 0)
    gT = projT[3 * D:4 * D]
    wo = _tile_w(w_out, D, D)
    r3 = _run(_prog("fox_out", build_fox_out, c), [{"oT": _sl(oT, i, TC), "gT": _sl(gT, i, TC), "xT": xT[i], "w": wo} for i in range(NC_)])
    return [r3[i]["xo"] for i in range(NC_)]


def _s5(c, xT, g, w_in, lam_re, lam_im, log_step, b_re, b_im, c_re, c_im, d_skip, w_out):
    D, S, TC, NC_ = c.D, c.S, c.TC, c.NCORE
    wt = _tile_w(w_in, D, D)
    gl = _pc(g)
    r1 = _run(_prog("s5_in", build_norm_gemm, c, D), [{"xT": xT[i], "g": gl, "w": wt} for i in range(NC_)])
    uT = _cat(r1, "o")
    del r1
    ncs = D // 256
    lstep = np.repeat(np.asarray(log_step, np.float32)[:, None], 64, 1)
    ins = []
    for ci in range(ncs):
        g0 = ci * 16
        st = lambda a: np.ascontiguousarray(np.asarray(a, np.float32)[g0:g0 + 16].reshape(8, 128).T)
        bre = np.zeros((8, 128, 128), np.float32)
        bim = np.zeros_like(bre)
        cre = np.zeros_like(bre)
        cim = np.zeros_like(bre)
        for j in range(8):
            cc = j // 4
            for gs in range(2):
                gl_ = 2 * j + gs
                gc = gl_ - cc * 8
                gg = g0 + gl_
                bre[j, gc * 16:(gc + 1) * 16, gs * 64:(gs + 1) * 64] = b_re[gg].T
                bim[j, gc * 16:(gc + 1) * 16, gs * 64:(gs + 1) * 64] = b_im[gg].T
                cre[j, gs * 64:(gs + 1) * 64, gc * 16:(gc + 1) * 16] = c_re[gg].T
                cim[j, gs * 64:(gs + 1) * 64, gc * 16:(gc + 1) * 16] = c_im[gg].T
        ins.append({"uT": np.ascontiguousarray(uT[ci * 256:(ci + 1) * 256]), "lre": st(lam_re), "lim": st(lam_im), "lstep": st(lstep),
                    "bre": bre, "bim": bim, "cre": cre, "cim": cim,
                    "dsk": np.ascontiguousarray(np.asarray(d_skip, np.float32)[ci * 256:(ci + 1) * 256].reshape(2, 128).T)})
    r2 = _run(_prog("s5_scan", build_s5_scan, c), ins)
    yT = np.concatenate([r2[ci]["yT"] for ci in range(ncs)], 0)
    wo = _tile_w(w_out, D, 2 * D)
    r3 = _run(_prog("s5_out", build_s5_out, c), [{"yT": _sl(yT, i, TC), "xT": xT[i], "w": wo} for i in range(NC_)])
    return [r3[i]["xo"] for i in range(NC_)]


def _rwkv(c, xT, g, Q):
    D, S, TC, KC, NC_ = c.D, c.S, c.TC, c.KC, c.NCORE
    bd = (np.arange(128)[:, None] // 64 == np.arange(128)[None, :] // 64).astype(np.float32)
    common = {"g": _pc(g), "mu": np.ascontiguousarray(np.asarray(Q['mu'], np.float32).reshape(6, KC, 128).transpose(2, 0, 1)),
              "wr": _tile_w(Q['w_rkv'][0], D, D), "wk": _tile_w(Q['w_rkv'][1], D, D), "wv": _tile_w(Q['w_rkv'][2], D, D),
              "w1": _tile_w(Q['w1'], D, Q['w1'].shape[1]), "a1": _tile_w(Q['a1'], D, Q['a1'].shape[1]), "g1": _tile_w(Q['g1'], D, Q['g1'].shape[1]),
              "w2": _tile_w(Q['w2'], Q['w2'].shape[0], D), "a2": _tile_w(Q['a2'], Q['a2'].shape[0], D), "g2": _tile_w(Q['g2'], Q['g2'].shape[0], D),
              "vecs": np.ascontiguousarray(np.stack([Q['w0'], Q['a0'], Q['k_k'], Q['k_a']]).astype(np.float32).reshape(4, KC, 128).transpose(2, 0, 1)),
              "bd": bd}
    ins = []
    for i in range(NC_):
        xe = np.zeros((D, TC + 1), np.float32)
        xe[:, 1:] = xT[i]
        if i > 0:
            xe[:, 0] = xT[i - 1][:, -1]
        d = dict(common)
        d["xe"] = xe
        ins.append(d)
    r1 = _run(_prog("rwkv_pre", build_rwkv_pre, c), ins)
    X = {n: _cat(r1, n) for n in ("rT", "kT", "vT", "gT")}
    XS = {n: np.concatenate([r[n] for r in r1], 2) for n in ("rS", "wS", "kS", "aS", "bS")}
    del r1
    H = D // 64
    ncs = H // 4
    sel = np.zeros((2, 12, 128), np.float32)
    for gi in range(2):
        for m in range(128):
            for part in range(3):
                sel[gi, part * 4 + gi * 2 + m // 64, m] = 1
    sel = sel.astype(XS["rS"].dtype)
    ins2 = []
    for ci in range(ncs):
        d = {"sel": sel}
        for n, key in (("w", "wS"), ("k", "kS"), ("a", "aS"), ("b", "bS"), ("r", "rS")):
            blk = XS[key][:, ci * 256:(ci + 1) * 256].reshape(3, 4, 64, S)
            d[n + "h"] = np.ascontiguousarray(blk.transpose(0, 1, 3, 2)).reshape(12, S * 64)
        vb = X["vT"][ci * 256:(ci + 1) * 256].reshape(2, 2, 64, S)
        d["vv"] = np.ascontiguousarray(vb.transpose(1, 2, 3, 0)).reshape(128, S, 2)
        ins2.append(d)
    r2 = _run(_prog("rwkv_scan", build_rwkv_scan, c), ins2)
    yT = np.concatenate([np.ascontiguousarray(r2[ci]["yy"].reshape(2, 64, S, 2).transpose(3, 0, 1, 2)).reshape(256, S) for ci in range(ncs)], 0)
    del r2, ins2, XS
    vecs = np.ascontiguousarray(np.stack([Q['ln_w'], Q['ln_b'], np.asarray(Q['r_k']).reshape(-1)]).astype(np.float32).reshape(3, KC, 128).transpose(2, 0, 1))
    wo = _tile_w(Q['w_out'], D, D)
    ins3 = [{"yT": _sl(yT, i, TC), "rT": _sl(X["rT"], i, TC), "kT": _sl(X["kT"], i, TC), "vT": _sl(X["vT"], i, TC), "gT": _sl(X["gT"], i, TC),
             "xT": xT[i], "vecs": vecs, "bd": bd, "w": wo} for i in range(NC_)]
    r3 = _run(_prog("rwkv_post", build_rwkv_post, c), ins3)
    return [r3[i]["xo"] for i in range(NC_)]


def _forward(c, inp, depth):
    x = np.asarray(inp['x'], np.float32)[0]
    TC, NC_ = c.TC, c.NCORE
    xT = [np.ascontiguousarray(x[i * TC:(i + 1) * TC].T) for i in range(NC_)]
    ia = ib = ic = 0
    for l in range(depth):
        xT = _ffn(c, xT, inp['norm_w'][l, 0], inp['ffn_w_up'][l, 0], inp['ffn_w_down'][l, 0])
        g = inp['norm_w'][l, 1]
        m = l % 3
        if m == 0:
            xT = _fox(c, xT, g, inp['fox_w_in'][ia], inp['fox_b_f'][ia], inp['fox_qk_gain'][ia], inp['fox_w_out'][ia])
            ia += 1
        elif m == 1:
            Q = {k[5:]: np.asarray(inp[k][ib]) for k in inp if k.startswith('rwkv_')}
            xT = _rwkv(c, xT, g, Q)
            ib += 1
        else:
            xT = _s5(c, xT, g, inp['s5_w_in'][ic], inp['s5_lam_re'][ic], inp['s5_lam_im'][ic], inp['s5_log_step'][ic],
                     inp['s5_b_re'][ic], inp['s5_b_im'][ic], inp['s5_c_re'][ic], inp['s5_c_im'][ic], inp['s5_d'][ic], inp['s5_w_out'][ic])
            ic += 1
        xT = _ffn(c, xT, inp['norm_w'][l, 2], inp['ffn_w_up'][l, 1], inp['ffn_w_down'][l, 1])
    rf = _run(_prog("final_norm", build_final_norm, c), [{"xT": xT[i], "g": _pc(inp['final_norm'])} for i in range(NC_)])
    out = np.concatenate([rf[i]["o"].T for i in range(NC_)], 0)
    return np.ascontiguousarray(out[None]).astype(np.float32)


def kernel(**inputs):
    inp = {k: np.asarray(v) for k, v in inputs.items()}
    D = inp['x'].shape[-1]
    S = inp['x'].shape[1]
    c = CFG(D=D, S=S)
    return _forward(c, inp, inp['norm_w'].shape[0])
```

```python
import numpy as np
import concourse.bass as bass
import concourse.mybir as mybir
from concourse.bass_utils import run_bass_kernel_spmd

F32 = mybir.dt.float32
BF16 = mybir.dt.bfloat16
AF = mybir.ActivationFunctionType
ALU = mybir.AluOpType
AX = mybir.AxisListType

ENGS = ("pe", "act", "dve", "pool", "sp")
EIDX = {e: i for i, e in enumerate(ENGS)}


class Op:
    __slots__ = ("eng", "fn", "idx", "dma", "waits", "sig", "sem", "target", "know", "dmaknow", "deps")

    def __init__(self, eng, fn, dma):
        self.eng = eng
        self.fn = fn
        self.dma = dma
        self.idx = -1
        self.waits = []
        self.sig = False
        self.sem = None
        self.target = 0
        self.know = None


class Sched:
    NDMA = 12

    def __init__(self, nc):
        self.nc = nc
        self.eng_ops = {e: [] for e in ENGS}
        self.last_w = {}
        self.readers = {}
        self.know = {e: [-1] * len(ENGS) for e in ENGS}
        self.dmaknow = {e: set() for e in ENGS}
        self.dma_slots = {e: [None] * self.NDMA for e in ENGS}
        self.dma_cnt = {e: 0 for e in ENGS}
        self.dma_slot_uses = {e: [0] * self.NDMA for e in ENGS}
        self.nops = 0

    def add(self, eng, fn, r=(), w=(), dma=False):
        op = Op(eng, fn, dma)
        ops = self.eng_ops[eng]
        op.idx = len(ops)
        deps = []
        for b in r:
            x = self.last_w.get(b)
            if x is not None:
                deps.append(x)
        for b in w:
            x = self.last_w.get(b)
            if x is not None:
                deps.append(x)
            deps.extend(self.readers.get(b, ()))
        if dma:
            k = self.dma_cnt[eng]
            slot = k % self.NDMA
            prev = self.dma_slots[eng][slot]
            if prev is not None:
                deps.append(prev)
            self.dma_slots[eng][slot] = op
            self.dma_cnt[eng] = k + 1
            self.dma_slot_uses[eng][slot] += 1
            op.sem = (eng, slot)
            op.target = 16 * self.dma_slot_uses[eng][slot]
            op.sig = True
        know = self.know[eng]
        dk = self.dmaknow[eng]
        best = {}
        for d in deps:
            if d is op:
                continue
            if d.dma:
                if id(d) in dk:
                    continue
                best[("dma", id(d))] = d
            else:
                if d.eng == "pe" and eng == "pe" and not dma:
                    continue
                if know[EIDX[d.eng]] >= d.idx:
                    continue
                cur = best.get(d.eng)
                if cur is None or cur.idx < d.idx:
                    best[d.eng] = d
        for key, d in best.items():
            if d.dma:
                op.waits.append(d)
                dk.add(id(d))
            else:
                if know[EIDX[d.eng]] >= d.idx:
                    continue
                op.waits.append(d)
                d.sig = True
                dkv = d.know
                for i in range(len(ENGS)):
                    if dkv[i] > know[i]:
                        know[i] = dkv[i]
                if know[EIDX[d.eng]] < d.idx:
                    know[EIDX[d.eng]] = d.idx
        op.know = list(know)
        for b in r:
            self.readers.setdefault(b, []).append(op)
        for b in w:
            self.last_w[b] = op
            self.readers[b] = []
        ops.append(op)
        self.nops += 1
        if len(dk) > 4096:
            dk.clear()
        return op

    def emit(self, final_wait_ops=()):
        nc = self.nc
        from contextlib import ExitStack
        with ExitStack() as es:
            esem = {e: es.enter_context(nc.semaphore("s_" + e)) for e in ENGS}
            dsem = {}
            for e in ENGS:
                if self.dma_cnt[e] > 0:
                    for s in range(min(self.NDMA, self.dma_cnt[e])):
                        dsem[(e, s)] = es.enter_context(nc.semaphore("d_%s_%d" % (e, s)))
            for e in ENGS:
                c = 0
                for op in self.eng_ops[e]:
                    if op.dma:
                        continue
                    if op.sig:
                        c += 1
                        op.target = c
                        op.sem = e
            block = es.enter_context(nc.Block())

            def run(e, eng):
                for op in self.eng_ops[e]:
                    for d in op.waits:
                        if d.dma:
                            eng.wait_ge(dsem[d.sem], d.target)
                        else:
                            eng.wait_ge(esem[d.sem], d.target)
                    ins = op.fn(eng)
                    if op.dma:
                        ins.then_inc(dsem[op.sem], 16)
                    elif op.sig:
                        ins.then_inc(esem[e], 1)
                if e == "sp":
                    fw = list(final_wait_ops)
                    for q in ENGS:
                        for d in self.dma_slots[q]:
                            if d is not None:
                                fw.append(d)
                    for d in fw:
                        if d.dma:
                            eng.wait_ge(dsem[d.sem], d.target)
                        else:
                            eng.wait_ge(esem[d.sem], d.target)

            @block.tensor
            def _(eng):
                run("pe", eng)

            @block.scalar
            def _(eng):
                run("act", eng)

            @block.vector
            def _(eng):
                run("dve", eng)

            @block.gpsimd
            def _(eng):
                run("pool", eng)

            @block.sync
            def _(eng):
                run("sp", eng)


from contextlib import ExitStack
import math


class CFG:
    def __init__(self, D=2048, S=16384, NCORE=8):
        self.D = D
        self.S = S
        self.NCORE = NCORE
        self.F = ((int(2 * 4 * D / 3) + 255) // 256) * 256
        self.TC = S // NCORE
        self.TT = min(512, self.TC)
        self.NTT = self.TC // self.TT
        self.KC = D // 128
        self.FH = D // 128
        self.FOXIN = 4 * D + self.FH


def _alloc(nc, es, name, shape, dt):
    return es.enter_context(nc.sbuf_tensor(name, shape, dt))


class P:
    def __init__(self):
        self.nc = bass.Bass("TRN2", target_bir_lowering=False)
        self.es = ExitStack()
        self.S = Sched(self.nc)
        self.banks = [self.es.enter_context(self.nc.psum_tensor("bank%d" % i, [128, 512], F32)) for i in range(8)]
        self.nbank = 0
        self.cnt = 0

    def inp(self, name, shape, dt=F32):
        return self.nc.dram_tensor(name, list(shape), dt, kind="ExternalInput").ap()

    def outp(self, name, shape, dt=F32):
        return self.nc.dram_tensor(name, list(shape), dt, kind="ExternalOutput").ap()

    def sb(self, name, shape, dt=F32):
        return _alloc(self.nc, self.es, name, list(shape), dt)

    def add(self, *a, **k):
        return self.S.add(*a, **k)

    def dma(self, q, out, in_, r=(), w=()):
        return self.S.add(q, lambda e: e.dma_start(out=out, in_=in_), r=r, w=w, dma=True)

    def finish(self):
        self.S.emit()
        self.es.close()
        return self.nc

    def const(self, name, val, shape=(128, 1)):
        t = self.sb(name, shape)
        self.add("pool", lambda e: e.memset(t[:], val), w=[name])
        return t


def emit_rstd(p, xs, xs_name, KC, width, ones, rstd, rstd_name, epst, eps_name, scr, D_total):
    ps = p.banks[0]
    for kc in range(KC):
        sqb = scr[kc % 2]
        sn = "scr%d" % (kc % 2)
        p.add("act", lambda e, kc=kc, sqb=sqb: e.activation(out=sqb[:, :width], in_=xs[:, kc, :width], func=AF.Square),
              r=[xs_name], w=[sn])
        p.add("pe", lambda e, kc=kc, sqb=sqb: e.matmul(ps[:, :width], lhsT=ones[:], rhs=sqb[:, :width], start=(kc == 0), stop=(kc == KC - 1)),
              r=["ones", sn], w=["bank0"])
    p.add("act", lambda e: e.activation(out=scr[0][:, :width], in_=ps[:, :width], func=AF.Sqrt, scale=1.0 / D_total, bias=epst[:]),
          r=["bank0", eps_name], w=["scr0"])
    p.add("dve", lambda e: e.reciprocal(out=rstd[:, :width], in_=scr[0][:, :width]), r=["scr0"], w=[rstd_name])


def emit_norm_bf(p, c, xT, g, abf, eps=1e-6):
    KC, TT = c.KC, c.TT
    ones = p.const("ones", 1.0, (128, 128))
    epst = p.const("epst", eps)
    gt = p.sb("gt", [128, KC])
    xs = p.sb("xs", [128, KC, TT])
    scr = [p.sb("scr%d" % i, [128, TT]) for i in range(2)]
    rstd = p.sb("rstd", [128, TT])
    p.dma("sp", gt[:], g, w=["gt"])
    xv = xT.rearrange("(c p) t -> p c t", p=128)
    for tt in range(c.NTT):
        tsl = slice(tt * TT, (tt + 1) * TT)
        p.dma("sp", xs[:], xv[:, :, tsl], w=["xs"])
        emit_rstd(p, xs, "xs", KC, TT, ones, rstd, "rstd", epst, "epst", scr, c.D)
        for kc in range(KC):
            p.add("dve", lambda e, kc=kc, tsl=tsl: e.scalar_tensor_tensor(
                out=abf[:, kc, tsl], in0=xs[:, kc, :], scalar=gt[:, kc:kc + 1], in1=rstd[:],
                op0=ALU.mult, op1=ALU.mult), r=["xs", "gt", "rstd"], w=["abf"])


def emit_gemm(p, c, w, wname, KC, abf, abf_name, MC, epi, t0=0, ntt=None, nbanks=8, krows=None, mrows=None, NW=3, after=None):
    TT = c.TT
    ntt = c.NTT if ntt is None else ntt
    wb = [p.sb("%s_b%d" % (wname, i), [128, KC * 128], BF16) for i in range(NW)]
    for j in range(MC):
        wbj = wb[j % NW]
        wn = "%s_b%d" % (wname, j % NW)
        p.dma("pool", wbj[:], w[j], w=[wn])
        ms = 128 if (mrows is None or j < MC - 1) else mrows
        for kc in range(KC):
            ks = 128 if (krows is None or kc < KC - 1) else krows
            for tt in range(ntt):
                bk = p.nbank_of(j, tt, ntt, nbanks)
                p.add("pe", lambda e, kc=kc, tt=tt, bk=bk, wbj=wbj, ks=ks, ms=ms: e.matmul(
                    p.banks[bk][:ms, :TT], lhsT=wbj[:ks, kc * 128:kc * 128 + ms],
                    rhs=abf[:ks, kc, t0 + tt * TT:t0 + (tt + 1) * TT], start=(kc == 0), stop=(kc == KC - 1)),
                    r=[wn, abf_name], w=["bank%d" % bk])
        for tt in range(ntt):
            bk = p.nbank_of(j, tt, ntt, nbanks)
            epi(j, tt, p.banks[bk], "bank%d" % bk, ms)


def _nbank_of(self, j, tt, ntt, nbanks):
    return (j * ntt + tt) % nbanks


P.nbank_of = _nbank_of


def build_ffn_up(c):
    p = P()
    D, F, TC, TT, KC = c.D, c.F, c.TC, c.TT, c.KC
    xT = p.inp("xT", [D, TC])
    g = p.inp("g", [128, KC])
    w = p.inp("w", [2 * F // 128, 128, KC * 128])
    out = p.outp("act", [F, TC], BF16)
    abf = p.sb("abf", [128, KC, TC], BF16)
    emit_norm_bf(p, c, xT, g, abf)
    NW = 3
    wb = [p.sb("wb%d" % i, [128, 2, KC * 128], BF16) for i in range(NW)]
    sg = [p.sb("sg%d" % i, [128, TT]) for i in range(2)]
    ob = [p.sb("ob%d" % i, [128, TC], BF16) for i in range(2)]
    FC = F // 128
    NTT = c.NTT
    for j in range(FC):
        wbj = wb[j % NW]
        wn = "wb%d" % (j % NW)
        p.dma("pool", wbj[:, 0, :], w[j], w=[wn + "g"])
        p.dma("pool", wbj[:, 1, :], w[FC + j], w=[wn + "u"])
        for half in range(2):
            for kc in range(KC):
                for tt in range(NTT):
                    bk = half * 4 + tt
                    p.add("pe", lambda e, half=half, kc=kc, tt=tt, bk=bk, wbj=wbj: e.matmul(
                        p.banks[bk][:, :TT], lhsT=wbj[:, half, kc * 128:(kc + 1) * 128],
                        rhs=abf[:, kc, tt * TT:(tt + 1) * TT], start=(kc == 0), stop=(kc == KC - 1)),
                        r=[wn + "gu"[half], "abf"], w=["bank%d" % bk])
        obj = ob[j % 2]
        on = "ob%d" % (j % 2)
        for tt in range(NTT):
            sgb = sg[tt % 2]
            sn = "sg%d" % (tt % 2)
            p.add("act", lambda e, tt=tt, sgb=sgb: e.activation(out=sgb[:], in_=p.banks[tt][:, :TT], func=AF.Silu),
                  r=["bank%d" % tt], w=[sn])
            p.add("dve", lambda e, tt=tt, sgb=sgb, obj=obj: e.tensor_tensor(
                out=obj[:, tt * TT:(tt + 1) * TT], in0=sgb[:], in1=p.banks[4 + tt][:, :TT], op=ALU.mult),
                r=[sn, "bank%d" % (4 + tt)], w=[on])
        p.dma("sp", out[j * 128:(j + 1) * 128, :], obj[:], r=[on])
    return p.finish()


def emit_resid_gemm(p, c, w, KCI, abf, xT, out, scale, t0, ntt):
    TT = c.TT
    xb = [p.sb("xb%d" % i, [128, ntt * TT]) for i in range(2)]
    ob = [p.sb("rob%d" % i, [128, ntt * TT]) for i in range(2)]

    def epi(j, tt, bank, bname, ms):
        if tt == 0:
            p.dma("sp", xb[j % 2][:], xT[j * 128:(j + 1) * 128, t0:t0 + ntt * TT], w=["xb%d" % (j % 2)])
        p.add("dve", lambda e: e.scalar_tensor_tensor(
            out=ob[j % 2][:, tt * TT:(tt + 1) * TT], in0=bank[:, :TT], scalar=scale, in1=xb[j % 2][:, tt * TT:(tt + 1) * TT],
            op0=ALU.mult, op1=ALU.add), r=[bname, "xb%d" % (j % 2)], w=["rob%d" % (j % 2)])
        if tt == ntt - 1:
            p.dma("sp", out[j * 128:(j + 1) * 128, t0:t0 + ntt * TT], ob[j % 2][:], r=["rob%d" % (j % 2)])

    emit_gemm(p, c, w, "wd", KCI, abf, "abf", c.KC, epi, t0=0, ntt=ntt, nbanks=8)


def build_ffn_down(c):
    p = P()
    D, F, TC, TT = c.D, c.F, c.TC, c.TT
    FC = F // 128
    act = p.inp("act", [F, TC], BF16)
    xT = p.inp("xT", [D, TC])
    w = p.inp("w", [c.KC, 128, FC * 128])
    out = p.outp("xo", [D, TC])
    ngrp = 2 if c.NTT >= 2 else 1
    ntt = c.NTT // ngrp
    abf = p.sb("abf", [128, FC, ntt * TT], BF16)
    av = act.rearrange("(c p) t -> p c t", p=128)
    TTg = ntt * TT
    xb = [p.sb("xb%d" % i, [128, TTg]) for i in range(2)]
    ob = [p.sb("rob%d" % i, [128, TTg]) for i in range(2)]
    NW = 3
    wb = [p.sb("wd_b%d" % i, [128, FC * 128], BF16) for i in range(NW)]
    it = 0
    for gi in range(ngrp):
        t0 = gi * TTg
        p.dma("sp", abf[:], av[:, :, t0:t0 + TTg], w=["abf"])
        for j in range(c.KC):
            wbj = wb[it % NW]
            wn = "wd_b%d" % (it % NW)
            it += 1
            p.dma("pool", wbj[:], w[j], w=[wn])
            for kc in range(FC):
                for tt in range(ntt):
                    bk = (j * ntt + tt) % 8
                    p.add("pe", lambda e, kc=kc, tt=tt, bk=bk, wbj=wbj: e.matmul(
                        p.banks[bk][:, :TT], lhsT=wbj[:, kc * 128:(kc + 1) * 128],
                        rhs=abf[:, kc, tt * TT:(tt + 1) * TT], start=(kc == 0), stop=(kc == FC - 1)),
                        r=[wn, "abf"], w=["bank%d" % bk])
            xbj, obj = xb[j % 2], ob[j % 2]
            p.dma("sp", xbj[:], xT[j * 128:(j + 1) * 128, t0:t0 + TTg], w=["xb%d" % (j % 2)])
            for tt in range(ntt):
                bk = (j * ntt + tt) % 8
                p.add("dve", lambda e, tt=tt, bk=bk, xbj=xbj, obj=obj: e.scalar_tensor_tensor(
                    out=obj[:, tt * TT:(tt + 1) * TT], in0=p.banks[bk][:, :TT], scalar=0.5, in1=xbj[:, tt * TT:(tt + 1) * TT],
                    op0=ALU.mult, op1=ALU.add), r=["bank%d" % bk, "xb%d" % (j % 2)], w=["rob%d" % (j % 2)])
            p.dma("sp", out[j * 128:(j + 1) * 128, t0:t0 + TTg], obj[:], r=["rob%d" % (j % 2)])
    return p.finish()


def build_norm_gemm(c, M):
    p = P()
    D, TC, TT, KC = c.D, c.TC, c.TT, c.KC
    MC = (M + 127) // 128
    xT = p.inp("xT", [D, TC])
    g = p.inp("g", [128, KC])
    w = p.inp("w", [MC, 128, KC * 128])
    out = p.outp("o", [MC * 128, TC])
    abf = p.sb("abf", [128, KC, TC], BF16)
    emit_norm_bf(p, c, xT, g, abf)
    ob = [p.sb("sob%d" % i, [128, TC]) for i in range(2)]

    def epi(j, tt, bank, bname, ms):
        p.add("act", lambda e: e.copy(out=ob[j % 2][:, tt * TT:(tt + 1) * TT], in_=bank[:, :TT]),
              r=[bname], w=["sob%d" % (j % 2)])
        if tt == c.NTT - 1:
            p.dma("sp", out[j * 128:(j + 1) * 128, :], ob[j % 2][:], r=["sob%d" % (j % 2)])

    emit_gemm(p, c, w, "wg", KC, abf, "abf", MC, epi)
    return p.finish()


def build_fox_out(c):
    p = P()
    D, TC, TT, KC = c.D, c.TC, c.TT, c.KC
    oT = p.inp("oT", [D, TC])
    gT = p.inp("gT", [D, TC])
    xT = p.inp("xT", [D, TC])
    w = p.inp("w", [KC, 128, KC * 128])
    out = p.outp("xo", [D, TC])
    abf = p.sb("abf", [128, KC, TC], BF16)
    ot = [p.sb("ot%d" % i, [128, TC]) for i in range(2)]
    gt = [p.sb("gtt%d" % i, [128, TC]) for i in range(2)]
    for kc in range(KC):
        o_, g_ = ot[kc % 2], gt[kc % 2]
        p.dma("sp", o_[:], oT[kc * 128:(kc + 1) * 128, :], w=["ot%d" % (kc % 2)])
        p.dma("sp", g_[:], gT[kc * 128:(kc + 1) * 128, :], w=["gtt%d" % (kc % 2)])
        p.add("act", lambda e, g_=g_: e.activation(out=g_[:], in_=g_[:], func=AF.Sigmoid),
              r=["gtt%d" % (kc % 2)], w=["gtt%d" % (kc % 2)])
        p.add("dve", lambda e, kc=kc, o_=o_, g_=g_: e.tensor_tensor(out=abf[:, kc, :], in0=o_[:], in1=g_[:], op=ALU.mult),
              r=["ot%d" % (kc % 2), "gtt%d" % (kc % 2)], w=["abf"])
    emit_resid_gemm(p, c, w, KC, abf, xT, out, 1.0, 0, c.NTT)
    return p.finish()


def build_fox_attn(c, HPC):
    p = P()
    S = c.S
    NB = S // 128
    QW = min(512, S)
    NQ = S // QW
    scale = 128 ** -0.5
    qT = p.inp("qT", [HPC, 128, S])
    kT = p.inp("kT", [HPC, 128, S])
    v = p.inp("v", [HPC, 128, (S // 128) * 128])
    fl = p.inp("fl", [HPC, S])
    negb = p.inp("negb", [HPC, 128, 1])
    qg = p.inp("qg", [128, 1])
    kg = p.inp("kg", [128, 1])
    masks = p.inp("masks", [4, 128, 512], BF16)
    ustr = p.inp("ustr", [128, 128])
    ident = p.inp("ident", [128, 128])
    oT = p.outp("oT", [HPC, 128, S])
    scr_d = p.nc.dram_tensor("cscr", [HPC, 3, S], BF16, kind="Internal").ap()

    ones = p.const("ones", 1.0, (128, 128))
    onesb = p.sb("onesb", [128, 128], BF16)
    p.add("dve", lambda e: e.tensor_copy(out=onesb[:], in_=ones[:]), r=["ones"], w=["onesb"])
    epst = p.const("epst", 1e-6)
    onec = p.const("onec", 1.0)
    qgs = p.sb("qgs", [128, 1])
    kgs = p.sb("kgs", [128, 1])
    p.dma("sp", qgs[:], qg, w=["qgs"])
    p.dma("sp", kgs[:], kg, w=["kgs"])
    p.add("dve", lambda e: e.tensor_scalar(out=qgs[:], in0=qgs[:], scalar1=scale, scalar2=None, op0=ALU.mult), r=["qgs"], w=["qgs"])
    us = p.sb("us", [128, 128])
    idt = p.sb("idt", [128, 128])
    idb = p.sb("idb", [128, 128], BF16)
    p.dma("sp", us[:], ustr, w=["us"])
    p.dma("sp", idt[:], ident, w=["idt"])
    p.add("dve", lambda e: e.tensor_copy(out=idb[:], in_=idt[:]), r=["idt"], w=["idb"])
    mk = p.sb("mk", [128, 4, 512], BF16)
    p.dma("sp", mk[:], masks.rearrange("d p q -> p d q"), w=["mk"])
    sel = p.sb("sel", [96, 128], BF16)
    p.add("pool", lambda e: e.memset(sel[:], 0.0), w=["sel"])
    for r_ in (0, 32, 64):
        p.add("pool", lambda e, r_=r_: e.memset(sel[r_:r_ + 1, :], -1.0), r=[], w=["sel"])

    qbf = p.sb("qbf", [128, S], BF16)
    kbf = p.sb("kbf", [128, S], BF16)
    vbf = p.sb("vbf", [128, NB, 128], BF16)
    xs = p.sb("xs", [128, 1, QW])
    scr = [p.sb("scr%d" % i, [128, QW]) for i in range(2)]
    rstd = p.sb("rstd", [128, QW])
    flt = p.sb("flt", [128, 128])
    l1 = p.sb("l1", [128, 128])
    cw = p.sb("cw", [128, 128])
    nbt = p.sb("nbt", [128, 1])
    off = p.sb("off", [128, 1])
    cposT = p.sb("cposT", [128, NB])
    crow = p.sb("crow", [96, S], BF16)
    chs = [p.sb("chs%d" % i, [128, 128], BF16) for i in range(3)]
    rr = p.sb("rr", [128, 128])
    pt = [p.sb("pt%d" % i, [128, QW], BF16) for i in range(4)]
    rl = p.sb("rl", [128, QW])
    lacc = [[p.sb("lacc%d_%d" % (i, k), [128, QW]) for k in range(2)] for i in range(2)]
    osb = [p.sb("osb%d" % i, [128, QW]) for i in range(2)]
    p.add("pool", lambda e: e.memset(crow[:], 0.0), w=["crow"])
    LB = [2, 3, 4, 5]
    it = 0
    for h in range(HPC):
        p.dma("sp", flt[:NB, :], fl[h].rearrange("(b w) -> b w", w=128), w=["flt"])
        p.dma("sp", nbt[:], negb[h], w=["nbt"])
        p.add("act", lambda e: e.activation(out=l1[:NB, :], in_=flt[:NB, :], func=AF.Exp, scale=-1.0, bias=nbt[:NB, :]),
              r=["flt", "nbt"], w=["l1"])
        p.add("act", lambda e: e.activation(out=l1[:NB, :], in_=l1[:NB, :], func=AF.Ln, bias=onec[:NB, :]),
              r=["l1", "onec"], w=["l1"])
        p.add("dve", lambda e: e.tensor_tensor_scan(out=cw[:NB, :], data0=ones[:NB, :], data1=l1[:NB, :], initial=0.0,
                                                     op0=ALU.mult, op1=ALU.add), r=["ones", "l1"], w=["cw"])
        p.add("pe", lambda e: e.matmul(p.banks[1][:NB, 0:1], lhsT=us[:NB, :NB], rhs=cw[:NB, 127:128], start=True, stop=True),
              r=["us", "cw"], w=["bank1"])
        p.add("act", lambda e: e.copy(out=off[:NB, :], in_=p.banks[1][:NB, 0:1]), r=["bank1"], w=["off"])
        p.add("dve", lambda e: e.tensor_scalar(out=cw[:NB, :], in0=cw[:NB, :], scalar1=off[:NB, :], scalar2=None, op0=ALU.add),
              r=["cw", "off"], w=["cw"])
        p.add("pe", lambda e: e.transpose(p.banks[1][:, :NB], cw[:NB, :], idt[:NB, :NB]), r=["cw", "idt"], w=["bank1"])
        p.add("act", lambda e: e.copy(out=cposT[:], in_=p.banks[1][:, :NB]), r=["bank1"], w=["cposT"])
        p.add("dve", lambda e: e.tensor_copy(out=chs[0][:NB, :], in_=cw[:NB, :]), r=["cw"], w=["chs0"])
        p.add("dve", lambda e: e.tensor_tensor(out=rr[:NB, :], in0=cw[:NB, :], in1=chs[0][:NB, :], op=ALU.subtract), r=["cw", "chs0"], w=["rr"])
        p.add("dve", lambda e: e.tensor_copy(out=chs[1][:NB, :], in_=rr[:NB, :]), r=["rr"], w=["chs1"])
        p.add("dve", lambda e: e.tensor_tensor(out=rr[:NB, :], in0=rr[:NB, :], in1=chs[1][:NB, :], op=ALU.subtract), r=["rr", "chs1"], w=["rr"])
        p.add("dve", lambda e: e.tensor_copy(out=chs[2][:NB, :], in_=rr[:NB, :]), r=["rr"], w=["chs2"])
        for k3 in range(3):
            p.dma("sp", scr_d[h, k3].rearrange("(b w) -> b w", w=128), chs[k3][:NB, :], r=["chs%d" % k3], w=["cscr%d" % k3])
            p.dma("sp", crow[32 * k3:32 * k3 + 1, :], scr_d[h, k3:k3 + 1, :], r=["cscr%d" % k3], w=["crow"])
        for (src, dst, dn, gsb, gn) in ((qT, qbf, "qbf", qgs, "qgs"), (kT, kbf, "kbf", kgs, "kgs")):
            for qb in range(NQ):
                qs = slice(qb * QW, (qb + 1) * QW)
                p.dma("sp", xs[:, 0, :], src[h][:, qs], w=["xs"])
                emit_rstd(p, xs, "xs", 1, QW, ones, rstd, "rstd", epst, "epst", scr, 128)
                p.add("dve", lambda e, qs=qs, dst=dst, gsb=gsb: e.scalar_tensor_tensor(
                    out=dst[:, qs], in0=xs[:, 0, :], scalar=gsb[:, 0:1], in1=rstd[:], op0=ALU.mult, op1=ALU.mult),
                    r=["xs", gn, "rstd"], w=[dn])
        p.dma("pool", vbf[:], v[h].rearrange("w (b d) -> w b d", d=128), w=["vbf"])
        tiles = []
        for qb in range(NQ):
            q0 = qb * QW
            nkb = (q0 + QW) // 128
            for kb in range(nkb):
                tiles.append((qb, kb, nkb))
        LOOK = 2
        NPT = len(pt)

        def emit_logits(ti):
            qb, kb, nkb = tiles[ti]
            q0 = qb * QW
            qs = slice(q0, q0 + QW)
            lb = LB[ti % 4]
            ptb = pt[ti % NPT]
            pn = "pt%d" % (ti % NPT)
            d = kb * 128 - q0
            diag = d >= 0
            p.add("pe", lambda e: e.matmul(p.banks[lb][:, :QW], lhsT=kbf[:, kb * 128:(kb + 1) * 128], rhs=qbf[:, qs], start=True, stop=False),
                  r=["kbf", "qbf"], w=["bank%d" % lb])
            p.add("pe", lambda e: e.matmul(p.banks[lb][:, :QW], lhsT=sel[:, :], rhs=crow[:, qs], start=False, stop=(not diag)),
                  r=["sel", "crow"], w=["bank%d" % lb])
            if diag:
                p.add("pe", lambda e: e.matmul(p.banks[lb][:, :QW], lhsT=idb[:], rhs=mk[:, d // 128, :QW], start=False, stop=True),
                      r=["idb", "mk"], w=["bank%d" % lb])
            p.add("act", lambda e: e.activation(out=ptb[:], in_=p.banks[lb][:, :QW], func=AF.Exp, bias=cposT[:, kb:kb + 1]),
                  r=["bank%d" % lb, "cposT"], w=[pn])

        def emit_pv(ti):
            qb, kb, nkb = tiles[ti]
            q0 = qb * QW
            qs = slice(q0, q0 + QW)
            ptb = pt[ti % NPT]
            pn = "pt%d" % (ti % NPT)
            ob_, lb_ = (6, 7) if (qb % 2 == 0) else (0, 1)
            p.add("pe", lambda e: e.matmul(p.banks[ob_][:, :QW], lhsT=vbf[:, kb, :], rhs=ptb[:], start=(kb == 0), stop=(kb == nkb - 1)),
                  r=["vbf", pn], w=["bank%d" % ob_])
            eng_ = "pool" if kb % 2 == 0 else "dve"
            acc = lacc[qb % 2][kb % 2]
            an_ = "lacc%d_%d" % (qb % 2, kb % 2)
            if kb < 2:
                p.add(eng_, lambda e: e.tensor_copy(out=acc[:], in_=ptb[:]), r=[pn], w=[an_])
            else:
                p.add(eng_, lambda e: e.tensor_tensor(out=acc[:], in0=acc[:], in1=ptb[:], op=ALU.add), r=[pn, an_], w=[an_])
            if kb == nkb - 1:
                for half in range(2):
                    p.add("pe", lambda e, half=half: e.matmul(p.banks[lb_][:, :QW], lhsT=ones[:], rhs=lacc[qb % 2][half][:], start=(half == 0), stop=(half == 1)),
                          r=["ones", "lacc%d_%d" % (qb % 2, half)], w=["bank%d" % lb_])
                osb_ = osb[qb % 2]
                p.add("dve", lambda e: e.reciprocal(out=rl[:], in_=p.banks[lb_][:, :QW]), r=["bank%d" % lb_], w=["rl"])
                p.add("dve", lambda e: e.tensor_tensor(out=osb_[:], in0=p.banks[ob_][:, :QW], in1=rl[:], op=ALU.mult),
                      r=["bank%d" % ob_, "rl"], w=["osb%d" % (qb % 2)])
                p.dma("sp", oT[h][:, qs], osb_[:], r=["osb%d" % (qb % 2)])

        for ti in range(len(tiles) + LOOK):
            if ti < len(tiles):
                emit_logits(ti)
            if ti - LOOK >= 0:
                emit_pv(ti - LOOK)
    return p.finish()


def build_s5_scan(c):
    p = P()
    S = c.S
    LS = min(512, S)
    NSEG = S // LS
    NT = 8
    LV = int(math.log2(LS))
    uT = p.inp("uT", [256, S])
    lre = p.inp("lre", [128, NT])
    lim = p.inp("lim", [128, NT])
    lstep = p.inp("lstep", [128, NT])
    bre = p.inp("bre", [NT, 128, 128])
    bim = p.inp("bim", [NT, 128, 128])
    cre = p.inp("cre", [NT, 128, 128])
    cim = p.inp("cim", [NT, 128, 128])
    dsk = p.inp("dsk", [128, 2])
    yT = p.outp("yT", [256, S])

    def small(name):
        return p.sb(name, [128, NT])

    def tt(out, a, b, op, eng="dve", r=(), w=()):
        p.add(eng, lambda e: e.tensor_tensor(out=out, in0=a, in1=b, op=op), r=r, w=w)

    lr, li, dt_, rho, th = small("lr"), small("li"), small("dt"), small("rho"), small("th")
    p.dma("sp", lr[:], lre, w=["lr"])
    p.dma("sp", li[:], lim, w=["li"])
    p.dma("sp", dt_[:], lstep, w=["dt"])
    dskt = p.sb("dskt", [128, 2])
    p.dma("sp", dskt[:], dsk, w=["dskt"])
    halfpi = p.const("halfpi", math.pi / 2)
    p.add("dve", lambda e: e.tensor_scalar(out=lr[:], in0=lr[:], scalar1=-1e-4, scalar2=None, op0=ALU.min), r=["lr"], w=["lr"])
    p.add("act", lambda e: e.activation(out=dt_[:], in_=dt_[:], func=AF.Exp), r=["dt"], w=["dt"])
    tt(rho[:], lr[:], dt_[:], ALU.mult, r=["lr", "dt"], w=["rho"])
    p.add("act", lambda e: e.activation(out=rho[:], in_=rho[:], func=AF.Exp), r=["rho"], w=["rho"])
    tt(th[:], li[:], dt_[:], ALU.mult, r=["li", "dt"], w=["th"])
    cr = [small("cr%d" % k) for k in range(LV + 1)]
    ci = [small("ci%d" % k) for k in range(LV + 1)]
    a_, b_ = small("tmpa"), small("tmpb")
    p.add("act", lambda e: e.activation(out=b_[:], in_=th[:], func=AF.Sin, scale=1.0 / 16), r=["th"], w=["tmpb"])
    p.add("act", lambda e: e.activation(out=a_[:], in_=th[:], func=AF.Sin, scale=-1.0 / 16, bias=halfpi[:]), r=["th", "halfpi"], w=["tmpa"])

    def csq(or_, oi_, ir_, ii_, names):
        t1, t2 = small("sq1_" + names), small("sq2_" + names)
        tt(t1[:], ir_[:], ir_[:], ALU.mult, r=[names + "ir"], w=[names + "t1"])
        tt(t2[:], ii_[:], ii_[:], ALU.mult, r=[names + "ii"], w=[names + "t2"])
        p.add("dve", lambda e: e.scalar_tensor_tensor(out=oi_[:], in0=ir_[:], scalar=2.0, in1=ii_[:], op0=ALU.mult, op1=ALU.mult),
              r=[names + "ir", names + "ii"], w=[names + "oi"])
        tt(or_[:], t1[:], t2[:], ALU.subtract, r=[names + "t1", names + "t2"], w=[names + "or"])

    chain = [(a_, b_)] + [(small("qa%d" % i), small("qb%d" % i)) for i in range(3)] + [(cr[0], ci[0])]
    for i in range(4):
        ir_, ii_ = chain[i]
        or_, oi_ = chain[i + 1]
        t1, t2 = small("s1_%d" % i), small("s2_%d" % i)
        tt(t1[:], ir_[:], ir_[:], ALU.mult, r=["tmpa", "tmpb", "prel"], w=["prel"])
        tt(t2[:], ii_[:], ii_[:], ALU.mult, r=["prel"], w=["prel"])
        p.add("dve", lambda e, oi_=oi_, ir_=ir_, ii_=ii_: e.scalar_tensor_tensor(out=oi_[:], in0=ir_[:], scalar=2.0, in1=ii_[:], op0=ALU.mult, op1=ALU.mult),
              r=["prel"], w=["prel"])
        tt(or_[:], t1[:], t2[:], ALU.subtract, r=["prel"], w=["prel"])
    for k in range(LV):
        t1, t2 = small("l1_%d" % k), small("l2_%d" % k)
        tt(t1[:], cr[k][:], cr[k][:], ALU.mult, r=["prel"], w=["prel"])
        tt(t2[:], ci[k][:], ci[k][:], ALU.mult, r=["prel"], w=["prel"])
        p.add("dve", lambda e, k=k: e.scalar_tensor_tensor(out=ci[k + 1][:], in0=cr[k][:], scalar=2.0, in1=ci[k][:], op0=ALU.mult, op1=ALU.mult),
              r=["prel"], w=["prel"])
        tt(cr[k + 1][:], t1[:], t2[:], ALU.subtract, r=["prel"], w=["prel"])
    den, nr, ni, qre, qim, t3 = small("den"), small("nr"), small("ni"), small("qre"), small("qim"), small("t3")
    tt(den[:], lr[:], lr[:], ALU.mult, r=["lr"], w=["prel"])
    tt(t3[:], li[:], li[:], ALU.mult, r=["li"], w=["prel"])
    tt(den[:], den[:], t3[:], ALU.add, r=["prel"], w=["prel"])
    p.add("dve", lambda e: e.reciprocal(out=den[:], in_=den[:]), r=["prel"], w=["prel"])
    tt(nr[:], rho[:], cr[0][:], ALU.mult, r=["rho", "prel"], w=["prel"])
    p.add("dve", lambda e: e.tensor_scalar(out=nr[:], in0=nr[:], scalar1=-1.0, scalar2=None, op0=ALU.add), r=["prel"], w=["prel"])
    tt(ni[:], rho[:], ci[0][:], ALU.mult, r=["rho", "prel"], w=["prel"])
    tt(qre[:], nr[:], lr[:], ALU.mult, r=["prel", "lr"], w=["prel"])
    tt(t3[:], ni[:], li[:], ALU.mult, r=["prel", "li"], w=["prel"])
    tt(qre[:], qre[:], t3[:], ALU.add, r=["prel"], w=["prel"])
    tt(qre[:], qre[:], den[:], ALU.mult, r=["prel"], w=["prel"])
    tt(qim[:], ni[:], lr[:], ALU.mult, r=["prel", "lr"], w=["prel"])
    tt(t3[:], nr[:], li[:], ALU.mult, r=["prel", "li"], w=["prel"])
    tt(qim[:], qim[:], t3[:], ALU.subtract, r=["prel"], w=["prel"])
    tt(qim[:], qim[:], den[:], ALU.mult, r=["prel"], w=["prel"])
    Er = p.sb("Er", [128, NT, LS])
    Ei = p.sb("Ei", [128, NT, LS])
    Fr = p.sb("Fr", [128, NT, LS])
    Fi = p.sb("Fi", [128, NT, LS])
    tmpL = p.sb("tmpL", [128, LS])
    p.add("pool", lambda e: e.memset(Er[:, :, 0:1], 1.0), w=["E"])
    p.add("pool", lambda e: e.memset(Ei[:, :, 0:1], 0.0), w=["E"])
    for j in range(NT):
        for k in range(LV):
            n = 1 << k
            crk, cik = cr[k][:, j:j + 1], ci[k][:, j:j + 1]
            p.add("dve", lambda e, j=j, n=n, cik=cik: e.tensor_scalar(out=tmpL[:, :n], in0=Ei[:, j, 0:n], scalar1=cik, scalar2=None, op0=ALU.mult),
                  r=["E", "prel"], w=["tmpL"])
            p.add("dve", lambda e, j=j, n=n, crk=crk: e.scalar_tensor_tensor(out=Er[:, j, n:2 * n], in0=Er[:, j, 0:n], scalar=crk, in1=tmpL[:, :n],
                                                                              op0=ALU.mult, op1=ALU.subtract), r=["E", "prel", "tmpL"], w=["Ern"])
            p.add("dve", lambda e, j=j, n=n, cik=cik: e.tensor_scalar(out=tmpL[:, :n], in0=Er[:, j, 0:n], scalar1=cik, scalar2=None, op0=ALU.mult),
                  r=["E", "prel", "Ern"], w=["tmpL"])
            p.add("dve", lambda e, j=j, n=n, crk=crk: e.scalar_tensor_tensor(out=Ei[:, j, n:2 * n], in0=Ei[:, j, 0:n], scalar=crk, in1=tmpL[:, :n],
                                                                              op0=ALU.mult, op1=ALU.add), r=["E", "prel", "tmpL"], w=["E"])
            p.add("dve", lambda e: e.engine_nop(), r=["Ern"], w=["E"]) if False else None
        qr_, qi_ = qre[:, j:j + 1], qim[:, j:j + 1]
        p.add("dve", lambda e, j=j, qi_=qi_: e.tensor_scalar(out=tmpL[:], in0=Ei[:, j, :], scalar1=qi_, scalar2=None, op0=ALU.mult),
              r=["E", "Ern", "prel"], w=["tmpL"])
        p.add("dve", lambda e, j=j, qr_=qr_: e.scalar_tensor_tensor(out=Fr[:, j, :], in0=Er[:, j, :], scalar=qr_, in1=tmpL[:], op0=ALU.mult, op1=ALU.add),
              r=["E", "Ern", "prel", "tmpL"], w=["F"])
        p.add("dve", lambda e, j=j, qr_=qr_: e.tensor_scalar(out=tmpL[:], in0=Ei[:, j, :], scalar1=qr_, scalar2=None, op0=ALU.mult),
              r=["E", "Ern", "prel", "F"], w=["tmpL"])
        p.add("dve", lambda e, j=j, qi_=qi_: e.scalar_tensor_tensor(out=Fi[:, j, :], in0=Er[:, j, :], scalar=qi_, in1=tmpL[:], op0=ALU.mult, op1=ALU.subtract),
              r=["E", "Ern", "prel", "tmpL"], w=["F"])
    bre_b = p.sb("bre_b", [128, NT, 128], BF16)
    bim_b = p.sb("bim_b", [128, NT, 128], BF16)
    cre_b = p.sb("cre_b", [128, NT, 128], BF16)
    cim_b = p.sb("cim_b", [128, NT, 128], BF16)
    for (dst, src, n_) in ((bre_b, bre, "bre_b"), (bim_b, bim, "bim_b"), (cre_b, cre, "cre_b"), (cim_b, cim, "cim_b")):
        p.dma("pool", dst[:], src.rearrange("j k m -> k j m"), w=[n_])
    ir_ = small("init_r")
    ii_ = small("init_i")
    p.add("pool", lambda e: e.memset(ir_[:], 0.0), w=["init"])
    p.add("pool", lambda e: e.memset(ii_[:], 0.0), w=["init"])
    ubf = [p.sb("ubf%d" % i, [128, 2, LS], BF16) for i in range(2)]
    uf = [p.sb("uf%d" % i, [128, 2, LS]) for i in range(2)]
    W = {}
    for nm in ("t1", "t2", "xr", "xi", "hr", "hi", "a", "b", "c", "d"):
        W[nm] = [p.sb("w_%s%d" % (nm, i), [128, LS]) for i in range(2)]
    hrb = [p.sb("hrb%d" % i, [128, LS], BF16) for i in range(2)]
    hib = [p.sb("hib%d" % i, [128, LS], BF16) for i in range(2)]
    yt = p.sb("yt", [128, LS])
    y2 = p.sb("y2", [128, LS])
    yo = [p.sb("yo%d" % i, [128, LS]) for i in range(2)]
    ELr, ELi = cr[LV], ci[LV]
    sc1s = [small("sc1_%d" % i) for i in range(2)]
    uv = uT.rearrange("(c p) t -> p c t", p=128)
    it = 0
    for sg in range(NSEG):
        ts = slice(sg * LS, (sg + 1) * LS)
        ub, uf_ = ubf[sg % 2], uf[sg % 2]
        ubn, ufn = "ubf%d" % (sg % 2), "uf%d" % (sg % 2)
        p.dma("pool", ub[:], uv[:, :, ts], w=[ubn])
        p.dma("sp", uf_[:], uv[:, :, ts], w=[ufn])
        for j in range(NT):
            cc = j // 4
            k_ = it % 2
            it += 1
            bA, bB = 2 + 2 * k_, 3 + 2 * k_
            nA, nB = "bank%d" % bA, "bank%d" % bB
            g = lambda nm: (W[nm][k_], "w_%s%d" % (nm, k_))
            p.add("pe", lambda e, j=j, cc=cc, bA=bA, ub=ub: e.matmul(p.banks[bA][:, :LS], lhsT=bre_b[:, j, :], rhs=ub[:, cc, :], start=True, stop=True),
                  r=["bre_b", ubn], w=[nA])
            p.add("pe", lambda e, j=j, cc=cc, bB=bB, ub=ub: e.matmul(p.banks[bB][:, :LS], lhsT=bim_b[:, j, :], rhs=ub[:, cc, :], start=True, stop=True),
                  r=["bim_b", ubn], w=[nB])
            (t1, t1n), (t2, t2n), (xr, xrn), (xi, xin) = g("t1"), g("t2"), g("xr"), g("xi")
            (hr, hrn), (hi, hin), (a, an), (b, bn), (c_, cn), (d_, dn) = g("hr"), g("hi"), g("a"), g("b"), g("c"), g("d")
            tt(t1[:], Fi[:, j, :], p.banks[bB][:, :LS], ALU.mult, r=["F", nB], w=[t1n])
            tt(xr[:], Fr[:, j, :], p.banks[bA][:, :LS], ALU.mult, r=["F", nA], w=[xrn])
            tt(t2[:], Fi[:, j, :], p.banks[bA][:, :LS], ALU.mult, r=["F", nA], w=[t2n])
            tt(xi[:], Fr[:, j, :], p.banks[bB][:, :LS], ALU.mult, r=["F", nB], w=[xin])
            tt(xr[:], xr[:], t1[:], ALU.subtract, eng="pool", r=[xrn, t1n], w=[xrn])
            tt(xi[:], xi[:], t2[:], ALU.add, eng="pool", r=[xin, t2n], w=[xin])
            rb = rho[:, j:j + 1].to_broadcast([128, LS])
            p.add("dve", lambda e, j=j, rb=rb, hr=hr, xr=xr: e.tensor_tensor_scan(out=hr[:], data0=rb, data1=xr[:], initial=ir_[:, j:j + 1],
                                                                                 op0=ALU.mult, op1=ALU.add), r=["rho", xrn, "init"], w=[hrn])
            p.add("dve", lambda e, j=j, rb=rb, hi=hi, xi=xi: e.tensor_tensor_scan(out=hi[:], data0=rb, data1=xi[:], initial=ii_[:, j:j + 1],
                                                                                 op0=ALU.mult, op1=ALU.add), r=["rho", xin, "init"], w=[hin])
            sc1 = sc1s[k_]
            p.add("dve", lambda e, j=j, hi=hi, sc1=sc1: e.tensor_scalar(out=sc1[:, 0:1], in0=hi[:, LS - 1:LS], scalar1=ELi[:, j:j + 1], scalar2=None, op0=ALU.mult),
                  r=[hin, "prel"], w=["sc1_%d" % k_])
            p.add("dve", lambda e, j=j, hr=hr, sc1=sc1: e.scalar_tensor_tensor(out=ir_[:, j:j + 1], in0=hr[:, LS - 1:LS], scalar=ELr[:, j:j + 1], in1=sc1[:, 0:1],
                                                                               op0=ALU.mult, op1=ALU.subtract), r=[hrn, "prel", "sc1_%d" % k_, "init"], w=["init"])
            p.add("dve", lambda e, j=j, hr=hr, sc1=sc1: e.tensor_scalar(out=sc1[:, 1:2], in0=hr[:, LS - 1:LS], scalar1=ELi[:, j:j + 1], scalar2=None, op0=ALU.mult),
                  r=[hrn, "prel"], w=["sc1_%d" % k_])
            p.add("dve", lambda e, j=j, hi=hi, sc1=sc1: e.scalar_tensor_tensor(out=ii_[:, j:j + 1], in0=hi[:, LS - 1:LS], scalar=ELr[:, j:j + 1], in1=sc1[:, 1:2],
                                                                               op0=ALU.mult, op1=ALU.add), r=[hin, "prel", "sc1_%d" % k_, "init"], w=["init"])
            tt(a[:], Er[:, j, :], hr[:], ALU.mult, eng="pool", r=["E", "Ern", hrn], w=[an])
            tt(b[:], Ei[:, j, :], hi[:], ALU.mult, eng="pool", r=["E", "Ern", hin], w=[bn])
            tt(c_[:], Er[:, j, :], hi[:], ALU.mult, eng="dve", r=["E", "Ern", hin], w=[cn])
            tt(d_[:], Ei[:, j, :], hr[:], ALU.mult, eng="dve", r=["E", "Ern", hrn], w=[dn])
            hb_, ib_ = hrb[k_], hib[k_]
            tt(hb_[:], a[:], b[:], ALU.subtract, eng="pool", r=[an, bn], w=["hrb%d" % k_])
            p.add("dve", lambda e, c_=c_, d_=d_, ib_=ib_: e.scalar_tensor_tensor(out=ib_[:], in0=c_[:], scalar=-1.0, in1=d_[:], op0=ALU.mult, op1=ALU.subtract),
                  r=[cn, dn], w=["hib%d" % k_])
            yb = 6 + (cc % 2)
            p.add("pe", lambda e, j=j, yb=yb, hb_=hb_: e.matmul(p.banks[yb][:, :LS], lhsT=cre_b[:, j, :], rhs=hb_[:], start=(j % 4 == 0), stop=False),
                  r=["cre_b", "hrb%d" % k_], w=["bank%d" % yb])
            p.add("pe", lambda e, j=j, yb=yb, ib_=ib_: e.matmul(p.banks[yb][:, :LS], lhsT=cim_b[:, j, :], rhs=ib_[:], start=False, stop=(j % 4 == 3)),
                  r=["cim_b", "hib%d" % k_], w=["bank%d" % yb])
            if j % 4 == 3:
                yo_ = yo[cc % 2]
                yon = "yo%d" % (cc % 2)
                p.add("dve", lambda e, cc=cc, yb=yb, uf_=uf_: e.scalar_tensor_tensor(out=yt[:], in0=uf_[:, cc, :], scalar=dskt[:, cc:cc + 1], in1=p.banks[yb][:, :LS],
                                                                                    op0=ALU.mult, op1=ALU.add), r=[ufn, "dskt", "bank%d" % yb], w=["yt"])
                p.add("act", lambda e: e.activation(out=y2[:], in_=yt[:], func=AF.Square), r=["yt"], w=["y2"])
                p.add("dve", lambda e: e.tensor_scalar(out=y2[:], in0=y2[:], scalar1=0.044715, scalar2=1.0, op0=ALU.mult, op1=ALU.add), r=["y2"], w=["y2"])
                tt(y2[:], y2[:], yt[:], ALU.mult, r=["y2", "yt"], w=["y2"])
                p.add("act", lambda e: e.activation(out=y2[:], in_=y2[:], func=AF.Tanh, scale=math.sqrt(2.0 / math.pi)), r=["y2"], w=["y2"])
                p.add("dve", lambda e: e.tensor_scalar(out=y2[:], in0=y2[:], scalar1=0.5, scalar2=0.5, op0=ALU.mult, op1=ALU.add), r=["y2"], w=["y2"])
                tt(yo_[:], y2[:], yt[:], ALU.mult, r=["y2", "yt"], w=[yon])
                p.dma("sp", yT[cc * 128:(cc + 1) * 128, ts], yo_[:], r=[yon])
    return p.finish()


def build_s5_out(c):
    p = P()
    D, TC, TT, KC, NTT = c.D, c.TC, c.TT, c.KC, c.NTT
    yT = p.inp("yT", [D, TC])
    xT = p.inp("xT", [D, TC])
    w = p.inp("w", [2 * KC, 128, KC * 128])
    out = p.outp("xo", [D, TC])
    abf = p.sb("abf", [128, KC, TC], BF16)
    p.dma("pool", abf[:], yT.rearrange("(c p) t -> p c t", p=128), w=["abf"])
    NW = 3
    wb = [p.sb("wb%d" % i, [128, 2, KC * 128], BF16) for i in range(NW)]
    sg = [p.sb("sg%d" % i, [128, TT]) for i in range(2)]
    xb = [p.sb("xb%d" % i, [128, TC]) for i in range(2)]
    ob = [p.sb("ob%d" % i, [128, TC]) for i in range(2)]
    for j in range(KC):
        wbj = wb[j % NW]
        wn = "wb%d" % (j % NW)
        p.dma("pool", wbj[:, 0, :], w[j], w=[wn + "g"])
        p.dma("pool", wbj[:, 1, :], w[KC + j], w=[wn + "u"])
        for half in range(2):
            for kc in range(KC):
                for tt in range(NTT):
                    bk = half * 4 + tt
                    p.add("pe", lambda e, half=half, kc=kc, tt=tt, bk=bk, wbj=wbj: e.matmul(
                        p.banks[bk][:, :TT], lhsT=wbj[:, half, kc * 128:(kc + 1) * 128],
                        rhs=abf[:, kc, tt * TT:(tt + 1) * TT], start=(kc == 0), stop=(kc == KC - 1)),
                        r=[wn + "gu"[half], "abf"], w=["bank%d" % bk])
        obj, xbj = ob[j % 2], xb[j % 2]
        on, xn = "ob%d" % (j % 2), "xb%d" % (j % 2)
        p.dma("sp", xbj[:], xT[j * 128:(j + 1) * 128, :], w=[xn])
        for tt in range(NTT):
            sgb = sg[tt % 2]
            sn = "sg%d" % (tt % 2)
            tsl = slice(tt * TT, (tt + 1) * TT)
            p.add("act", lambda e, tt=tt, sgb=sgb: e.activation(out=sgb[:], in_=p.banks[4 + tt][:, :TT], func=AF.Sigmoid),
                  r=["bank%d" % (4 + tt)], w=[sn])
            p.add("dve", lambda e, tt=tt, sgb=sgb: e.tensor_tensor(out=sgb[:], in0=sgb[:], in1=p.banks[tt][:, :TT], op=ALU.mult),
                  r=[sn, "bank%d" % tt], w=[sn])
            p.add("pool", lambda e, sgb=sgb, obj=obj, xbj=xbj, tsl=tsl: e.tensor_tensor(out=obj[:, tsl], in0=sgb[:], in1=xbj[:, tsl], op=ALU.add),
                  r=[sn, xn], w=[on])
        p.dma("sp", out[j * 128:(j + 1) * 128, :], obj[:], r=[on])
    return p.finish()


def build_rwkv_pre(c):
    p = P()
    D, TC, KC = c.D, c.TC, c.KC
    RT = min(256, TC)
    NRT = TC // RT
    LW = max(32, int(round(1.8 * D ** 0.5 / 32)) * 32)
    LG = max(32, int(round(0.6 * D ** 0.8 / 32)) * 32)
    LGC = (LG + 127) // 128
    xe = p.inp("xe", [D, TC + 1])
    g = p.inp("g", [128, KC])
    mu = p.inp("mu", [128, 6, KC])
    wr = p.inp("wr", [KC, 128, KC * 128])
    wk = p.inp("wk", [KC, 128, KC * 128])
    wv = p.inp("wv", [KC, 128, KC * 128])
    w1 = p.inp("w1", [1, 128, KC * 128])
    a1 = p.inp("a1", [1, 128, KC * 128])
    g1 = p.inp("g1", [LGC, 128, KC * 128])
    w2 = p.inp("w2", [KC, 128, 128])
    a2 = p.inp("a2", [KC, 128, 128])
    g2 = p.inp("g2", [KC, 128, LGC * 128])
    vecs = p.inp("vecs", [128, 4, KC])
    bd = p.inp("bd", [128, 128])
    outs = {n: p.outp(n, [D, TC]) for n in ("rT", "kT", "vT", "gT")}
    souts = {n: p.outp(n, [3, D, TC], BF16) for n in ("rS", "wS", "kS", "aS", "bS")}

    ones = p.const("ones", 1.0, (128, 128))
    epst = p.const("epst", 1e-6)
    gt = p.sb("gt", [128, KC])
    mut = p.sb("mut", [128, 6, KC])
    vt = p.sb("vt", [128, 4, KC])
    bdt = p.sb("bdt", [128, 128])
    p.dma("sp", gt[:], g, w=["gt"])
    p.dma("sp", mut[:], mu, w=["mut"])
    p.dma("sp", vt[:], vecs, w=["vt"])
    p.dma("sp", bdt[:], bd, w=["bdt"])
    xs = p.sb("xs", [128, KC, RT + 1])
    hh = p.sb("hh", [128, KC, RT + 1])
    xx = p.sb("xx", [128, KC, RT])
    scr = [p.sb("scr%d" % i, [128, RT + 1]) for i in range(2)]
    rstd = p.sb("rstd", [128, RT + 1])
    xi = [p.sb("xi%d" % i, [128, KC, RT], BF16) for i in range(2)]
    kf = p.sb("kf", [128, KC, RT])
    af = p.sb("af", [128, KC, RT])
    lo = p.sb("lo", [128, LGC, RT], BF16)
    wbig = [p.sb("wbig%d" % i, [128, KC * 128], BF16) for i in range(3)]
    wsm = [p.sb("wsm%d" % i, [128, LGC * 128], BF16) for i in range(2)]
    st = [p.sb("st%d" % i, [128, RT]) for i in range(4)]
    tmp = [p.sb("tmp%d" % i, [128, RT]) for i in range(4)]
    xv = xe.rearrange("(c p) t -> p c t", p=128)
    cnt = {"w": 0, "s": 0, "st": 0, "bank": 0, "sp": 0}

    def lerp(i, buf):
        for kc in range(KC):
            p.add("dve", lambda e, kc=kc: e.scalar_tensor_tensor(out=xi[buf][:, kc, :], in0=xx[:, kc, :], scalar=mut[:, i, kc:kc + 1],
                                                                  in1=hh[:, kc, 1:RT + 1], op0=ALU.mult, op1=ALU.add),
                  r=["xx", "hh", "mut"], w=["xi%d" % buf])

    def gemm(wd, MC, KCI, rhs_of, rhs_name, epi, small=False, ks_last=128):
        for j in range(MC):
            if small:
                wb_, wn = wsm[cnt["s"] % 2], "wsm%d" % (cnt["s"] % 2)
                cnt["s"] += 1
            else:
                wb_, wn = wbig[cnt["w"] % 3], "wbig%d" % (cnt["w"] % 3)
                cnt["w"] += 1
            p.dma("pool", wb_[:, :KCI * 128], wd[j], w=[wn])
            bk = 1 + cnt["bank"] % 7
            cnt["bank"] += 1
            for kc in range(KCI):
                p.add("pe", lambda e, kc=kc, bk=bk, wb_=wb_: e.matmul(p.banks[bk][:, :RT], lhsT=wb_[:, kc * 128:(kc + 1) * 128], rhs=rhs_of(kc),
                                                                      start=(kc == 0), stop=(kc == KCI - 1)), r=[wn, rhs_name], w=["bank%d" % bk])
            epi(j, p.banks[bk], "bank%d" % bk)

    def store_epi(name, t0, split=None):
        def epi(j, bank, bn):
            s_ = st[cnt["st"] % 4]
            sn = "st%d" % (cnt["st"] % 4)
            cnt["st"] += 1
            p.add("act", lambda e: e.copy(out=s_[:], in_=bank[:, :RT]), r=[bn], w=[sn])
            p.dma("sp", outs[name][j * 128:(j + 1) * 128, t0:t0 + RT], s_[:], r=[sn])
            if split:
                split_store(split, j, t0, s_[:], sn)
        return epi

    def store(name, j, t0, src, srcname):
        p.dma("sp", outs[name][j * 128:(j + 1) * 128, t0:t0 + RT], src, r=[srcname])

    spb = [[p.sb("spb%d_%d" % (i, k), [128, RT], BF16) for k in range(3)] for i in range(2)]
    spr = [p.sb("spr%d" % i, [128, RT]) for i in range(2)]

    def split_store(name, j, t0, src, srcname):
        i = cnt["sp"] % 2
        cnt["sp"] += 1
        hi, mid, lo = spb[i]
        rr_ = spr[i]
        nh, nm, nl, nr_ = ("spb%d_%d" % (i, 0), "spb%d_%d" % (i, 1), "spb%d_%d" % (i, 2), "spr%d" % i)
        p.add("act", lambda e: e.copy(out=hi[:], in_=src), r=[srcname], w=[nh])
        p.add("dve", lambda e: e.tensor_tensor(out=rr_[:], in0=src, in1=hi[:], op=ALU.subtract), r=[srcname, nh], w=[nr_])
        p.add("act", lambda e: e.copy(out=mid[:], in_=rr_[:]), r=[nr_], w=[nm])
        p.add("dve", lambda e: e.tensor_tensor(out=rr_[:], in0=rr_[:], in1=mid[:], op=ALU.subtract), r=[nr_, nm], w=[nr_])
        p.add("act", lambda e: e.copy(out=lo[:], in_=rr_[:]), r=[nr_], w=[nl])
        for k, (t_, n_) in enumerate(((hi, nh), (mid, nm), (lo, nl))):
            p.dma("sp", souts[name][k, j * 128:(j + 1) * 128, t0:t0 + RT], t_[:], r=[n_])

    for tt in range(NRT):
        t0 = tt * RT
        p.dma("sp", xs[:], xv[:, :, t0:t0 + RT + 1], w=["xs"])
        emit_rstd(p, xs, "xs", KC, RT + 1, ones, rstd, "rstd", epst, "epst", scr, D)
        for kc in range(KC):
            p.add("dve", lambda e, kc=kc: e.scalar_tensor_tensor(out=hh[:, kc, :], in0=xs[:, kc, :], scalar=gt[:, kc:kc + 1], in1=rstd[:],
                                                                  op0=ALU.mult, op1=ALU.mult), r=["xs", "gt", "rstd"], w=["hh"])
        p.add("pool", lambda e: e.tensor_tensor(out=xx[:], in0=hh[:, :, 0:RT], in1=hh[:, :, 1:RT + 1], op=ALU.subtract), r=["hh"], w=["xx"])
        lerp(0, 0)
        gemm(wr, KC, KC, lambda kc: xi[0][:, kc, :], "xi0", store_epi("rT", t0, split="rS"))
        lerp(2, 1)

        def k_epi(j, bank, bn):
            p.add("act", lambda e: e.copy(out=kf[:, j, :], in_=bank[:, :RT]), r=[bn], w=["kf"])
        gemm(wk, KC, KC, lambda kc: xi[1][:, kc, :], "xi1", k_epi)
        lerp(3, 0)
        gemm(wv, KC, KC, lambda kc: xi[0][:, kc, :], "xi0", store_epi("vT", t0))
        lerp(1, 1)

        def w1_epi(j, bank, bn):
            p.add("act", lambda e: e.activation(out=lo[:, 0, :], in_=bank[:, :RT], func=AF.Tanh), r=[bn], w=["lo"])
        gemm(w1, 1, KC, lambda kc: xi[1][:, kc, :], "xi1", w1_epi)

        def w2_epi(j, bank, bn):
            s_ = st[cnt["st"] % 4]
            sn = "st%d" % (cnt["st"] % 4)
            cnt["st"] += 1
            p.add("act", lambda e: e.activation(out=s_[:], in_=bank[:, :RT], func=AF.Sigmoid, bias=vt[:, 0, j:j + 1]), r=[bn, "vt"], w=[sn])
            p.add("act", lambda e: e.activation(out=s_[:], in_=s_[:], func=AF.Exp, scale=-math.exp(-0.5)), r=[sn], w=[sn])
            split_store("wS", j, t0, s_[:], sn)
        gemm(w2, KC, 1, lambda kc: lo[:, 0, :], "lo", w2_epi, small=True)
        lerp(4, 0)

        def a1_epi(j, bank, bn):
            p.add("act", lambda e: e.copy(out=lo[:, 0, :], in_=bank[:, :RT]), r=[bn], w=["lo"])
        gemm(a1, 1, KC, lambda kc: xi[0][:, kc, :], "xi0", a1_epi)

        def a2_epi(j, bank, bn):
            p.add("act", lambda e: e.activation(out=af[:, j, :], in_=bank[:, :RT], func=AF.Sigmoid, bias=vt[:, 1, j:j + 1]), r=[bn, "vt"], w=["af"])
        gemm(a2, KC, 1, lambda kc: lo[:, 0, :], "lo", a2_epi, small=True)
        lerp(5, 1)

        def g1_epi(j, bank, bn):
            p.add("act", lambda e: e.activation(out=lo[:, j, :], in_=bank[:, :RT], func=AF.Sigmoid), r=[bn], w=["lo"])
        gemm(g1, LGC, KC, lambda kc: xi[1][:, kc, :], "xi1", g1_epi)
        gemm(g2, KC, LGC, lambda kc: lo[:, kc, :], "lo", store_epi("gT", t0), small=True)
        for kc in range(KC):
            kk, sq, rn, t4 = tmp
            p.add("dve", lambda e, kc=kc: e.tensor_scalar(out=kk[:], in0=kf[:, kc, :], scalar1=vt[:, 2, kc:kc + 1], scalar2=None, op0=ALU.mult),
                  r=["kf", "vt"], w=["tmp0"])
            p.add("act", lambda e: e.activation(out=sq[:], in_=kk[:], func=AF.Square), r=["tmp0"], w=["tmp1"])
            p.add("pe", lambda e: e.matmul(p.banks[0][:, :RT], lhsT=bdt[:], rhs=sq[:], start=True, stop=True), r=["bdt", "tmp1"], w=["bank0"])
            p.add("act", lambda e: e.activation(out=rn[:], in_=p.banks[0][:, :RT], func=AF.Sqrt), r=["bank0"], w=["tmp2"])
            p.add("dve", lambda e: e.tensor_scalar(out=rn[:], in0=rn[:], scalar1=1e-12, scalar2=None, op0=ALU.max), r=["tmp2"], w=["tmp2"])
            p.add("dve", lambda e: e.reciprocal(out=rn[:], in_=rn[:]), r=["tmp2"], w=["tmp2"])
            p.add("dve", lambda e: e.tensor_tensor(out=kk[:], in0=kk[:], in1=rn[:], op=ALU.mult), r=["tmp0", "tmp2"], w=["tmp0"])
            s_a = st[cnt["st"] % 4]; na = "st%d" % (cnt["st"] % 4); cnt["st"] += 1
            s_b = st[cnt["st"] % 4]; nb = "st%d" % (cnt["st"] % 4); cnt["st"] += 1
            s_k = st[cnt["st"] % 4]; nk = "st%d" % (cnt["st"] % 4); cnt["st"] += 1
            p.add("act", lambda e, s_a=s_a: e.mul(out=s_a[:], in_=kk[:], mul=-1.0), r=["tmp0"], w=[na])
            split_store("aS", kc, t0, s_a[:], na)
            p.add("pool", lambda e, kc=kc, s_b=s_b: e.tensor_tensor(out=s_b[:], in0=kk[:], in1=af[:, kc, :], op=ALU.mult), r=["tmp0", "af"], w=[nb])
            split_store("bS", kc, t0, s_b[:], nb)
            p.add("dve", lambda e, kc=kc: e.tensor_scalar(out=t4[:], in0=af[:, kc, :], scalar1=-1.0, scalar2=vt[:, 3, kc:kc + 1], op0=ALU.add, op1=ALU.mult),
                  r=["af", "vt"], w=["tmp3"])
            p.add("dve", lambda e, kc=kc, s_k=s_k: e.scalar_tensor_tensor(out=s_k[:], in0=t4[:], scalar=1.0, in1=kf[:, kc, :], op0=ALU.add, op1=ALU.mult),
                  r=["tmp3", "kf"], w=[nk])
            store("kT", kc, t0, s_k[:], nk)
            split_store("kS", kc, t0, s_k[:], nk)
    return p.finish()


def build_rwkv_scan(c):
    p = P()
    S = c.S
    CH = 8
    NCH = S // CH
    names = ("w", "k", "a", "b", "r")
    xin = {n: p.inp(n + "h", [12, S * 64], BF16) for n in names}
    vin = p.inp("vv", [128, S, 2])
    sel = p.inp("sel", [2, 12, 128], BF16)
    yout = p.outp("yy", [128, S, 2])
    selt = p.sb("selt", [12, 2, 128], BF16)
    p.dma("sp", selt[:], sel.rearrange("g h m -> h g m"), w=["selt"])
    LD = 8
    xh = {n: [p.sb("xh_%s%d" % (n, i), [12, LD * CH * 64], BF16) for i in range(2)] for n in names}
    NB = 3
    xb = {n: [p.sb("xb_%s%d" % (n, i), [128, CH, 2, 64]) for i in range(NB)] for n in names}
    kv = [p.sb("kv%d" % i, [128, CH, 2, 64]) for i in range(NB)]
    vb = [p.sb("vb%d" % i, [128, LD * CH, 2]) for i in range(2)]
    yb = [p.sb("yb%d" % i, [128, LD * CH, 2]) for i in range(2)]
    St = p.sb("St", [128, 2, 64])
    SW = p.sb("SW", [128, 2, 64])
    SWK = p.sb("SWK", [128, 2, 64])
    m = p.sb("m", [128, 2, 64])
    T = p.sb("T", [128, 2, 64])
    mrb = [p.sb("mrb%d" % i, [128, CH, 2, 64]) for i in range(NB)]
    sa = p.sb("sa", [128, 2])
    p.add("pool", lambda e: e.memset(St[:], 0.0), w=["St"])
    bankc = 0
    pending = [None]
    for ch in range(NCH):
        ld, li = divmod(ch, LD)
        if li == 0:
            for n in names:
                p.dma("sp", xh[n][ld % 2][:], xin[n][:, ld * LD * CH * 64:(ld + 1) * LD * CH * 64], w=["xh_%s%d" % (n, ld % 2)])
            p.dma("sp", vb[ld % 2][:], vin[:, ld * LD * CH:(ld + 1) * LD * CH, :], w=["vb%d" % (ld % 2)])
        b_ = ch % NB
        for n in names:
            for gi in range(2):
                bk = bankc % 8
                bankc += 1
                p.add("pe", lambda e, n=n, gi=gi, bk=bk, ld=ld, li=li: e.matmul(
                    p.banks[bk][:, :CH * 64], lhsT=selt[:, gi, :], rhs=xh[n][ld % 2][:, li * CH * 64:(li + 1) * CH * 64], start=True, stop=True),
                    r=["selt", "xh_%s%d" % (n, ld % 2)], w=["bank%d" % bk])
                p.add("act", lambda e, n=n, gi=gi, bk=bk, b_=b_: e.copy(out=xb[n][b_][:, :, gi, :],
                                                                       in_=p.banks[bk][:, :CH * 64].rearrange("p (t j) -> p t j", j=64)),
                      r=["bank%d" % bk], w=["xb_%s%d" % (n, b_)])
        vsl = vb[ld % 2][:, li * CH:(li + 1) * CH, :]
        p.add("pool", lambda e, b_=b_, vsl=vsl: e.tensor_tensor(out=kv[b_][:], in0=xb["k"][b_][:], in1=vsl.unsqueeze(3).to_broadcast([128, CH, 2, 64]), op=ALU.mult),
              r=["xb_k%d" % b_, "vb%d" % (ld % 2)], w=["kv%d" % b_])
        for s_ in range(CH):
            A, W_, B, R = (xb[n][b_][:, s_, :, :] for n in ("a", "w", "b", "r"))
            an, wn, bn, rn = ("xb_%s%d" % (n, b_) for n in ("a", "w", "b", "r"))
            p.add("dve", lambda e, A=A: e.tensor_tensor(out=m[:], in0=St[:], in1=A, op=ALU.mult), r=["St", an], w=["m"])
            p.add("pool", lambda e, W_=W_: e.tensor_tensor(out=SW[:], in0=St[:], in1=W_, op=ALU.mult), r=["St", wn], w=["SW"])
            if pending[0] is not None:
                pending[0]()
                pending[0] = None
            p.add("dve", lambda e: e.tensor_reduce(out=sa[:], in_=m[:], axis=AX.X, op=ALU.add), r=["m"], w=["sa"])
            p.add("dve", lambda e, s_=s_, b_=b_: e.tensor_tensor(out=SWK[:], in0=SW[:], in1=kv[b_][:, s_, :, :], op=ALU.add), r=["SW", "kv%d" % b_], w=["SWK"])
            p.add("dve", lambda e, B=B: e.tensor_tensor(out=T[:], in0=B, in1=sa[:].unsqueeze(2).to_broadcast([128, 2, 64]), op=ALU.mult),
                  r=[bn, "sa"], w=["T"])
            p.add("dve", lambda e: e.tensor_tensor(out=St[:], in0=SWK[:], in1=T[:], op=ALU.add), r=["SWK", "T"], w=["St"])
            def _mk(R=R, s_=s_, b_=b_, rn=rn, ld=ld, li=li):
                def f():
                    p.add("pool", lambda e: e.tensor_tensor(out=mrb[b_][:, s_, :, :], in0=St[:], in1=R, op=ALU.mult), r=["St", rn], w=["mrb%d" % b_])
                    if s_ == CH - 1:
                        p.add("dve", lambda e: e.tensor_reduce(out=yb[ld % 2][:, li * CH:(li + 1) * CH, :], in_=mrb[b_][:], axis=AX.X, op=ALU.add),
                              r=["mrb%d" % b_], w=["yb%d" % (ld % 2)])
                        if li == LD - 1:
                            p.dma("sp", yout[:, ld * LD * CH:(ld + 1) * LD * CH, :], yb[ld % 2][:], r=["yb%d" % (ld % 2)])
                return f
            pending[0] = _mk()
    if pending[0] is not None:
        pending[0]()
    return p.finish()


def build_rwkv_post(c):
    p = P()
    D, TC, TT, KC, NTT = c.D, c.TC, c.TT, c.KC, c.NTT
    ins = {n: p.inp(n, [D, TC]) for n in ("yT", "rT", "kT", "vT", "gT", "xT")}
    vecs = p.inp("vecs", [128, 3, KC])
    bd = p.inp("bd", [128, 128])
    w = p.inp("w", [KC, 128, KC * 128])
    out = p.outp("xo", [D, TC])
    vt = p.sb("vt", [128, 3, KC])
    bdt = p.sb("bdt", [128, 128])
    p.dma("sp", vt[:], vecs, w=["vt"])
    p.dma("sp", bdt[:], bd, w=["bdt"])
    lneps = p.const("lneps", 64e-5)
    abf = p.sb("abf", [128, KC, TC], BF16)
    L = {n: [p.sb("l_%s%d" % (n, i), [128, TT]) for i in range(2)] for n in ("yT", "rT", "kT", "vT", "gT")}
    t1, t2, t3 = (p.sb("pt%d" % i, [128, TT]) for i in range(3))
    it = 0
    for kc in range(KC):
        for tt in range(NTT):
            b_ = it % 2
            it += 1
            tsl = slice(tt * TT, (tt + 1) * TT)
            for n in L:
                p.dma("sp", L[n][b_][:], ins[n][kc * 128:(kc + 1) * 128, tsl], w=["l_%s%d" % (n, b_)])
            y, r_, k_, v_, g_ = (L[n][b_] for n in ("yT", "rT", "kT", "vT", "gT"))
            yn, rn, kn, vn, gn = ("l_%s%d" % (n, b_) for n in ("yT", "rT", "kT", "vT", "gT"))
            B0, B1, B2 = p.banks[0], p.banks[1], p.banks[2]
            p.add("pe", lambda e, y=y: e.matmul(B0[:, :TT], lhsT=bdt[:], rhs=y[:], start=True, stop=True), r=["bdt", yn], w=["bank0"])
            p.add("dve", lambda e, y=y: e.scalar_tensor_tensor(out=t1[:], in0=B0[:, :TT], scalar=-1.0 / 64, in1=y[:], op0=ALU.mult, op1=ALU.add),
                  r=["bank0", yn], w=["pt0"])
            p.add("act", lambda e: e.activation(out=t2[:], in_=t1[:], func=AF.Square), r=["pt0"], w=["pt1"])
            p.add("pe", lambda e: e.matmul(B1[:, :TT], lhsT=bdt[:], rhs=t2[:], start=True, stop=True), r=["bdt", "pt1"], w=["bank1"])
            p.add("act", lambda e: e.activation(out=t2[:], in_=B1[:, :TT], func=AF.Sqrt, scale=1.0 / 64, bias=lneps[:]), r=["bank1", "lneps"], w=["pt1"])
            p.add("dve", lambda e: e.reciprocal(out=t2[:], in_=t2[:]), r=["pt1"], w=["pt1"])
            p.add("dve", lambda e: e.tensor_tensor(out=t1[:], in0=t1[:], in1=t2[:], op=ALU.mult), r=["pt0", "pt1"], w=["pt0"])
            p.add("dve", lambda e, kc=kc: e.tensor_scalar(out=t1[:], in0=t1[:], scalar1=vt[:, 0, kc:kc + 1], scalar2=vt[:, 1, kc:kc + 1], op0=ALU.mult, op1=ALU.add),
                  r=["pt0", "vt"], w=["pt0"])
            p.add("dve", lambda e, kc=kc, r_=r_, k_=k_: e.scalar_tensor_tensor(out=t3[:], in0=r_[:], scalar=vt[:, 2, kc:kc + 1], in1=k_[:], op0=ALU.mult, op1=ALU.mult),
                  r=[rn, kn, "vt"], w=["pt2"])
            p.add("pe", lambda e: e.matmul(B2[:, :TT], lhsT=bdt[:], rhs=t3[:], start=True, stop=True), r=["bdt", "pt2"], w=["bank2"])
            p.add("dve", lambda e, v_=v_: e.tensor_tensor(out=t3[:], in0=B2[:, :TT], in1=v_[:], op=ALU.mult), r=["bank2", vn, "pt2"], w=["pt2"])
            p.add("pool", lambda e: e.tensor_tensor(out=t1[:], in0=t1[:], in1=t3[:], op=ALU.add), r=["pt0", "pt2"], w=["pt0"])
            p.add("dve", lambda e, kc=kc, tsl=tsl, g_=g_: e.tensor_tensor(out=abf[:, kc, tsl], in0=t1[:], in1=g_[:], op=ALU.mult), r=["pt0", gn], w=["abf"])
    emit_resid_gemm(p, c, w, KC, abf, ins["xT"], out, 1.0, 0, NTT)
    return p.finish()


def build_final_norm(c):
    p = P()
    D, TC, TT, KC = c.D, c.TC, c.TT, c.KC
    xT = p.inp("xT", [D, TC])
    g = p.inp("g", [128, KC])
    out = p.outp("o", [D, TC])
    ones = p.const("ones", 1.0, (128, 128))
    epst = p.const("epst", 1e-6)
    gt = p.sb("gt", [128, KC])
    xs = p.sb("xs", [128, KC, TT])
    ob = p.sb("ob", [128, KC, TT])
    scr = [p.sb("scr%d" % i, [128, TT]) for i in range(2)]
    rstd = p.sb("rstd", [128, TT])
    p.dma("sp", gt[:], g, w=["gt"])
    xv = xT.rearrange("(c p) t -> p c t", p=128)
    ov = out.rearrange("(c p) t -> p c t", p=128)
    for tt in range(c.NTT):
        tsl = slice(tt * TT, (tt + 1) * TT)
        p.dma("sp", xs[:], xv[:, :, tsl], w=["xs"])
        emit_rstd(p, xs, "xs", KC, TT, ones, rstd, "rstd", epst, "epst", scr, D)
        for kc in range(KC):
            p.add("dve", lambda e, kc=kc: e.scalar_tensor_tensor(out=ob[:, kc, :], in0=xs[:, kc, :], scalar=gt[:, kc:kc + 1], in1=rstd[:],
                                                                  op0=ALU.mult, op1=ALU.mult), r=["xs", "gt", "rstd"], w=["ob"])
        p.dma("sp", ov[:, :, tsl], ob[:], r=["ob"])
    return p.finish()


_PROGS = {}


def _prog(name, builder, *a):
    if name not in _PROGS:
        _PROGS[name] = builder(*a)
    return _PROGS[name]


def _run(nc, in_maps):
    n = len(in_maps)
    res = run_bass_kernel_spmd(nc, in_maps, core_ids=list(range(n)))
    return res.results


def _tile_w(w, K, M):
    Kp = (K + 127) // 128 * 128
    Mp = (M + 127) // 128 * 128
    if (Kp, Mp) != (K, M):
        wp = np.zeros((Kp, Mp), np.float32)
        wp[:K, :M] = w
    else:
        wp = np.asarray(w, np.float32)
    return np.ascontiguousarray(wp.reshape(Kp // 128, 128, Mp // 128, 128).transpose(2, 1, 0, 3)).reshape(Mp // 128, 128, (Kp // 128) * 128)


def _pc(v):
    return np.ascontiguousarray(np.asarray(v, np.float32).reshape(-1, 128).T)


def _cat(res, key):
    return np.concatenate([r[key] for r in res], 1)


def _sl(A, i, TC):
    return np.ascontiguousarray(A[:, i * TC:(i + 1) * TC])


def _ffn(c, xT, g, wup, wdn):
    NC_ = c.NCORE
    wt = _tile_w(wup, c.D, 2 * c.F)
    gl = _pc(g)
    r1 = _run(_prog("ffn_up", build_ffn_up, c), [{"xT": xT[i], "g": gl, "w": wt} for i in range(NC_)])
    del wt
    wd = _tile_w(wdn, c.F, c.D)
    r2 = _run(_prog("ffn_down", build_ffn_down, c), [{"act": r1[i]["act"], "xT": xT[i], "w": wd} for i in range(NC_)])
    return [r2[i]["xo"] for i in range(NC_)]


def _fox(c, xT, g, w_in, b_f, qk_gain, w_out):
    import ml_dtypes
    D, S, TC, NC_ = c.D, c.S, c.TC, c.NCORE
    wt = _tile_w(w_in, D, c.FOXIN)
    gl = _pc(g)
    r1 = _run(_prog("fox_in", build_norm_gemm, c, c.FOXIN), [{"xT": xT[i], "g": gl, "w": wt} for i in range(NC_)])
    projT = _cat(r1, "o")
    del r1, wt
    FH = c.FH
    HPC = max(1, FH // NC_)
    ncore = FH // HPC
    kk = np.arange(128)[:, None]
    qq = np.arange(512)[None, :]
    masks = np.stack([np.where(kk + d * 128 <= qq, 0.0, -30000.0) for d in range(4)]).astype(ml_dtypes.bfloat16)
    ustr = (np.arange(128)[:, None] < np.arange(128)[None, :]).astype(np.float32)
    ident = np.eye(128, dtype=np.float32)
    ins = []
    for ci in range(ncore):
        hs = list(range(ci * HPC, (ci + 1) * HPC))
        ins.append({"qT": np.stack([projT[hh * 128:(hh + 1) * 128] for hh in hs]),
                    "kT": np.stack([projT[D + hh * 128: D + (hh + 1) * 128] for hh in hs]),
                    "v": np.stack([np.ascontiguousarray(projT[2 * D + hh * 128: 2 * D + (hh + 1) * 128].reshape(128, S // 128, 128).transpose(2, 1, 0)).reshape(128, S) for hh in hs]),
                    "fl": np.stack([projT[4 * D + hh] for hh in hs]),
                    "negb": np.stack([np.full((128, 1), -np.float32(b_f[hh]), np.float32) for hh in hs]),
                    "qg": np.asarray(qk_gain[0], np.float32).reshape(128, 1).copy(),
                    "kg": np.asarray(qk_gain[1], np.float32).reshape(128, 1).copy(),
                    "masks": masks, "ustr": ustr, "ident": ident})
    r2 = _run(_prog("fox_attn", build_fox_attn, c, HPC), ins)
    oT = np.concatenate([r2[ci]["oT"][j] for ci in range(ncore) for j in range(HPC)], 0)
    gT = projT[3 * D:4 * D]
    wo = _tile_w(w_out, D, D)
    r3 = _run(_prog("fox_out", build_fox_out, c), [{"oT": _sl(oT, i, TC), "gT": _sl(gT, i, TC), "xT": xT[i], "w": wo} for i in range(NC_)])
    return [r3[i]["xo"] for i in range(NC_)]


def _s5(c, xT, g, w_in, lam_re, lam_im, log_step, b_re, b_im, c_re, c_im, d_skip, w_out):
    D, S, TC, NC_ = c.D, c.S, c.TC, c.NCORE
    wt = _tile_w(w_in, D, D)
    gl = _pc(g)
    r1 = _run(_prog("s5_in", build_norm_gemm, c, D), [{"xT": xT[i], "g": gl, "w": wt} for i in range(NC_)])
    uT = _cat(r1, "o")
    del r1
    ncs = D // 256
    lstep = np.repeat(np.asarray(log_step, np.float32)[:, None], 64, 1)
    ins = []
    for ci in range(ncs):
        g0 = ci * 16
        st = lambda a: np.ascontiguousarray(np.asarray(a, np.float32)[g0:g0 + 16].reshape(8, 128).T)
        bre = np.zeros((8, 128, 128), np.float32)
        bim = np.zeros_like(bre)
        cre = np.zeros_like(bre)
        cim = np.zeros_like(bre)
        for j in range(8):
            cc = j // 4
            for gs in range(2):
                gl_ = 2 * j + gs
                gc = gl_ - cc * 8
                gg = g0 + gl_
                bre[j, gc * 16:(gc + 1) * 16, gs * 64:(gs + 1) * 64] = b_re[gg].T
                bim[j, gc * 16:(gc + 1) * 16, gs * 64:(gs + 1) * 64] = b_im[gg].T
                cre[j, gs * 64:(gs + 1) * 64, gc * 16:(gc + 1) * 16] = c_re[gg].T
                cim[j, gs * 64:(gs + 1) * 64, gc * 16:(gc + 1) * 16] = c_im[gg].T
        ins.append({"uT": np.ascontiguousarray(uT[ci * 256:(ci + 1) * 256]), "lre": st(lam_re), "lim": st(lam_im), "lstep": st(lstep),
                    "bre": bre, "bim": bim, "cre": cre, "cim": cim,
                    "dsk": np.ascontiguousarray(np.asarray(d_skip, np.float32)[ci * 256:(ci + 1) * 256].reshape(2, 128).T)})
    r2 = _run(_prog("s5_scan", build_s5_scan, c), ins)
    yT = np.concatenate([r2[ci]["yT"] for ci in range(ncs)], 0)
    wo = _tile_w(w_out, D, 2 * D)
    r3 = _run(_prog("s5_out", build_s5_out, c), [{"yT": _sl(yT, i, TC), "xT": xT[i], "w": wo} for i in range(NC_)])
    return [r3[i]["xo"] for i in range(NC_)]


def _rwkv(c, xT, g, Q):
    D, S, TC, KC, NC_ = c.D, c.S, c.TC, c.KC, c.NCORE
    bd = (np.arange(128)[:, None] // 64 == np.arange(128)[None, :] // 64).astype(np.float32)
    common = {"g": _pc(g), "mu": np.ascontiguousarray(np.asarray(Q['mu'], np.float32).reshape(6, KC, 128).transpose(2, 0, 1)),
              "wr": _tile_w(Q['w_rkv'][0], D, D), "wk": _tile_w(Q['w_rkv'][1], D, D), "wv": _tile_w(Q['w_rkv'][2], D, D),
              "w1": _tile_w(Q['w1'], D, Q['w1'].shape[1]), "a1": _tile_w(Q['a1'], D, Q['a1'].shape[1]), "g1": _tile_w(Q['g1'], D, Q['g1'].shape[1]),
              "w2": _tile_w(Q['w2'], Q['w2'].shape[0], D), "a2": _tile_w(Q['a2'], Q['a2'].shape[0], D), "g2": _tile_w(Q['g2'], Q['g2'].shape[0], D),
              "vecs": np.ascontiguousarray(np.stack([Q['w0'], Q['a0'], Q['k_k'], Q['k_a']]).astype(np.float32).reshape(4, KC, 128).transpose(2, 0, 1)),
              "bd": bd}
    ins = []
    for i in range(NC_):
        xe = np.zeros((D, TC + 1), np.float32)
        xe[:, 1:] = xT[i]
        if i > 0:
            xe[:, 0] = xT[i - 1][:, -1]
        d = dict(common)
        d["xe"] = xe
        ins.append(d)
    r1 = _run(_prog("rwkv_pre", build_rwkv_pre, c), ins)
    X = {n: _cat(r1, n) for n in ("rT", "kT", "vT", "gT")}
    XS = {n: np.concatenate([r[n] for r in r1], 2) for n in ("rS", "wS", "kS", "aS", "bS")}
    del r1
    H = D // 64
    ncs = H // 4
    sel = np.zeros((2, 12, 128), np.float32)
    for gi in range(2):
        for m in range(128):
            for part in range(3):
                sel[gi, part * 4 + gi * 2 + m // 64, m] = 1
    sel = sel.astype(XS["rS"].dtype)
    ins2 = []
    for ci in range(ncs):
        d = {"sel": sel}
        for n, key in (("w", "wS"), ("k", "kS"), ("a", "aS"), ("b", "bS"), ("r", "rS")):
            blk = XS[key][:, ci * 256:(ci + 1) * 256].reshape(3, 4, 64, S)
            d[n + "h"] = np.ascontiguousarray(blk.transpose(0, 1, 3, 2)).reshape(12, S * 64)
        vb = X["vT"][ci * 256:(ci + 1) * 256].reshape(2, 2, 64, S)
        d["vv"] = np.ascontiguousarray(vb.transpose(1, 2, 3, 0)).reshape(128, S, 2)
        ins2.append(d)
    r2 = _run(_prog("rwkv_scan", build_rwkv_scan, c), ins2)
    yT = np.concatenate([np.ascontiguousarray(r2[ci]["yy"].reshape(2, 64, S, 2).transpose(3, 0, 1, 2)).reshape(256, S) for ci in range(ncs)], 0)
    del r2, ins2, XS
    vecs = np.ascontiguousarray(np.stack([Q['ln_w'], Q['ln_b'], np.asarray(Q['r_k']).reshape(-1)]).astype(np.float32).reshape(3, KC, 128).transpose(2, 0, 1))
    wo = _tile_w(Q['w_out'], D, D)
    ins3 = [{"yT": _sl(yT, i, TC), "rT": _sl(X["rT"], i, TC), "kT": _sl(X["kT"], i, TC), "vT": _sl(X["vT"], i, TC), "gT": _sl(X["gT"], i, TC),
             "xT": xT[i], "vecs": vecs, "bd": bd, "w": wo} for i in range(NC_)]
    r3 = _run(_prog("rwkv_post", build_rwkv_post, c), ins3)
    return [r3[i]["xo"] for i in range(NC_)]


def _forward(c, inp, depth):
    x = np.asarray(inp['x'], np.float32)[0]
    TC, NC_ = c.TC, c.NCORE
    xT = [np.ascontiguousarray(x[i * TC:(i + 1) * TC].T) for i in range(NC_)]
    ia = ib = ic = 0
    for l in range(depth):
        xT = _ffn(c, xT, inp['norm_w'][l, 0], inp['ffn_w_up'][l, 0], inp['ffn_w_down'][l, 0])
        g = inp['norm_w'][l, 1]
        m = l % 3
        if m == 0:
            xT = _fox(c, xT, g, inp['fox_w_in'][ia], inp['fox_b_f'][ia], inp['fox_qk_gain'][ia], inp['fox_w_out'][ia])
            ia += 1
        elif m == 1:
            Q = {k[5:]: np.asarray(inp[k][ib]) for k in inp if k.startswith('rwkv_')}
            xT = _rwkv(c, xT, g, Q)
            ib += 1
        else:
            xT = _s5(c, xT, g, inp['s5_w_in'][ic], inp['s5_lam_re'][ic], inp['s5_lam_im'][ic], inp['s5_log_step'][ic],
                     inp['s5_b_re'][ic], inp['s5_b_im'][ic], inp['s5_c_re'][ic], inp['s5_c_im'][ic], inp['s5_d'][ic], inp['s5_w_out'][ic])
            ic += 1
        xT = _ffn(c, xT, inp['norm_w'][l, 2], inp['ffn_w_up'][l, 1], inp['ffn_w_down'][l, 1])
    rf = _run(_prog("final_norm", build_final_norm, c), [{"xT": xT[i], "g": _pc(inp['final_norm'])} for i in range(NC_)])
    out = np.concatenate([rf[i]["o"].T for i in range(NC_)], 0)
    return np.ascontiguousarray(out[None]).astype(np.float32)


def kernel(**inputs):
    inp = {k: np.asarray(v) for k, v in inputs.items()}
    D = inp['x'].shape[-1]
    S = inp['x'].shape[1]
    c = CFG(D=D, S=S)
    return _forward(c, inp, inp['norm_w'].shape[0])
```
